# Optimizing a Trainium2 kernel written in Bass

```python
import jax, jax.numpy as jnp
from jax import lax
import numpy as np

D_MODEL = 2048
BATCH = 8
SEQ = 4096
DEPTH = 4

CHUNK = 64
EPS = 1e-6
N_EVEN = (DEPTH + 1) // 2
N_ODD = DEPTH // 2

RET_HEADS = 4
RET_DK = D_MODEL // 8
RET_DV = D_MODEL // 4
RET_THETA = 10000.0
MLA_HEADS = D_MODEL // 128
MLA_NOPE = 128
MLA_ROPE = 64
MLA_V = 128
MLA_Q_RANK = D_MODEL // 4
MLA_KV_RANK = D_MODEL // 4
MLA_THETA = 10000.0
Q_BLOCK = 128
RET_QK_W = RET_HEADS * RET_DK
RET_V_W = RET_HEADS * RET_DV
EVEN_IN = 2 * RET_QK_W + 2 * RET_V_W + MLA_Q_RANK + MLA_KV_RANK + MLA_ROPE
EVEN_OUT = RET_V_W + MLA_HEADS * MLA_V
SSD_INNER = 2 * D_MODEL
SSD_HEADDIM = 64
SSD_HEADS = SSD_INNER // SSD_HEADDIM
SSD_GROUPS = 8
SSD_STATE = 128
SSD_CONV = 4
SSD_CONV_DIM = SSD_INNER + 2 * SSD_GROUPS * SSD_STATE
ODD_IN = SSD_INNER + SSD_CONV_DIM + SSD_HEADS
FFN_HIDDEN = 256 * (-(-8 * D_MODEL // (3 * 256)))
FFN_CONV = 3

kernel_name = 'hybrid_retention_mla_ssd_convffn'


def _split(t, widths):
    idx = np.cumsum(widths)[:-1].tolist()
    return jnp.split(t, idx, axis=-1)


def rmsnorm(x, g):
    xf = x.astype(jnp.float32)
    y = xf * lax.rsqrt(jnp.mean(xf * xf, axis=-1, keepdims=True) + EPS)
    return (y * g.astype(jnp.float32)).astype(x.dtype)


def modulate(h, shift, scale):
    return h * (1.0 + scale[:, None, :]) + shift[:, None, :]


def rope(t, pos, base):
    d = t.shape[-1]
    half = d // 2
    inv = base ** (-jnp.arange(half, dtype=jnp.float32) / half)
    ang = pos.astype(jnp.float32)[:, None] * inv[None, :]
    cos = jnp.cos(ang)[:, None, :]
    sin = jnp.sin(ang)[:, None, :]
    tf = t.astype(jnp.float32)
    t1 = tf[..., :half]
    t2 = tf[..., half:]
    return jnp.concatenate([t1 * cos - t2 * sin, t1 * sin + t2 * cos], axis=-1).astype(t.dtype)


def causal_dwconv(x, w, b):
    width = w.shape[0]
    y = lax.conv_general_dilated(
        x, w[:, None, :].astype(x.dtype), window_strides=(1,), padding=[(width - 1, 0)],
        dimension_numbers=('NWC', 'WIO', 'NWC'), feature_group_count=x.shape[-1])
    return y + b.astype(y.dtype)


def retention(q, k, v, log_gamma):
    bsz, s, h, dk = q.shape
    dv = v.shape[-1]
    nc = s // CHUNK

    def to_chunks(t):
        return t.reshape(bsz, nc, CHUNK, h, t.shape[-1]).transpose(1, 0, 3, 2, 4)

    idx = jnp.arange(CHUNK, dtype=jnp.float32)
    rel = idx[:, None] - idx[None, :]
    decay = jnp.where(rel >= 0, jnp.exp(log_gamma[:, None, None] * jnp.maximum(rel, 0.0)), 0.0)
    xi = jnp.exp(log_gamma[:, None] * (idx + 1.0))[:, :, None]
    zeta = jnp.exp(log_gamma[:, None] * (CHUNK - 1.0 - idx))[:, :, None]
    chunk_decay = jnp.exp(log_gamma * CHUNK)[:, None, None]

    def step(r, qkv):
        qc, kc, vc = qkv
        inner = jnp.einsum('bhld,bhsd->bhls', qc, kc) * decay
        y = jnp.einsum('bhls,bhse->bhle', inner, vc) + jnp.einsum('bhld,bhde->bhle', qc, r) * xi
        r = chunk_decay * r + jnp.einsum('bhsd,bhse->bhde', kc, vc * zeta)
        return r, y

    r0 = jnp.zeros((bsz, h, dk, dv), jnp.float32)
    _, y = lax.scan(step, r0, (to_chunks(q), to_chunks(k), to_chunks(v)))
    return y.transpose(1, 0, 3, 2, 4).reshape(bsz, s, h, dv)


def mla_attention(q_nope, q_rope, k_nope, k_rope, v):
    bsz, s, h, _ = q_nope.shape
    nb = s // Q_BLOCK
    scale = (MLA_NOPE + MLA_ROPE) ** -0.5
    key_chunk = jnp.arange(s) // CHUNK
    qn_b = q_nope.reshape(bsz, nb, Q_BLOCK, h, MLA_NOPE).transpose(1, 0, 2, 3, 4)
    qr_b = q_rope.reshape(bsz, nb, Q_BLOCK, h, MLA_ROPE).transpose(1, 0, 2, 3, 4)

    def one_block(args):
        i, qn, qr = args
        sc = (jnp.einsum('bqhd,bkhd->bhqk', qn, k_nope).astype(jnp.float32)
              + jnp.einsum('bqhr,bkr->bhqk', qr, k_rope).astype(jnp.float32)) * scale
        q_chunk = (i * Q_BLOCK + jnp.arange(Q_BLOCK)) // CHUNK
        mask = key_chunk[None, :] <= q_chunk[:, None]
        sc = jnp.where(mask, sc, -jnp.inf)
        p = jax.nn.softmax(sc, axis=-1).astype(v.dtype)
        return jnp.einsum('bhqk,bkhd->bqhd', p, v)

    out = lax.map(one_block, (jnp.arange(nb), qn_b, qr_b))
    return out.transpose(1, 0, 2, 3, 4).reshape(bsz, s, h, v.shape[-1])


def hybrid_mixer(h, pos, w_in, q_norm_g, w_uq, kv_norm_g, w_ukv, ret_gn_g, w_out):
    bsz, s, _ = h.shape
    rq, rk, rv, rg, cq, ckv, kr = _split(
        h @ w_in, [RET_QK_W, RET_QK_W, RET_V_W, RET_V_W, MLA_Q_RANK, MLA_KV_RANK, MLA_ROPE])
    log_gamma = jnp.log1p(-jnp.exp2(-5.0 - jnp.arange(RET_HEADS, dtype=jnp.float32)))
    rq = rope(rq.reshape(bsz, s, RET_HEADS, RET_DK), pos, RET_THETA)
    rk = rope(rk.reshape(bsz, s, RET_HEADS, RET_DK), pos, RET_THETA) * (RET_DK ** -0.5)
    rv = rv.reshape(bsz, s, RET_HEADS, RET_DV)
    yr = retention(rq.astype(jnp.float32), rk.astype(jnp.float32), rv.astype(jnp.float32), log_gamma)
    yc = yr - jnp.mean(yr, axis=-1, keepdims=True)
    yr = yc * lax.rsqrt(jnp.mean(yc * yc, axis=-1, keepdims=True) + EPS)
    yr = yr.reshape(bsz, s, RET_V_W) * ret_gn_g.astype(jnp.float32)
    y_ret = (jax.nn.silu(rg.astype(jnp.float32)) * yr).astype(h.dtype)
    q = (rmsnorm(cq, q_norm_g) @ w_uq).reshape(bsz, s, MLA_HEADS, MLA_NOPE + MLA_ROPE)
    q_nope = q[..., :MLA_NOPE]
    q_rope = rope(q[..., MLA_NOPE:], pos, MLA_THETA)
    kv = (rmsnorm(ckv, kv_norm_g) @ w_ukv).reshape(bsz, s, MLA_HEADS, MLA_NOPE + MLA_V)
    k_nope = kv[..., :MLA_NOPE]
    v = kv[..., MLA_NOPE:]
    k_rope = rope(kr[:, :, None, :], pos, MLA_THETA)[:, :, 0, :]
    y_mla = mla_attention(q_nope, q_rope, k_nope, k_rope, v).reshape(bsz, s, MLA_HEADS * MLA_V)
    return jnp.concatenate([y_ret, y_mla], axis=-1) @ w_out


def ssd_scan(x, a, bm, cm):
    bsz, s, h, p = x.shape
    g, n = bm.shape[-2:]
    j = h // g
    nc = s // CHUNK
    xc = x.reshape(bsz, nc, CHUNK, g, j, p).transpose(1, 0, 2, 3, 4, 5)
    ac = a.reshape(bsz, nc, CHUNK, g, j).transpose(1, 0, 3, 4, 2)
    bc = bm.reshape(bsz, nc, CHUNK, g, n).transpose(1, 0, 2, 3, 4)
    cc = cm.reshape(bsz, nc, CHUNK, g, n).transpose(1, 0, 2, 3, 4)
    causal = jnp.tril(jnp.ones((CHUNK, CHUNK), dtype=bool))

    def step(state, inp):
        xk, ak, bk, ck = inp
        acum = jnp.cumsum(ak, axis=-1)
        seg = acum[..., :, None] - acum[..., None, :]
        lmat = jnp.where(causal, jnp.exp(jnp.where(causal, seg, 0.0)), 0.0)
        cb = jnp.einsum('blgn,bsgn->bgls', ck, bk)
        y_diag = jnp.einsum('bgls,bgjls,bsgjp->blgjp', cb, lmat, xk)
        y_off = jnp.einsum('blgn,bgjpn,bgjl->blgjp', ck, state, jnp.exp(acum))
        to_end = jnp.exp(acum[..., -1:] - acum)
        state = (state * jnp.exp(acum[..., -1])[..., None, None]
                 + jnp.einsum('bsgn,bgjs,bsgjp->bgjpn', bk, to_end, xk))
        return state, y_diag + y_off

    s0 = jnp.zeros((bsz, g, j, p, n), jnp.float32)
    _, y = lax.scan(step, s0, (xc, ac, bc, cc))
    return y.transpose(1, 0, 2, 3, 4, 5).reshape(bsz, s, h, p)


def ssd_mixer(h, w_in, conv_w, conv_b, dt_bias, a_log, d_skip, norm_g, w_out):
    bsz, s, _ = h.shape
    z, xbc, dt = _split(h @ w_in, [SSD_INNER, SSD_CONV_DIM, SSD_HEADS])
    xbc = jax.nn.silu(causal_dwconv(xbc, conv_w, conv_b))
    xs, bm, cm = _split(xbc, [SSD_INNER, SSD_GROUPS * SSD_STATE, SSD_GROUPS * SSD_STATE])
    xs = xs.reshape(bsz, s, SSD_HEADS, SSD_HEADDIM).astype(jnp.float32)
    bm = bm.reshape(bsz, s, SSD_GROUPS, SSD_STATE).astype(jnp.float32)
    cm = cm.reshape(bsz, s, SSD_GROUPS, SSD_STATE).astype(jnp.float32)
    dt = jax.nn.softplus(dt.astype(jnp.float32) + dt_bias.astype(jnp.float32))
    a = -jnp.exp(a_log.astype(jnp.float32))
    y = ssd_scan(xs * dt[..., None], dt * a, bm, cm)
    y = y + xs * d_skip.astype(jnp.float32)[:, None]
    y = y.reshape(bsz, s, SSD_INNER) * jax.nn.silu(z.astype(jnp.float32))
    return rmsnorm(y, norm_g).astype(h.dtype) @ w_out


def conv_ffn(h, w_up, conv_w, conv_b, w_down):
    u, g = jnp.split(h @ w_up, 2, axis=-1)
    g = causal_dwconv(g, conv_w, conv_b)
    return (jax.nn.silu(g) * u) @ w_down


def setup_inputs(seed: int = 0) -> dict:
    key = jax.random.key(seed)
    ks = jax.random.split(key, 32)
    f32 = jnp.float32

    def nrm(k, shape, fan_in, scale=1.0):
        return jax.random.normal(k, shape, f32) * (scale * fan_in ** -0.5)

    def gain(k, shape):
        return 1.0 + 0.02 * jax.random.normal(k, shape, f32)

    def small(k, shape):
        return 0.02 * jax.random.normal(k, shape, f32)

    dt0 = jnp.exp(jax.random.uniform(ks[17], (N_ODD, SSD_HEADS), f32, jnp.log(1e-3), jnp.log(1e-1)))
    return {
        'x': jax.random.normal(ks[0], (BATCH, SEQ, D_MODEL), f32),
        'c': jax.random.normal(ks[1], (BATCH, D_MODEL), f32),
        'ada_w': nrm(ks[2], (DEPTH, D_MODEL, 6 * D_MODEL), D_MODEL, 0.5),
        'ada_b': small(ks[3], (DEPTH, 6 * D_MODEL)),
        'norm_mix_g': gain(ks[4], (DEPTH, D_MODEL)),
        'norm_ffn_g': gain(ks[5], (DEPTH, D_MODEL)),
        'hyb_w_in': nrm(ks[6], (N_EVEN, D_MODEL, EVEN_IN), D_MODEL),
        'hyb_q_norm_g': gain(ks[7], (N_EVEN, MLA_Q_RANK)),
        'hyb_w_uq': nrm(ks[8], (N_EVEN, MLA_Q_RANK, MLA_HEADS * (MLA_NOPE + MLA_ROPE)), MLA_Q_RANK),
        'hyb_kv_norm_g': gain(ks[9], (N_EVEN, MLA_KV_RANK)),
        'hyb_w_ukv': nrm(ks[10], (N_EVEN, MLA_KV_RANK, MLA_HEADS * (MLA_NOPE + MLA_V)), MLA_KV_RANK),
        'hyb_ret_gn_g': gain(ks[11], (N_EVEN, RET_V_W)),
        'hyb_w_out': nrm(ks[12], (N_EVEN, EVEN_OUT, D_MODEL), EVEN_OUT),
        'ssd_w_in': nrm(ks[13], (N_ODD, D_MODEL, ODD_IN), D_MODEL),
        'ssd_conv_w': nrm(ks[14], (N_ODD, SSD_CONV, SSD_CONV_DIM), SSD_CONV),
        'ssd_conv_b': small(ks[15], (N_ODD, SSD_CONV_DIM)),
        'ssd_dt_bias': dt0 + jnp.log(-jnp.expm1(-dt0)),
        'ssd_a_log': jnp.log(jax.random.uniform(ks[18], (N_ODD, SSD_HEADS), f32, 1.0, 16.0)),
        'ssd_d': gain(ks[19], (N_ODD, SSD_HEADS)),
        'ssd_norm_g': gain(ks[20], (N_ODD, SSD_INNER)),
        'ssd_w_out': nrm(ks[21], (N_ODD, SSD_INNER, D_MODEL), SSD_INNER),
        'ffn_w_up': nrm(ks[22], (DEPTH, D_MODEL, 2 * FFN_HIDDEN), D_MODEL),
        'ffn_conv_w': nrm(ks[23], (DEPTH, FFN_CONV, FFN_HIDDEN), FFN_CONV),
        'ffn_conv_b': small(ks[24], (DEPTH, FFN_HIDDEN)),
        'ffn_w_down': nrm(ks[25], (DEPTH, FFN_HIDDEN, D_MODEL), FFN_HIDDEN),
        'final_norm_g': gain(ks[26], (D_MODEL,)),
    }


def reference(x, c, ada_w, ada_b, norm_mix_g, norm_ffn_g,
              hyb_w_in, hyb_q_norm_g, hyb_w_uq, hyb_kv_norm_g, hyb_w_ukv, hyb_ret_gn_g, hyb_w_out,
              ssd_w_in, ssd_conv_w, ssd_conv_b, ssd_dt_bias, ssd_a_log, ssd_d, ssd_norm_g, ssd_w_out,
              ffn_w_up, ffn_conv_w, ffn_conv_b, ffn_w_down, final_norm_g):
    pos = jnp.arange(x.shape[1], dtype=jnp.int32)
    mods = jnp.einsum('bd,lde->lbe', jax.nn.silu(c), ada_w) + ada_b[:, None, :]
    for l in range(DEPTH):
        sh_m, sc_m, gt_m, sh_f, sc_f, gt_f = jnp.split(mods[l], 6, axis=-1)
        hm = modulate(rmsnorm(x, norm_mix_g[l]), sh_m, sc_m)
        i = l // 2
        if l % 2 == 0:
            y = hybrid_mixer(hm, pos, hyb_w_in[i], hyb_q_norm_g[i], hyb_w_uq[i], hyb_kv_norm_g[i],
                             hyb_w_ukv[i], hyb_ret_gn_g[i], hyb_w_out[i])
        else:
            y = ssd_mixer(hm, ssd_w_in[i], ssd_conv_w[i], ssd_conv_b[i], ssd_dt_bias[i], ssd_a_log[i],
                          ssd_d[i], ssd_norm_g[i], ssd_w_out[i])
        x = x + gt_m[:, None, :] * y
        hf = modulate(rmsnorm(x, norm_ffn_g[l]), sh_f, sc_f)
        x = x + gt_f[:, None, :] * conv_ffn(hf, ffn_w_up[l], ffn_conv_w[l], ffn_conv_b[l], ffn_w_down[l])
    return rmsnorm(x, final_norm_g)
```

```python
import contextlib
import math
import numpy as np
import ml_dtypes
import concourse.bass as bass
import concourse.mybir as mybir
from concourse.bass_utils import run_bass_kernel_spmd

F32 = mybir.dt.float32
BF16 = mybir.dt.bfloat16
ALU = mybir.AluOpType
AF = mybir.ActivationFunctionType

S = 4096
DM = 2048
NT = 8
EPS = 1e-6
FH = 5632
N_CORES = 8

ENGS = ("pe", "act", "dve", "pool", "sp")
N_DMA_SLOTS = 14


class Tracker:
    def __init__(self, nc, sems):
        self.nc = nc
        self.sem = {}
        self.count = {}
        it = iter(sems)
        for e in ENGS:
            self.sem[e] = next(it)
            self.count[e] = 0
        self.slots = []
        for i in range(N_DMA_SLOTS):
            n = "dma%d" % i
            self.sem[n] = next(it)
            self.count[n] = 0
            self.slots.append(n)
        self.slot_rr = 0
        self.seen = {e: {} for e in ENGS}
        self.ops = {e: [] for e in ENGS}
        self.lw = {}
        self.rd = {}
        self.n_ops = 0

    def _deps(self, reads, writes):
        deps = {}
        lw = self.lw
        for r in reads:
            ev = lw.get(r)
            if ev is not None and deps.get(ev[0], 0) < ev[1]:
                deps[ev[0]] = ev[1]
        for w in writes:
            ev = lw.get(w)
            if ev is not None and deps.get(ev[0], 0) < ev[1]:
                deps[ev[0]] = ev[1]
            d = self.rd.get(w)
            if d:
                for p, c in d.items():
                    if deps.get(p, 0) < c:
                        deps[p] = c
        return deps

    def _emit_waits(self, eng, deps):
        seen = self.seen[eng]
        for p, c in deps.items():
            if p == eng and eng in ("pe", "sp"):
                continue
            if seen.get(p, 0) >= c:
                continue
            seen[p] = c
            self.ops[eng].append((0, (p, c)))

    def _commit(self, ev, reads, writes):
        p, c = ev
        for r in reads:
            d = self.rd.setdefault(r, {})
            if d.get(p, 0) < c:
                d[p] = c
        for w in writes:
            self.lw[w] = ev
            self.rd[w] = {}

    max_ops = 10 ** 9

    def op(self, eng, fn, reads=(), writes=(), event=True):
        if self.n_ops >= self.max_ops:
            return
        self._emit_waits(eng, self._deps(reads, writes))
        if event:
            self.count[eng] += 1
            ev = (eng, self.count[eng])
            self.ops[eng].append((1, fn))
        else:
            ev = (eng, self.count[eng] + 1)
            self.ops[eng].append((2, fn))
        self._commit(ev, reads, writes)
        self.n_ops += 1

    def dma(self, fn, reads=(), writes=(), q="sp"):
        if self.n_ops >= self.max_ops:
            return
        deps = self._deps(reads, writes)
        slot = self.slots[self.slot_rr]
        self.slot_rr = (self.slot_rr + 1) % len(self.slots)
        if deps.get(slot, 0) < self.count[slot]:
            deps[slot] = self.count[slot]
        self._emit_waits(q, deps)
        self.count[slot] += 16
        ev = (slot, self.count[slot])
        self.ops[q].append((3, (fn, slot)))
        self._commit(ev, reads, writes)
        self.n_ops += 1

    def barrier(self):
        deps = {p: c for p, c in self.count.items() if c > 0}
        for e in ENGS:
            self._emit_waits(e, dict(deps))
        self.lw = {}
        self.rd = {}

    def _replay(self, eng, e):
        sem = self.sem
        for kind, pl in self.ops[eng]:
            if kind == 0:
                e.wait_ge(sem[pl[0]], pl[1])
            elif kind == 1:
                pl(e).then_inc(sem[eng], 1)
            elif kind == 2:
                pl(e)
            else:
                pl[0](e).then_inc(sem[pl[1]], 16)
        self.ops[eng] = []

    def emit(self):
        with self.nc.Block() as block:
            @block.tensor
            def _(e):
                self._replay("pe", e)

            @block.scalar
            def _(e):
                self._replay("act", e)

            @block.vector
            def _(e):
                self._replay("dve", e)

            @block.gpsimd
            def _(e):
                self._replay("pool", e)

            @block.sync
            def _(e):
                self._replay("sp", e)


def MM(out, lhsT, rhs, st, sp):
    return lambda e: e.matmul(out, lhsT=lhsT, rhs=rhs, start=st, stop=sp)

def TR(out, in_, ident):
    return lambda e: e.transpose(out=out, in_=in_, identity=ident)

def ACTF(out, in_, func, **kw):
    return lambda e: e.activation(out=out, in_=in_, func=func, **kw)

def TT(out, a, b, op):
    return lambda e: e.tensor_tensor(out=out, in0=a, in1=b, op=op)

def TS(out, a, s1, s2, op0, op1):
    return lambda e: e.tensor_scalar(out=out, in0=a, scalar1=s1, scalar2=s2, op0=op0, op1=op1)

def STT(out, a, s, b, op0, op1):
    return lambda e: e.scalar_tensor_tensor(out=out, in0=a, scalar=s, in1=b, op0=op0, op1=op1)

def CP(out, in_):
    return lambda e: e.tensor_copy(out=out, in_=in_)

def ACP(out, in_):
    return lambda e: e.copy(out=out, in_=in_)

def RECIP(out, in_):
    return lambda e: e.reciprocal(out=out, in_=in_)

def MSET(ap, v):
    return lambda e: e.memset(ap, v)

def DMA(out, in_):
    return lambda e: e.dma_start(out=out, in_=in_)


class Phase:
    def __init__(self, B, name):
        self.B = B
        self.name = name
        self.es = contextlib.ExitStack()
        self.np_ = 0
        self.rr = 0

    def __enter__(self):
        self.es.__enter__()
        self.es.enter_context(self.B.nc.named_scope(self.name))
        return self

    def __exit__(self, *a):
        self.B.T.barrier()
        self.B.T.emit()
        return self.es.__exit__(*a)

    def sb(self, name, shape, dt):
        return self.es.enter_context(self.B.nc.sbuf_tensor(self.name + "_" + name, shape, dt))

    def ps(self, name, shape=(128, 512), dt=F32):
        return self.es.enter_context(self.B.nc.psum_tensor(self.name + "_" + name, list(shape), dt))

    def banks(self, n):
        self.pb = [self.ps("pb%d" % i) for i in range(n)]
        self.npb = n

    def nextp(self):
        i = self.rr
        self.rr = (self.rr + 1) % self.npb
        return i


class Builder:
    def __init__(self, dbg=False, only=None):
        self.nc = bass.Bass("TRN2", target_bir_lowering=False)
        self.D = {}
        self.dbg = dbg
        self.only = only
        self.uid = 0

    def din(self, name, shape, dt=F32):
        if self.only is not None and name not in self.only:
            return
        self.D[name] = self.nc.dram_tensor(name, list(shape), dt, kind="ExternalInput").ap()

    def dscr(self, name, shape, dt, out=False):
        kind = "ExternalOutput" if (out or (self.dbg and name in self.dbg)) else "Internal"
        self.D[name] = self.nc.dram_tensor(name, list(shape), dt, kind=kind).ap()

    def phase(self, name):
        self.uid += 1
        return Phase(self, "%s%d" % (name, self.uid))

    def mmg(self, out, pairs, reads, wkey):
        n = len(pairs)
        for k, (l, r) in enumerate(pairs):
            self.T.op("pe", MM(out, l, r, k == 0, k == n - 1), reads=reads if k == 0 else (), writes=[wkey], event=(k == n - 1))

    def cast(self, src, dst, rows, blk=256):
        for r0 in range(0, rows, blk):
            self.T.dma(DMA(dst[r0:r0 + blk, :], src[r0:r0 + blk, :]), q="pool")

    def cast_layer(self, l):
        D = self.D
        i = l // 2
        if True:
            if l % 2 == 0:
                self.cast(D["hyb_win"][i], D["b_hyb_win"], 2048)
                self.cast(D["hyb_wuq"][i], D["b_hyb_wuq"], 512)
                self.cast(D["hyb_wukv"][i], D["b_hyb_wukv"], 512)
                self.cast(D["hyb_wout"][i], D["b_wout"], 4096)
            else:
                self.cast(D["ssd_win"][i], D["b_ssd_win"], 2048)
                self.cast(D["ssd_wout"][i], D["b_wout"], 4096)
            self.cast(D["ffn_wup"][l], D["b_ffn_wup"], 2048)
            self.cast(D["ffn_wdown"][l], D["b_ffn_wdown"], FH)

    def phase_cast(self, l):
        with self.phase("cast"):
            self.cast_layer(l)

    def phase_ada(self, cast_l=None):
        D, T = self.D, self.T
        with self.phase("ada") as P:
            if cast_l is not None:
                self.cast_layer(cast_l)
            sc = P.sb("sc", [128, 16], F32)
            scs = P.sb("scs", [128, 16], F32)
            scb = P.sb("scb", [128, 16, 128], F32)
            ones1 = P.sb("ones1", [1, 128], F32)
            wp = [P.sb("wp%d" % i, [128, 16, 512], F32) for i in range(2)]
            br = [P.sb("br%d" % i, [1, 512], F32) for i in range(2)]
            gr = [P.sb("gr%d" % i, [1, 512], F32) for i in range(2)]
            gb = [P.sb("gb%d" % i, [128, 512], F32) for i in range(2)]
            rs = [P.sb("rs%d" % i, [128, 512], F32) for i in range(2)]
            pm = [P.ps("pm%d" % i) for i in range(2)]
            pg = [P.ps("pg%d" % i) for i in range(2)]
            T.dma(DMA(sc[:], D["cT"]), writes=["sc"])
            T.op("pool", MSET(ones1[:], 1.0), writes=["ones1"])
            T.op("act", ACTF(scs[:], sc[:], AF.Silu), reads=["sc"], writes=["scs"])
            T.op("dve", CP(scb[:], scs[:].unsqueeze(2).broadcast_to([128, 16, 128])), reads=["scs"], writes=["scb"])
            it = 0
            for l in range(4):
                for j in range(6):
                    for n in range(4):
                        i = it % 2
                        it += 1
                        c0 = j * 2048 + n * 512
                        T.dma(DMA(wp[i][:], D["ada_w"][l, :, c0:c0 + 512].rearrange("(kc p) n -> p kc n", p=128)), writes=[("wp", i)])
                        T.dma(DMA(br[i][:], D["ada_b"][l, :, c0:c0 + 512]), writes=[("br", i)])
                        pairs = [(scb[:, kc, :], wp[i][:, kc, :]) for kc in range(16)] + [(ones1[0:1, :], br[i][0:1, :])]
                        self.mmg(pm[i][:], pairs, ["scb", "ones1", ("wp", i), ("br", i)], ("pm", i))
                        if j in (1, 4):
                            gsrc = D["norm_mix_g"] if j == 1 else D["norm_ffn_g"]
                            T.dma(DMA(gr[i][:], gsrc[l, :, n * 512:(n + 1) * 512]), writes=[("gr", i)])
                            self.mmg(pg[i][:], [(ones1[0:1, :], gr[i][0:1, :])], ["ones1", ("gr", i)], ("pg", i))
                            T.op("act", ACP(gb[i][:], pg[i][:]), reads=[("pg", i)], writes=[("gb", i)])
                            T.op("dve", STT(rs[i][:], pm[i][:], 1.0, gb[i][:], ALU.add, ALU.mult), reads=[("pm", i), ("gb", i)], writes=[("rs", i)])
                        else:
                            T.op("act", ACP(rs[i][:], pm[i][:]), reads=[("pm", i)], writes=[("rs", i)])
                        T.dma(DMA(D["modsb"][l, j, :, n * 512:(n + 1) * 512], rs[i][:]), reads=[("rs", i)])

    def phase_norm(self, x_src, l, j_gs, j_sh, dst):
        D, T = self.D, self.T
        with self.phase("norm") as P:
            gs = P.sb("gs", [128, 2048], F32)
            sh = P.sb("sh", [128, 2048], F32)
            ident = P.sb("ident", [128, 128], BF16)
            xt = [P.sb("xt%d" % i, [128, 4, 2048], F32) for i in range(2)]
            tmp = [P.sb("tmp%d" % i, [128, 2048], F32) for i in range(2)]
            hmb = [P.sb("hmb%d" % i, [128, 4, 2048], BF16) for i in range(2)]
            hT = [P.sb("hT%d" % i, [128, 16, 512], BF16) for i in range(2)]
            junk = P.sb("junk", [128, 2048], BF16)
            ss = P.sb("ss", [128, 8], F32)
            sd = P.sb("sd", [128, 8], F32)
            rstd = P.sb("rstd", [128, 8], F32)
            pt = [P.ps("pt%d" % i, [128, 1024], BF16) for i in range(4)]
            T.dma(DMA(gs[:], D["modsb"][l, j_gs]), writes=["gs"])
            T.dma(DMA(sh[:], D["modsb"][l, j_sh]), writes=["sh"])
            T.dma(DMA(ident[:], D["ident"]), writes=["ident"])
            g = 0
            for tt in range(NT):
                i = tt % 2
                T.dma(DMA(xt[i][:], x_src[tt * 512:(tt + 1) * 512, :].rearrange("(s p) d -> p s d", p=128)), writes=[("xt", i)])
                for s in range(4):
                    c = i * 4 + s
                    T.op("act", ACTF(junk[:], xt[i][:, s, :], AF.Square, accum_out=ss[:, c:c + 1]), reads=[("xt", i)], writes=[("ss", c)])
                    T.op("act", ACTF(sd[:, c:c + 1], ss[:, c:c + 1], AF.Sqrt, scale=1.0 / DM, bias=EPS), reads=[("ss", c)], writes=[("sd", c)])
                    T.op("dve", RECIP(rstd[:, c:c + 1], sd[:, c:c + 1]), reads=[("sd", c)], writes=[("rstd", c)])
                    T.op("dve", STT(tmp[s % 2][:], xt[i][:, s, :], rstd[:, c:c + 1], gs[:], ALU.mult, ALU.mult),
                         reads=[("xt", i), ("rstd", c), "gs"], writes=[("tmp", s % 2)])
                    T.op("pool", TT(hmb[i][:, s, :], tmp[s % 2][:], sh[:], ALU.add), reads=[("tmp", s % 2), "sh"], writes=[("hmb", i, s)])
                for kc in range(16):
                    pi = g % 4
                    g += 1
                    for s in range(4):
                        T.op("pe", TR(pt[pi][:, s * 128:(s + 1) * 128], hmb[i][:, s, kc * 128:(kc + 1) * 128], ident[:]),
                             reads=[("hmb", i, s), "ident"], writes=[("pt", pi)], event=(s == 3))
                    if kc % 2 == 0:
                        T.op("dve", CP(hT[i][:, kc, :], pt[pi][:, 0:512]), reads=[("pt", pi)], writes=[("hT", i, kc)])
                    else:
                        T.op("act", ACP(hT[i][:, kc, :], pt[pi][:, 0:512]), reads=[("pt", pi)], writes=[("hT", i, kc)])
                T.dma(DMA(dst[:, tt * 512:(tt + 1) * 512].rearrange("(kc p) t -> p kc t", p=128), hT[i][:]),
                      reads=[("hT", i, kc) for kc in range(16)])

    def phase_final(self, x_src):
        D, T = self.D, self.T
        with self.phase("fin") as P:
            gsb = P.sb("gsb", [128, 2048], F32)
            grow = P.sb("grow", [1, 2048], F32)
            ones1 = P.sb("ones1", [1, 128], F32)
            xt = [P.sb("xt%d" % i, [128, 4, 2048], F32) for i in range(2)]
            ot = [P.sb("ot%d" % i, [128, 4, 2048], F32) for i in range(2)]
            junk = P.sb("junk", [128, 2048], BF16)
            ss = P.sb("ss", [128, 8], F32)
            sd = P.sb("sd", [128, 8], F32)
            rstd = P.sb("rstd", [128, 8], F32)
            P.banks(4)
            T.dma(DMA(grow[:], D["final_norm_g"]), writes=["grow"])
            T.op("pool", MSET(ones1[:], 1.0), writes=["ones1"])
            for n in range(4):
                self.mmg(P.pb[n][:], [(ones1[0:1, :], grow[0:1, n * 512:(n + 1) * 512])], ["ones1", "grow"], ("pb", n))
                T.op("act", ACP(gsb[:, n * 512:(n + 1) * 512], P.pb[n][:]), reads=[("pb", n)], writes=["gsb"])
            for tt in range(NT):
                i = tt % 2
                T.dma(DMA(xt[i][:], x_src[tt * 512:(tt + 1) * 512, :].rearrange("(s p) d -> p s d", p=128)), writes=[("xt", i)])
                for s in range(4):
                    c = i * 4 + s
                    T.op("act", ACTF(junk[:], xt[i][:, s, :], AF.Square, accum_out=ss[:, c:c + 1]), reads=[("xt", i)], writes=[("ss", c)])
                    T.op("act", ACTF(sd[:, c:c + 1], ss[:, c:c + 1], AF.Sqrt, scale=1.0 / DM, bias=EPS), reads=[("ss", c)], writes=[("sd", c)])
                    T.op("dve", RECIP(rstd[:, c:c + 1], sd[:, c:c + 1]), reads=[("sd", c)], writes=[("rstd", c)])
                    T.op("dve", STT(ot[i][:, s, :], xt[i][:, s, :], rstd[:, c:c + 1], gsb[:], ALU.mult, ALU.mult),
                         reads=[("xt", i), ("rstd", c), "gsb"], writes=[("ot", i, s)])
                T.dma(DMA(D["out"][tt * 512:(tt + 1) * 512, :].rearrange("(s p) d -> p s d", p=128), ot[i][:]),
                      reads=[("ot", i, s) for s in range(4)])

    def phase_outproj(self, AT, KC, W, l, j_gate, x_src, x_dst):
        D, T = self.D, self.T
        TB = 1024
        PW = 512 if KC <= 32 else 256
        with self.phase("oproj") as P:
            aT = P.sb("aT", [128, KC, TB], BF16)
            wpan = [P.sb("wpan%d" % i, [128, KC, PW], BF16) for i in range(2)]
            gate = P.sb("gate", [128, 2048], F32)
            xp = [P.sb("xp%d" % i, [128, PW], F32) for i in range(4)]
            tm = [P.sb("tm%d" % i, [128, PW], F32) for i in range(4)]
            P.banks(6)
            T.dma(DMA(gate[:], D["modsb"][l, j_gate]), writes=["gate"])
            g = 0
            wi = 0
            for tt in range(S // TB):
                T.dma(DMA(aT[:], AT[:, tt * TB:(tt + 1) * TB].rearrange("(kc p) t -> p kc t", p=128)), writes=["aT"])
                for n in range(2048 // PW):
                    w = wi % 2
                    wi += 1
                    cs = slice(n * PW, (n + 1) * PW)
                    T.dma(DMA(wpan[w][:], W[:, cs].rearrange("(kc p) n -> p kc n", p=128)), writes=[("wpan", w)])
                    for s in range(TB // 128):
                        pi = P.nextp()
                        b = g % 4
                        g += 1
                        r0 = tt * TB + s * 128
                        T.dma(DMA(xp[b][:], x_src[r0:r0 + 128, cs]), writes=[("xp", b)])
                        self.mmg(P.pb[pi][:, 0:PW], [(aT[:, kc, s * 128:(s + 1) * 128], wpan[w][:, kc, :]) for kc in range(KC)],
                                 ["aT", ("wpan", w)], ("pb", pi))
                        T.op("dve", TT(tm[b][:], P.pb[pi][:, 0:PW], gate[:, cs], ALU.mult), reads=[("pb", pi), "gate"], writes=[("tm", b)])
                        T.op("pool", TT(xp[b][:], xp[b][:], tm[b][:], ALU.add), reads=[("xp", b), ("tm", b)], writes=[("xp", b)])
                        T.dma(DMA(x_dst[r0:r0 + 128, cs], xp[b][:]), reads=[("xp", b)])

    def phase_ffnup(self, l):
        D, T = self.D, self.T
        W = D["b_ffn_wup"]
        with self.phase("ffnup") as P:
            aTb = P.sb("aTb", [128, 16, 2048], BF16)
            wu = [P.sb("wu%d" % i, [128, 16, 512], BF16) for i in range(2)]
            wg = [P.sb("wg%d" % i, [128, 16, 512], BF16) for i in range(2)]
            taps = P.sb("taps", [128, 44, 3], F32)
            cb = P.sb("cb", [128, 44], F32)
            halo = P.sb("halo", [128, 44, 2], F32)
            gext = [P.sb("gext%d" % i, [128, 514], F32) for i in range(4)]
            acc = [P.sb("acc%d" % i, [128, 512], F32) for i in range(4)]
            sg = [P.sb("sg%d" % i, [128, 512], F32) for i in range(4)]
            hst = [P.sb("hst%d" % i, [128, 4, 512], BF16) for i in range(3)]
            P.banks(8)
            T.dma(DMA(taps[:], D["ffn_cw"][l]), writes=["taps"])
            T.dma(DMA(cb[:], D["ffn_cb"][l]), writes=["cb"])
            T.op("pool", MSET(halo[:], 0.0), writes=[("halo", j) for j in range(44)])
            g = 0
            wi = 0
            hi = 0
            pend = []
            for TT_ in range(2):
                T.dma(DMA(aTb[:], D["HMT"][:, TT_ * 2048:(TT_ + 1) * 2048].rearrange("(kc p) t -> p kc t", p=128)), writes=["aT"])
                for pn in range(11):
                    w = wi % 2
                    wi += 1
                    T.dma(DMA(wu[w][:], W[:, pn * 512:(pn + 1) * 512].rearrange("(kc p) n -> p kc n", p=128)), writes=[("wu", w)])
                    T.dma(DMA(wg[w][:], W[:, FH + pn * 512:FH + (pn + 1) * 512].rearrange("(kc p) n -> p kc n", p=128)), writes=[("wg", w)])
                    for sub in range(4):
                        tt = TT_ * 4 + sub
                        ss_ = slice(sub * 512, (sub + 1) * 512)
                        hb = hi % 3
                        hi += 1
                        for m in range(4):
                            j = pn * 4 + m
                            b = g % 4
                            g += 1
                            pu = P.nextp()
                            pg_ = P.nextp()
                            self.mmg(P.pb[pg_][:], [(wg[w][:, kc, m * 128:(m + 1) * 128], aTb[:, kc, ss_]) for kc in range(16)],
                                     ["aT", ("wg", w)], ("pb", pg_))
                            self.mmg(P.pb[pu][:], [(wu[w][:, kc, m * 128:(m + 1) * 128], aTb[:, kc, ss_]) for kc in range(16)],
                                     ["aT", ("wu", w)], ("pb", pu))
                            ge = gext[b]
                            T.op("pool", CP(ge[:, 0:2], halo[:, j, :]), reads=[("halo", j)], writes=[("gext", b)])
                            T.op("act", ACP(ge[:, 2:514], P.pb[pg_][:]), reads=[("pb", pg_)], writes=[("gext", b)])
                            T.op("pool", CP(halo[:, j, :], ge[:, 512:514]), reads=[("gext", b)], writes=[("halo", j)])
                            T.op("dve", TS(acc[b][:], ge[:, 0:512], taps[:, j, 0:1], cb[:, j:j + 1], ALU.mult, ALU.add),
                                 reads=[("gext", b), "taps", "cb"], writes=[("acc", b)])
                            T.op("dve", STT(acc[b][:], ge[:, 1:513], taps[:, j, 1:2], acc[b][:], ALU.mult, ALU.add),
                                 reads=[("gext", b), ("acc", b), "taps"], writes=[("acc", b)])
                            T.op("dve", STT(acc[b][:], ge[:, 2:514], taps[:, j, 2:3], acc[b][:], ALU.mult, ALU.add),
                                 reads=[("gext", b), ("acc", b), "taps"], writes=[("acc", b)])
                            for fn in pend:
                                fn()
                            pend = []

                            def back(b=b, pu=pu, hb=hb, m=m):
                                T.op("act", ACTF(sg[b][:], acc[b][:], AF.Silu), reads=[("acc", b)], writes=[("sg", b)])
                                T.op("dve", TT(hst[hb][:, m, :], P.pb[pu][:], sg[b][:], ALU.mult), reads=[("pb", pu), ("sg", b)], writes=[("hst", hb, m)])
                            pend.append(back)
                            if m == 3:
                                def store(pn=pn, tt=tt, hb=hb):
                                    T.dma(DMA(D["HT"][pn * 512:(pn + 1) * 512, tt * 512:(tt + 1) * 512].rearrange("(m p) t -> p m t", p=128), hst[hb][:]),
                                          reads=[("hst", hb, m_) for m_ in range(4)])
                                pend.append(store)
            for fn in pend:
                fn()

    def phase_hybproj(self, l):
        D, T = self.D, self.T
        i_ = l // 2
        W = D["b_hyb_win"]
        TB = 1024
        NSUB = TB // 512
        with self.phase("hproj") as P:
            aTb = P.sb("aTb", [128, 16, TB], BF16)
            wpan = [P.sb("wpan%d" % i, [128, 16, 512], BF16) for i in range(2)]
            cosr = [P.sb("cosr%d" % i, [128, 512], F32) for i in range(NSUB)]
            sinr = [P.sb("sinr%d" % i, [128, 512], F32) for i in range(NSUB)]
            c2 = [P.sb("c2%d" % i, [128, 512], F32) for i in range(NSUB)]
            s2 = [P.sb("s2%d" % i, [128, 512], F32) for i in range(NSUB)]
            xi = P.sb("xi", [128, 4, 128], F32)
            gq = P.sb("gq", [128, 4], F32)
            gkv = P.sb("gkv", [128, 4], F32)
            onesf = P.sb("onesf", [128, 128], F32)
            mt8 = [P.sb("mt%d" % i, [128, 512], F32) for i in range(8)]
            mt = mt8[0:4]
            st4 = [P.sb("st4%d" % i, [128, 4, 512], BF16) for i in range(2)]
            sx4 = [P.sb("sx4%d" % i, [128, 4, 512], BF16) for i in range(2)]
            craw = P.sb("craw", [128, 4, 512], F32)
            sq = P.sb("sq", [128, 4, 512], F32)
            sdt = P.sb("sdt", [128, 512], F32)
            rst = P.sb("rst", [128, 512], F32)
            cst = [P.sb("cst%d" % i, [128, 4, 512], BF16) for i in range(2)]
            krst = [P.sb("krst%d" % i, [128, 512], BF16) for i in range(2)]
            tst = [P.sb("tst%d" % i, [128, 512], BF16) for i in range(4)]
            P.banks(8)
            T.dma(DMA(xi[:], D["XI"]), writes=["xi"])
            T.dma(DMA(gq[:], D["hyb_qg"][i_]), writes=["gq"])
            T.dma(DMA(gkv[:], D["hyb_kvg"][i_]), writes=["gkv"])
            T.op("pool", MSET(onesf[:], 1.0), writes=["onesf"])
            wi = 0
            tg = 0
            sti = 0
            csi = 0
            for TT_ in range(S // TB):
                T.dma(DMA(aTb[:], D["HMT"][:, TT_ * TB:(TT_ + 1) * TB].rearrange("(kc p) t -> p kc t", p=128)), writes=["aT"])
                for sub in range(NSUB):
                    ts_ = slice(TT_ * TB + sub * 512, TT_ * TB + (sub + 1) * 512)
                    T.dma(DMA(cosr[sub][:], D["COSR"][:, ts_]), writes=[("cosr", sub)])
                    T.dma(DMA(sinr[sub][:], D["SINR"][:, ts_]), writes=[("sinr", sub)])
                    T.dma(DMA(c2[sub][:], D["C2"][:, ts_]), writes=[("c2", sub)])
                    T.dma(DMA(s2[sub][:], D["S2"][:, ts_]), writes=[("s2", sub)])

                def load_panel(c0, ncols=512):
                    nonlocal wi
                    w = wi % 2
                    wi += 1
                    T.dma(DMA(wpan[w][:, :, 0:ncols], W[:, c0:c0 + ncols].rearrange("(kc p) n -> p kc n", p=128)), writes=[("wpan", w)])
                    return w

                def fm(w, m, sub):
                    pi = P.nextp()
                    self.mmg(P.pb[pi][:], [(wpan[w][:, kc, m * 128:(m + 1) * 128], aTb[:, kc, sub * 512:(sub + 1) * 512]) for kc in range(16)],
                             ["aT", ("wpan", w)], ("pb", pi))
                    return pi

                def tm_(w, s, sub):
                    pi = P.nextp()
                    t0 = sub * 512 + s * 128
                    self.mmg(P.pb[pi][:], [(aTb[:, kc, t0:t0 + 128], wpan[w][:, kc, :]) for kc in range(16)],
                             ["aT", ("wpan", w)], ("pb", pi))
                    return pi

                for which in range(2):
                    for pn in range(2):
                        w = load_panel(which * 1024 + pn * 512)
                        for sub in range(NSUB):
                            ts_ = slice(TT_ * TB + sub * 512, TT_ * TB + (sub + 1) * 512)
                            sb_ = sti % 2
                            sti += 1
                            st = st4[sb_]
                            sx = sx4[sb_]
                            for hh in range(2):
                                h = pn * 2 + hh
                                p1 = fm(w, hh * 2, sub)
                                p2 = fm(w, hh * 2 + 1, sub)
                                rd1 = [("pb", p1), ("cosr", sub), ("sinr", sub)]
                                rd2 = [("pb", p2), ("cosr", sub), ("sinr", sub)]
                                mo = 4 * hh
                                ma, mb_, mc, md = mt8[mo], mt8[mo + 1], mt8[mo + 2], mt8[mo + 3]
                                T.op("dve", TT(ma[:], P.pb[p1][:], cosr[sub][:], ALU.mult), reads=rd1, writes=[("mt", mo)])
                                T.op("dve", TT(mb_[:], P.pb[p2][:], sinr[sub][:], ALU.mult), reads=rd2, writes=[("mt", mo + 1)])
                                T.op("dve", TT(mc[:], P.pb[p1][:], sinr[sub][:], ALU.mult), reads=rd1, writes=[("mt", mo + 2)])
                                T.op("dve", TT(md[:], P.pb[p2][:], cosr[sub][:], ALU.mult), reads=rd2, writes=[("mt", mo + 3)])
                                T.op("pool", TT(st[:, 2 * hh, :], ma[:], mb_[:], ALU.subtract), reads=[("mt", mo), ("mt", mo + 1)], writes=[("st", sb_, 2 * hh)])
                                T.op("pool", TT(st[:, 2 * hh + 1, :], mc[:], md[:], ALU.add), reads=[("mt", mo + 2), ("mt", mo + 3)], writes=[("st", sb_, 2 * hh + 1)])
                                for cc in (2 * hh, 2 * hh + 1):
                                    if which == 1:
                                        T.op("act", ACTF(st[:, cc, :], st[:, cc, :], AF.Copy, scale=1.0 / 16.0), reads=[("st", sb_, cc)], writes=[("st", sb_, cc)])
                                    else:
                                        T.op("pool", TT(sx[:, cc, :].rearrange("p (c l) -> p c l", l=128), st[:, cc, :].rearrange("p (c l) -> p c l", l=128),
                                                        xi[:, h, :].unsqueeze(1).broadcast_to([128, 4, 128]), ALU.mult),
                                             reads=[("st", sb_, cc), "xi"], writes=[("sx", sb_, cc)])
                            dst = D["RQT"] if which == 0 else D["RKT"]
                            rows = slice(pn * 512, (pn + 1) * 512)
                            T.dma(DMA(dst[rows, ts_].rearrange("(c p) t -> p c t", p=128), st[:]), reads=[("st", sb_, cc) for cc in range(4)])
                            if which == 0:
                                T.dma(DMA(D["RQXT"][rows, ts_].rearrange("(c p) t -> p c t", p=128), sx[:]), reads=[("sx", sb_, cc) for cc in range(4)])
                for which in range(2):
                    dst = D["RV"] if which == 0 else D["RG"]
                    for pn in range(4):
                        w = load_panel(2048 + which * 2048 + pn * 512)
                        for sub in range(NSUB):
                            for s in range(4):
                                pi = tm_(w, s, sub)
                                b = tg % 4
                                tg += 1
                                if which == 0:
                                    T.op("act", ACP(tst[b][:], P.pb[pi][:]), reads=[("pb", pi)], writes=[("tst", b)])
                                else:
                                    T.op("act", ACTF(tst[b][:], P.pb[pi][:], AF.Silu), reads=[("pb", pi)], writes=[("tst", b)])
                                r0 = TT_ * TB + sub * 512 + s * 128
                                T.dma(DMA(dst[r0:r0 + 128, pn * 512:(pn + 1) * 512], tst[b][:]), reads=[("tst", b)])
                for which in range(2):
                    w = load_panel(6144 + which * 512)
                    gcol = gq if which == 0 else gkv
                    dst = D["CQT"] if which == 0 else D["CKVT"]
                    for sub in range(NSUB):
                        ts_ = slice(TT_ * TB + sub * 512, TT_ * TB + (sub + 1) * 512)
                        cb_ = csi % 2
                        csi += 1
                        for c in range(4):
                            pi = fm(w, c, sub)
                            T.op("act", ACP(craw[:, c, :], P.pb[pi][:]), reads=[("pb", pi)], writes=[("craw", c)])
                            T.op("act", ACTF(sq[:, c, :], P.pb[pi][:], AF.Square), reads=[("pb", pi)], writes=[("sq", c)])
                        pi = P.nextp()
                        self.mmg(P.pb[pi][:], [(onesf[:], sq[:, c, :]) for c in range(4)], ["onesf"] + [("sq", c) for c in range(4)], ("pb", pi))
                        T.op("act", ACTF(sdt[:], P.pb[pi][:], AF.Sqrt, scale=1.0 / 512, bias=EPS), reads=[("pb", pi)], writes=["sdt"])
                        T.op("dve", RECIP(rst[:], sdt[:]), reads=["sdt"], writes=["rst"])
                        for c in range(4):
                            T.op("dve", STT(cst[cb_][:, c, :], craw[:, c, :], gcol[:, c:c + 1], rst[:], ALU.mult, ALU.mult),
                                 reads=[("craw", c), "rst", "gq", "gkv"], writes=[("cst", cb_, c)])
                        T.dma(DMA(dst[:, ts_].rearrange("(c p) t -> p c t", p=128), cst[cb_][:]), reads=[("cst", cb_, c) for c in range(4)])
                w = load_panel(7168, 256)
                for sub in range(NSUB):
                    ts_ = slice(TT_ * TB + sub * 512, TT_ * TB + (sub + 1) * 512)
                    pa = fm(w, 0, sub)
                    pb_ = fm(w, 1, sub)
                    T.op("dve", TT(mt[0][:], P.pb[pa][:], c2[sub][:], ALU.mult), reads=[("pb", pa), ("c2", sub)], writes=[("mt", 0)])
                    T.op("dve", TT(mt[1][:], P.pb[pb_][:], s2[sub][:], ALU.mult), reads=[("pb", pb_), ("s2", sub)], writes=[("mt", 1)])
                    T.op("pool", TT(krst[sub][:], mt[0][:], mt[1][:], ALU.add), reads=[("mt", 0), ("mt", 1)], writes=[("krst", sub)])
                    T.dma(DMA(D["KRT"][:, ts_], krst[sub][:]), reads=[("krst", sub)])

    def phase_ret(self, l, consts):
        D, T = self.D, self.T
        i_ = l // 2
        g128 = consts["g128"]
        with self.phase("ret") as P:
            ident = P.sb("ident", [128, 128], BF16)
            decT = P.sb("decT", [128, 4, 128], F32)
            zeta = P.sb("zeta", [128, 4], F32)
            gnb = P.sb("gnb", [128, 2048], F32)
            grow = P.sb("grow", [1, 2048], F32)
            ones1 = P.sb("ones1", [1, 128], F32)
            qT = [P.sb("qT%d" % i, [128, 8, 512], BF16) for i in range(2)]
            qxT = [P.sb("qxT%d" % i, [128, 8, 512], BF16) for i in range(2)]
            kT = [P.sb("kT%d" % i, [128, 8, 512], BF16) for i in range(2)]
            v = [P.sb("v%d" % i, [128, 2048], BF16) for i in range(2)]
            rg = [P.sb("rg%d" % i, [128, 2048], BF16) for i in range(2)]
            ktm = [P.sb("ktm%d" % i, [128, 1024], BF16) for i in range(2)]
            R32 = P.sb("R32", [128, 8, 512], F32)
            Rb = P.sb("Rb", [128, 8, 512], BF16)
            A = [P.sb("A%d" % i, [128, 128], BF16) for i in range(2)]
            st6 = [P.sb("st6%d" % i, [128, 6], F32) for i in range(2)]
            mv = [P.sb("mv%d" % i, [128, 2], F32) for i in range(2)]
            sd = [P.sb("sd%d" % i, [128, 1], F32) for i in range(2)]
            rs = [P.sb("rs%d" % i, [128, 1], F32) for i in range(2)]
            nm = [P.sb("nm%d" % i, [128, 1], F32) for i in range(2)]
            yn = [P.sb("yn%d" % i, [128, 512], F32) for i in range(2)]
            yg = [P.sb("yg%d" % i, [128, 512], F32) for i in range(2)]
            yr = [P.sb("yr%d" % i, [128, 2048], BF16) for i in range(2)]
            yst = [P.sb("yst%d" % i, [128, 16, 512], BF16) for i in range(2)]
            P.banks(6)
            pt = [P.ps("pt%d" % i, [128, 1024], BF16) for i in range(2)]
            T.dma(DMA(ident[:], D["ident"]), writes=["ident"])
            T.dma(DMA(decT[:], D["DECT"]), writes=["decT"])
            T.dma(DMA(zeta[:], D["ZETA"]), writes=["zeta"])
            T.dma(DMA(grow[:], D["hyb_gn"][i_]), writes=["grow"])
            T.op("pool", MSET(ones1[:], 1.0), writes=["ones1"])
            T.op("pool", MSET(R32[:], 0.0), writes=[("R32", c) for c in range(8)])
            T.op("pool", MSET(Rb[:], 0.0), writes=[("Rb", c) for c in range(8)])
            for n in range(4):
                pi = P.nextp()
                self.mmg(P.pb[pi][:], [(ones1[0:1, :], grow[0:1, n * 512:(n + 1) * 512])], ["ones1", "grow"], ("pb", pi))
                T.op("act", ACP(gnb[:, n * 512:(n + 1) * 512], P.pb[pi][:]), reads=[("pb", pi)], writes=["gnb"])
            tp = 0
            for tt in range(NT):
                a = tt % 2
                ts_ = slice(tt * 512, (tt + 1) * 512)
                T.dma(DMA(qT[a][:], D["RQT"][:, ts_].rearrange("(c p) t -> p c t", p=128)), writes=[("qT", a)])
                T.dma(DMA(qxT[a][:], D["RQXT"][:, ts_].rearrange("(c p) t -> p c t", p=128)), writes=[("qxT", a)])
                T.dma(DMA(kT[a][:], D["RKT"][:, ts_].rearrange("(c p) t -> p c t", p=128)), writes=[("kT", a)])
                for cs in range(4):
                    ck = tt * 4 + cs
                    b = ck % 2
                    cs_ = slice(cs * 128, (cs + 1) * 128)
                    r0 = ck * 128
                    T.dma(DMA(v[b][:], D["RV"][r0:r0 + 128, :]), writes=[("v", b)])
                    T.dma(DMA(rg[b][:], D["RG"][r0:r0 + 128, :]), writes=[("rg", b)])
                    for half in range(2):
                        pi = tp % 2
                        tp += 1
                        for c4 in range(4):
                            c = half * 4 + c4
                            T.op("pe", TR(pt[pi][:, c4 * 128:(c4 + 1) * 128], kT[a][:, c, cs_], ident[:]), reads=[("kT", a), "ident"], writes=[("pt", pi)], event=(c4 == 3))
                        for hh in range(2):
                            h = half * 2 + hh
                            T.op("act", ACTF(ktm[b][:, h * 256:(h + 1) * 256], pt[pi][:, hh * 256:(hh + 1) * 256], AF.Copy, scale=zeta[:, h:h + 1]),
                                 reads=[("pt", pi), "zeta"], writes=[("ktm", b, h)])
                    for h in range(4):
                        hb = h % 2
                        pi = P.nextp()
                        self.mmg(P.pb[pi][:, 0:128], [(kT[a][:, 2 * h + dc, cs_], qT[a][:, 2 * h + dc, cs_]) for dc in range(2)],
                                 [("kT", a), ("qT", a)], ("pb", pi))
                        T.op("dve", TT(A[hb][:], P.pb[pi][:, 0:128], decT[:, h, :], ALU.mult), reads=[("pb", pi), "decT"], writes=[("A", hb)])
                        py = P.nextp()
                        pairs = [(A[hb][:], v[b][:, h * 512:(h + 1) * 512])] + [(qxT[a][:, 2 * h + dc, cs_], Rb[:, 2 * h + dc, :]) for dc in range(2)]
                        self.mmg(P.pb[py][:], pairs, [("A", hb), ("v", b), ("qxT", a), ("Rb", 2 * h), ("Rb", 2 * h + 1)], ("pb", py))
                        for dc in range(2):
                            c = 2 * h + dc
                            pr = P.nextp()
                            self.mmg(P.pb[pr][:], [(ktm[b][:, c * 128:(c + 1) * 128], v[b][:, h * 512:(h + 1) * 512])], [("ktm", b, h), ("v", b)], ("pb", pr))
                            T.op("dve", STT(R32[:, c, :], R32[:, c, :], float(g128[h]), P.pb[pr][:], ALU.mult, ALU.add),
                                 reads=[("R32", c), ("pb", pr)], writes=[("R32", c)])
                            T.op("act", ACP(Rb[:, c, :], R32[:, c, :]), reads=[("R32", c)], writes=[("Rb", c)])
                        T.op("dve", lambda e, o=st6[hb], i=P.pb[py]: e.bn_stats(out=o[:], in_=i[:]), reads=[("pb", py)], writes=[("st6", hb)])
                        T.op("dve", lambda e, o=mv[hb], i=st6[hb]: e.bn_aggr(out=o[:], in_=i[:]), reads=[("st6", hb)], writes=[("mv", hb)])
                        T.op("act", ACTF(sd[hb][:], mv[hb][:, 1:2], AF.Sqrt, scale=1.0, bias=EPS), reads=[("mv", hb)], writes=[("sd", hb)])
                        T.op("dve", RECIP(rs[hb][:], sd[hb][:]), reads=[("sd", hb)], writes=[("rs", hb)])
                        T.op("dve", STT(nm[hb][:], mv[hb][:, 0:1], -1.0, rs[hb][:], ALU.mult, ALU.mult), reads=[("mv", hb), ("rs", hb)], writes=[("nm", hb)])
                        T.op("act", ACTF(yn[hb][:], P.pb[py][:], AF.Identity, scale=rs[hb][:, 0:1], bias=nm[hb][:, 0:1]),
                             reads=[("pb", py), ("rs", hb), ("nm", hb)], writes=[("yn", hb)])
                        T.op("pool", TT(yg[hb][:], yn[hb][:], gnb[:, h * 512:(h + 1) * 512], ALU.mult), reads=[("yn", hb), "gnb"], writes=[("yg", hb)])
                        T.op("pool", TT(yr[b][:, h * 512:(h + 1) * 512], yg[hb][:], rg[b][:, h * 512:(h + 1) * 512], ALU.mult),
                             reads=[("yg", hb), ("rg", b)], writes=[("yr", b, h)])
                    for q4 in range(4):
                        pi = tp % 2
                        tp += 1
                        for c4 in range(4):
                            c = q4 * 4 + c4
                            T.op("pe", TR(pt[pi][:, c4 * 128:(c4 + 1) * 128], yr[b][:, c * 128:(c + 1) * 128], ident[:]),
                                 reads=[("yr", b, q4), "ident"], writes=[("pt", pi)], event=(c4 == 3))
                        src = pt[pi][:, 0:512].rearrange("p (c t) -> p c t", t=128)
                        if q4 % 2 == 0:
                            T.op("dve", CP(yst[a][:, q4 * 4:(q4 + 1) * 4, cs_], src), reads=[("pt", pi)], writes=[("yst", a, cs, q4)])
                        else:
                            T.op("act", ACP(yst[a][:, q4 * 4:(q4 + 1) * 4, cs_], src), reads=[("pt", pi)], writes=[("yst", a, cs, q4)])
                T.dma(DMA(D["YT"][0:2048, ts_].rearrange("(c p) t -> p c t", p=128), yst[a][:]),
                      reads=[("yst", a, cs, q4) for cs in range(4) for q4 in range(4)])

    def phase_mla(self, l):
        D, T = self.D, self.T
        Wq = D["b_hyb_wuq"]
        Wkv = D["b_hyb_wukv"]
        scale = (128 + 64) ** -0.5
        with self.phase("mla") as P:
            krt = P.sb("krt", [128, S], BF16)
            ones = P.sb("ones", [128, 128], BF16)
            maskT = P.sb("maskT", [128, 128], BF16)
            cq = [P.sb("cq%d" % i, [128, 4, 512], BF16) for i in range(2)]
            ckv = [P.sb("ckv%d" % i, [128, 4, 512], BF16) for i in range(2)]
            c2 = [P.sb("c2%d" % i, [128, 512], F32) for i in range(2)]
            s2 = [P.sb("s2%d" % i, [128, 512], F32) for i in range(2)]
            wq = [P.sb("wq%d" % i, [128, 4, 512], BF16) for i in range(2)]
            wkv = [P.sb("wkv%d" % i, [128, 4, 512], BF16) for i in range(2)]
            qn = [P.sb("qn%d" % i, [128, 2, S], BF16) for i in range(2)]
            qr = [P.sb("qr%d" % i, [128, S], BF16) for i in range(2)]
            kn = [P.sb("kn%d" % i, [128, 2, S], BF16) for i in range(2)]
            vv = [P.sb("vv%d" % i, [128, 32, 256], BF16) for i in range(2)]
            mt = [P.sb("mt%d" % i, [128, 512], F32) for i in range(2)]
            pp = [P.sb("pp%d" % i, [128, 512], BF16) for i in range(4)]
            rden = [P.sb("rden%d" % i, [128, 512], F32) for i in range(2)]
            ost = [P.sb("ost%d" % i, [128, 512], BF16) for i in range(2)]
            P.banks(8)
            T.dma(DMA(krt[:], D["KRT"]), writes=["krt"])
            T.dma(DMA(maskT[:], D["MASKT"]), writes=["maskT"])
            T.op("pool", MSET(ones[:], 1.0), writes=["ones"])
            prep_rr = 0
            sc_rr = 0
            pp_rr = 0
            for hp in range(8):
                x = hp % 2
                T.dma(DMA(wq[x][:, :, 0:256], Wq[:, hp * 256:(hp + 1) * 256].rearrange("(kc p) n -> p kc n", p=128)), writes=[("wq", x)])
                T.dma(DMA(wq[x][:, :, 256:384], Wq[:, 2048 + hp * 128:2048 + (hp + 1) * 128].rearrange("(kc p) n -> p kc n", p=128)), writes=[("wq", x)])
                T.dma(DMA(wq[x][:, :, 384:512], Wq[:, 3072 + hp * 128:3072 + (hp + 1) * 128].rearrange("(kc p) n -> p kc n", p=128)), writes=[("wq", x)])
                T.dma(DMA(wkv[x][:, :, 0:256], Wkv[:, hp * 256:(hp + 1) * 256].rearrange("(kc p) n -> p kc n", p=128)), writes=[("wkv", x)])
                T.dma(DMA(wkv[x][:, :, 256:512], Wkv[:, 2048 + hp * 256:2048 + (hp + 1) * 256].rearrange("(kc p) n -> p kc n", p=128)), writes=[("wkv", x)])
                for tt in range(NT):
                    a = tt % 2
                    ts_ = slice(tt * 512, (tt + 1) * 512)
                    T.dma(DMA(cq[a][:], D["CQT"][:, ts_].rearrange("(c p) t -> p c t", p=128)), writes=[("cq", a)])
                    T.dma(DMA(ckv[a][:], D["CKVT"][:, ts_].rearrange("(c p) t -> p c t", p=128)), writes=[("ckv", a)])
                    T.dma(DMA(c2[a][:], D["C2"][:, ts_]), writes=[("c2", a)])
                    T.dma(DMA(s2[a][:], D["S2"][:, ts_]), writes=[("s2", a)])

                    def prep(wt, wname, m, src, sname):
                        nonlocal prep_rr
                        pi = 7
                        prep_rr += 1
                        self.mmg(P.pb[pi][:], [(wt[:, kc, m * 128:(m + 1) * 128], src[:, kc, :]) for kc in range(4)],
                                 [(wname, x), (sname, a)], ("pb", pi))
                        return pi
                    for hh in range(2):
                        pi = prep(wq[x], "wq", hh, cq[a], "cq")
                        T.op("act", ACP(qn[x][:, hh, ts_], P.pb[pi][:]), reads=[("pb", pi)], writes=[("qn", x, tt)])
                        pi = prep(wkv[x], "wkv", hh, ckv[a], "ckv")
                        T.op("dve", CP(kn[x][:, hh, ts_], P.pb[pi][:]), reads=[("pb", pi)], writes=[("kn", x, tt)])
                    pa = prep(wq[x], "wq", 2, cq[a], "cq")
                    T.op("dve", TT(mt[0][:], P.pb[pa][:], c2[a][:], ALU.mult), reads=[("pb", pa), ("c2", a)], writes=[("mt", 0)])
                    pb_ = prep(wq[x], "wq", 3, cq[a], "cq")
                    T.op("dve", TT(mt[1][:], P.pb[pb_][:], s2[a][:], ALU.mult), reads=[("pb", pb_), ("s2", a)], writes=[("mt", 1)])
                    T.op("pool", TT(qr[x][:, ts_], mt[0][:], mt[1][:], ALU.add), reads=[("mt", 0), ("mt", 1)], writes=[("qr", x, tt)])
                    for s in range(4):
                        pi = 7
                        prep_rr += 1
                        self.mmg(P.pb[pi][:, 0:256], [(ckv[a][:, kc, s * 128:(s + 1) * 128], wkv[x][:, kc, 256:512]) for kc in range(4)],
                                 [("wkv", x), ("ckv", a)], ("pb", pi))
                        T.op("act", ACP(vv[x][:, tt * 4 + s, :], P.pb[pi][:, 0:256]), reads=[("pb", pi)], writes=[("vv", x, tt)])
                LOOK = 2
                for hh in range(2):
                    h = hp * 2 + hh
                    po = slice(hh * 64, (hh + 1) * 64)
                    tiles = [(qt, kt) for qt in range(NT) for kt in range(4 * qt + 4)]
                    info = {}

                    def front(i):
                        nonlocal sc_rr, pp_rr
                        qt, kt = tiles[i]
                        ks = slice(kt * 128, (kt + 1) * 128)
                        j = kt - 4 * qt
                        q0 = 0 if j <= 0 else j * 128
                        si = sc_rr % 3
                        sc_rr += 1
                        qa = slice(qt * 512 + q0, (qt + 1) * 512)
                        rdq = [("qn", x, qt), ("qr", x, qt), ("kn", x, kt // 4), "krt"]
                        self.mmg(P.pb[si][:, q0:512], [(kn[x][:, hh, ks], qn[x][:, hh, qa]), (krt[po, ks], qr[x][po, qa])], rdq, ("pb", si))
                        pb_i = pp_rr % 4
                        pp_rr += 1
                        T.op("act", ACTF(pp[pb_i][:, q0:512], P.pb[si][:, q0:512], AF.Exp, scale=scale), reads=[("pb", si)], writes=[("pp", pb_i)])
                        if j >= 0:
                            T.op("pool", TT(pp[pb_i][:, j * 128:(j + 1) * 128], pp[pb_i][:, j * 128:(j + 1) * 128], maskT[:], ALU.mult),
                                 reads=[("pp", pb_i), "maskT"], writes=[("pp", pb_i)])
                        info[i] = (pb_i, q0)

                    def back(i):
                        qt, kt = tiles[i]
                        pb_i, q0 = info.pop(i)
                        nk = 4 * qt + 4
                        ob = 3 + (h * NT + qt) % 2
                        db = 5 + (h * NT + qt) % 2
                        first = (kt == 0)
                        last = (kt == nk - 1)
                        T.op("pe", MM(P.pb[ob][:, q0:512], vv[x][:, kt, hh * 128:(hh + 1) * 128], pp[pb_i][:, q0:512], first, last),
                             reads=[("pp", pb_i), ("vv", x, kt // 4)], writes=[("pb", ob)], event=last)
                        T.op("pe", MM(P.pb[db][:, q0:512], ones[:], pp[pb_i][:, q0:512], first, last),
                             reads=[("pp", pb_i), "ones"], writes=[("pb", db)], event=True)
                        if last:
                            qs = slice(qt * 512, (qt + 1) * 512)
                            r = (h * NT + qt) % 2
                            T.op("dve", RECIP(rden[r][:], P.pb[db][:]), reads=[("pb", db)], writes=[("rden", r)])
                            T.op("dve", TT(ost[r][:], P.pb[ob][:], rden[r][:], ALU.mult), reads=[("pb", ob), ("rden", r)], writes=[("ost", r)])
                            T.dma(DMA(D["YT"][2048 + h * 128:2048 + (h + 1) * 128, qs], ost[r][:]), reads=[("ost", r)])

                    for i in range(len(tiles) + LOOK):
                        if i < len(tiles):
                            front(i)
                        if i >= LOOK:
                            back(i - LOOK)

    def phase_ssdproj(self, l):
        D, T = self.D, self.T
        i_ = l // 2
        W = D["b_ssd_win"]
        TB = 2048
        NSUB = TB // 512
        with self.phase("sproj") as P:
            ident = P.sb("ident", [128, 128], BF16)
            aTb = P.sb("aTb", [128, 16, TB], BF16)
            wpan = [P.sb("wpan%d" % i, [128, 16, 512], BF16) for i in range(2)]
            taps = P.sb("taps", [128, 48, 4], F32)
            cb = P.sb("cb", [128, 48], F32)
            halo = P.sb("halo", [128, 48, 3], F32)
            gext = [P.sb("gext%d" % i, [128, 515], F32) for i in range(4)]
            acc = [P.sb("acc%d" % i, [128, 512], F32) for i in range(4)]
            sgb = [P.sb("sgb%d" % i, [128, 512], BF16) for i in range(4)]
            tst = [P.sb("tst%d" % i, [128, 512], BF16) for i in range(4)]
            xst = [P.sb("xst%d" % i, [128, 4, 512], BF16) for i in range(3)]
            bcst = [P.sb("bcst%d" % i, [128, 4, 512], BF16) for i in range(3)]
            dtb = P.sb("dtb", [128, 64], F32)
            ab = P.sb("ab", [128, 64], F32)
            row = P.sb("row", [1, 128], F32)
            ones1 = P.sb("ones1", [1, 128], F32)
            dt1 = [P.sb("dt1%d" % i, [128, 64], F32) for i in range(2)]
            dt2 = [P.sb("dt2%d" % i, [128, 2, 64], F32) for i in range(2)]
            P.banks(6)
            pt = [P.ps("pt%d" % i, [128, 1024], BF16) for i in range(2)]
            T.dma(DMA(ident[:], D["ident"]), writes=["ident"])
            T.dma(DMA(taps[:], D["ssd_cw"][i_]), writes=["taps"])
            T.dma(DMA(cb[:], D["ssd_cb"][i_]), writes=["cb"])
            T.dma(DMA(row[:, 0:64], D["ssd_dtb"][i_]), writes=["row"])
            T.dma(DMA(row[:, 64:128], D["ssd_alog"][i_]), writes=["row"])
            T.op("pool", MSET(ones1[:], 1.0), writes=["ones1"])
            T.op("pool", MSET(halo[:], 0.0), writes=[("halo", j) for j in range(48)])
            pi = P.nextp()
            self.mmg(P.pb[pi][:, 0:128], [(ones1[0:1, :], row[0:1, :])], ["ones1", "row"], ("pb", pi))
            T.op("act", ACP(dtb[:], P.pb[pi][:, 0:64]), reads=[("pb", pi)], writes=["dtb"])
            T.op("act", ACTF(ab[:], P.pb[pi][:, 64:128], AF.Exp), reads=[("pb", pi)], writes=["ab"])
            T.op("dve", (lambda e: e.tensor_scalar_mul(out=ab[:], in0=ab[:], scalar1=-1.0)), reads=["ab"], writes=["ab"])
            wi = 0
            tg = 0
            g = 0
            tp = 0
            xi_ = 0
            pend = []
            for TT_ in range(S // TB):
                T.dma(DMA(aTb[:], D["HMT"][:, TT_ * TB:(TT_ + 1) * TB].rearrange("(kc p) t -> p kc t", p=128)), writes=["aT"])

                def load_panel(c0, ncols=512):
                    nonlocal wi
                    w = wi % 2
                    wi += 1
                    T.dma(DMA(wpan[w][:, :, 0:ncols], W[:, c0:c0 + ncols].rearrange("(kc p) n -> p kc n", p=128)), writes=[("wpan", w)])
                    return w
                for pn in range(8):
                    w = load_panel(pn * 512)
                    for sub in range(NSUB):
                        for s in range(4):
                            t0 = sub * 512 + s * 128
                            pi = P.nextp()
                            self.mmg(P.pb[pi][:], [(aTb[:, kc, t0:t0 + 128], wpan[w][:, kc, :]) for kc in range(16)],
                                     ["aT", ("wpan", w)], ("pb", pi))
                            b = tg % 4
                            tg += 1
                            T.op("act", ACTF(tst[b][:], P.pb[pi][:], AF.Silu), reads=[("pb", pi)], writes=[("tst", b)])
                            r0 = TT_ * TB + t0
                            T.dma(DMA(D["ZS"][r0:r0 + 128, pn * 512:(pn + 1) * 512], tst[b][:]), reads=[("tst", b)])
                w = load_panel(10240, 64)
                for sub in range(NSUB):
                    for s in range(4):
                        t0 = sub * 512 + s * 128
                        pi = P.nextp()
                        self.mmg(P.pb[pi][:, 0:64], [(aTb[:, kc, t0:t0 + 128], wpan[w][:, kc, 0:64]) for kc in range(16)],
                                 ["aT", ("wpan", w)], ("pb", pi))
                        b = s % 2
                        T.op("dve", TT(dt1[b][:], P.pb[pi][:, 0:64], dtb[:], ALU.add), reads=[("pb", pi), "dtb"], writes=[("dt1", b)])
                        T.op("act", ACTF(dt1[b][:], dt1[b][:], AF.Exp), reads=[("dt1", b)], writes=[("dt1", b)])
                        T.op("act", ACTF(dt2[b][:, 0, :], dt1[b][:], AF.Ln, bias=1.0), reads=[("dt1", b)], writes=[("dt2", b)])
                        T.op("dve", TT(dt2[b][:, 1, :], dt2[b][:, 0, :], ab[:], ALU.mult), reads=[("dt2", b), "ab"], writes=[("dt2", b)])
                        r0 = TT_ * TB + t0
                        T.dma(DMA(D["DT"][r0:r0 + 128, :, :], dt2[b][:]), reads=[("dt2", b)])
                for pn in range(12):
                    w = load_panel(4096 + pn * 512)
                    for sub in range(NSUB):
                        ts_ = slice(TT_ * TB + sub * 512, TT_ * TB + (sub + 1) * 512)
                        xb = xi_ % 3
                        xi_ += 1
                        for m in range(4):
                            j = pn * 4 + m
                            b = g % 4
                            g += 1
                            pi = P.nextp()
                            self.mmg(P.pb[pi][:], [(wpan[w][:, kc, m * 128:(m + 1) * 128], aTb[:, kc, sub * 512:(sub + 1) * 512]) for kc in range(16)],
                                     ["aT", ("wpan", w)], ("pb", pi))
                            ge = gext[b]
                            T.op("pool", CP(ge[:, 0:3], halo[:, j, :]), reads=[("halo", j)], writes=[("gext", b)])
                            T.op("act", ACP(ge[:, 3:515], P.pb[pi][:]), reads=[("pb", pi)], writes=[("gext", b)])
                            T.op("pool", CP(halo[:, j, :], ge[:, 512:515]), reads=[("gext", b)], writes=[("halo", j)])
                            T.op("dve", TS(acc[b][:], ge[:, 0:512], taps[:, j, 0:1], cb[:, j:j + 1], ALU.mult, ALU.add),
                                 reads=[("gext", b), "taps", "cb"], writes=[("acc", b)])
                            for k in range(1, 4):
                                T.op("dve", STT(acc[b][:], ge[:, k:k + 512], taps[:, j, k:k + 1], acc[b][:], ALU.mult, ALU.add),
                                     reads=[("gext", b), ("acc", b), "taps"], writes=[("acc", b)])
                            for fn in pend:
                                fn()
                            pend = []

                            def back(j=j, b=b, m=m, xb=xb):
                                nonlocal tp
                                if j < 40:
                                    T.op("act", ACTF(sgb[b][:], acc[b][:], AF.Silu), reads=[("acc", b)], writes=[("sgb", b)])
                                    pti = tp % 2
                                    tp += 1
                                    for s in range(4):
                                        T.op("pe", TR(pt[pti][:, s * 128:(s + 1) * 128], sgb[b][:, s * 128:(s + 1) * 128], ident[:]),
                                             reads=[("sgb", b), "ident"], writes=[("pt", pti)], event=(s == 3))
                                    src = pt[pti][:, 0:512].rearrange("p (s f) -> p s f", f=128)
                                    T.op("dve" if m % 2 == 0 else "act", (CP if m % 2 == 0 else ACP)(xst[xb][:, :, m * 128:(m + 1) * 128], src),
                                         reads=[("pt", pti)], writes=[("xst", xb, m)])
                                    if j >= 32:
                                        T.op("pool", CP(bcst[xb][:, m, :], sgb[b][:]), reads=[("sgb", b)], writes=[("bcst", xb, m)])
                                else:
                                    T.op("act", ACTF(bcst[xb][:, m, :], acc[b][:], AF.Silu), reads=[("acc", b)], writes=[("bcst", xb, m)])
                            pend.append(back)
                            if m == 3:
                                def store(pn=pn, ts_=ts_, xb=xb):
                                    if pn < 8:
                                        T.dma(DMA(D["XS"][ts_, pn * 512:(pn + 1) * 512].rearrange("(s p) f -> p s f", p=128), xst[xb][:]),
                                              reads=[("xst", xb, m_) for m_ in range(4)])
                                    elif pn < 10:
                                        T.dma(DMA(D["BTM"][ts_, (pn - 8) * 512:(pn - 7) * 512].rearrange("(s p) f -> p s f", p=128), xst[xb][:]),
                                              reads=[("xst", xb, m_) for m_ in range(4)])
                                        T.dma(DMA(D["BT"][(pn - 8) * 512:(pn - 7) * 512, ts_].rearrange("(m p) t -> p m t", p=128), bcst[xb][:]),
                                              reads=[("bcst", xb, m_) for m_ in range(4)])
                                    else:
                                        T.dma(DMA(D["CT"][(pn - 10) * 512:(pn - 9) * 512, ts_].rearrange("(m p) t -> p m t", p=128), bcst[xb][:]),
                                              reads=[("bcst", xb, m_) for m_ in range(4)])
                                pend.append(store)
            for fn in pend:
                fn()

    def phase_ssdcore(self, l):
        D, T = self.D, self.T
        i_ = l // 2
        with self.phase("score") as P:
            ident = P.sb("ident", [128, 128], BF16)
            ucm = P.sb("ucm", [128, 128], F32)
            usm = P.sb("usm", [128, 128], F32)
            onesf = P.sb("onesf", [128, 128], F32)
            ones1 = P.sb("ones1", [1, 128], F32)
            row = P.sb("row", [1, 64], F32)
            dsk = P.sb("dsk", [128, 64], F32)
            ngb = P.sb("ngb", [128, 4096], F32)
            nrw = [P.sb("nrw%d" % i, [1, 512], F32) for i in range(2)]
            xs = [P.sb("xs%d" % i, [128, 4096], BF16) for i in range(1)] * 2
            zs = [P.sb("zs%d" % i, [128, 4096], BF16) for i in range(1)] * 2
            btm = [P.sb("btm%d" % i, [128, 1024], BF16) for i in range(2)]
            bT = [P.sb("bT%d" % i, [128, 8, 128], BF16) for i in range(2)]
            cT = [P.sb("cT%d" % i, [128, 8, 128], BF16) for i in range(2)]
            dtt = [P.sb("dtt%d" % i, [128, 2, 64], F32) for i in range(2)]
            rbg = [P.sb("rbg%d" % i, [128, 8, 128], F32) for i in range(2)]
            cbm = P.sb("cbm", [128, 8, 128], F32)
            ee = [P.sb("ee%d" % i, [128, 4, 128], F32) for i in range(4)]
            mT = P.sb("mT", [128, 64, 128], BF16)
            xdt = P.sb("xdt", [128, 4096], BF16)
            xw = P.sb("xw", [128, 4096], BF16)
            eac = P.sb("eac", [128, 64], F32)
            tot = P.sb("tot", [128, 64], F32)
            edec = P.sb("edec", [128, 64], F32)
            ten = P.sb("ten", [128, 64], F32)
            acs = P.sb("acs", [128, 64], F32)
            st32 = P.sb("st32", [128, 4096], F32)
            stb = P.sb("stb", [128, 4096], BF16)
            t1 = [P.sb("t1%d" % i, [128, 512], F32) for i in range(2)]
            t2 = [P.sb("t2%d" % i, [128, 512], F32) for i in range(2)]
            yg = P.sb("yg", [128, 4096], F32)
            ynb = P.sb("ynb", [128, 4096], BF16)
            ss = P.sb("ss", [128, 1], F32)
            sd = P.sb("sd", [128, 1], F32)
            rstd = P.sb("rstd", [128, 1], F32)
            yst = [P.sb("yst%d" % i, [128, 32, 128], BF16) for i in range(1)] * 2
            P.banks(6)
            pt = [P.ps("pt%d" % i, [128, 1024], BF16) for i in range(2)]
            T.dma(DMA(ident[:], D["ident"]), writes=["ident"])
            T.dma(DMA(ucm[:], D["UCM"]), writes=["ucm"])
            T.dma(DMA(usm[:], D["USM"]), writes=["usm"])
            T.dma(DMA(row[:], D["ssd_d"][i_]), writes=["row"])
            T.op("pool", MSET(ones1[:], 1.0), writes=["ones1"])
            T.op("pool", MSET(onesf[:], 1.0), writes=["onesf"])
            T.op("pool", MSET(st32[:], 0.0), writes=[("st32", g) for g in range(8)])
            T.op("pool", MSET(stb[:], 0.0), writes=[("stb", g) for g in range(8)])
            pi = P.nextp()
            self.mmg(P.pb[pi][:, 0:64], [(ones1[0:1, :], row[0:1, :])], ["ones1", "row"], ("pb", pi))
            T.op("act", ACP(dsk[:], P.pb[pi][:, 0:64]), reads=[("pb", pi)], writes=["dsk"])
            for n in range(8):
                pi = P.nextp()
                T.dma(DMA(nrw[n % 2][:], D["ssd_ng"][i_][:, n * 512:(n + 1) * 512]), writes=[("nrw", n % 2)])
                self.mmg(P.pb[pi][:], [(ones1[0:1, :], nrw[n % 2][0:1, :])], ["ones1", ("nrw", n % 2)], ("pb", pi))
                T.op("act", ACP(ngb[:, n * 512:(n + 1) * 512], P.pb[pi][:]), reads=[("pb", pi)], writes=["ngb"])
            tp = 0
            def prologue(ck):
                    b = ck % 2
                    r0 = ck * 128
                    rs_ = slice(r0, r0 + 128)
                    T.dma(DMA(xs[b][:], D["XS"][rs_, :]), writes=[("xs", 0), ("xs", 1)])
                    T.dma(DMA(zs[b][:], D["ZS"][rs_, :]), writes=[("zs", 0), ("zs", 1)])
                    T.dma(DMA(btm[b][:], D["BTM"][rs_, :]), writes=[("btm", b)])
                    T.dma(DMA(bT[b][:], D["BT"][:, rs_].rearrange("(g p) t -> p g t", p=128)), writes=[("bT", b)])
                    T.dma(DMA(cT[b][:], D["CT"][:, rs_].rearrange("(g p) t -> p g t", p=128)), writes=[("cT", b)])
                    T.dma(DMA(dtt[b][:], D["DT"][rs_, :, :]), writes=[("dtt", b)])
                    dt_ = dtt[b][:, 0, :]
                    dta = dtt[b][:, 1, :]
                    pa = P.nextp()
                    dboth = dtt[b][:].rearrange("p a h -> p (a h)")
                    self.mmg(P.pb[pa][:, 0:128], [(ucm[:], dboth)], ["ucm", ("dtt", b)], ("pb", pa))
                    T.op("act", ACTF(eac[:], P.pb[pa][:, 64:128], AF.Exp), reads=[("pb", pa)], writes=["eac"])
                    T.op("act", ACP(acs[:], P.pb[pa][:, 64:128]), reads=[("pb", pa)], writes=["acs"])
                    ptot = P.nextp()
                    self.mmg(P.pb[ptot][:, 0:128], [(onesf[:], dboth)], ["onesf", ("dtt", b)], ("pb", ptot))
                    T.op("act", ACTF(edec[:], P.pb[ptot][:, 64:128], AF.Exp), reads=[("pb", ptot)], writes=["edec"])
                    T.op("act", ACP(tot[:], P.pb[ptot][:, 64:128]), reads=[("pb", ptot)], writes=["tot"])
                    T.op("dve", TT(ten[:], tot[:], acs[:], ALU.subtract), reads=["tot", "acs"], writes=["ten"])
                    T.op("act", ACTF(ten[:], ten[:], AF.Exp), reads=["ten"], writes=["ten"])
                    T.op("pool", TT(xdt[:].rearrange("p (h e) -> p h e", e=64), xs[b][:].rearrange("p (h e) -> p h e", e=64),
                                    dt_.unsqueeze(2).broadcast_to([128, 64, 64]), ALU.mult), reads=[("xs", b), ("dtt", b)], writes=["xdt"])
                    T.op("pool", TT(xw[:].rearrange("p (h e) -> p h e", e=64), xdt[:].rearrange("p (h e) -> p h e", e=64),
                                    ten[:].unsqueeze(2).broadcast_to([128, 64, 64]), ALU.mult), reads=["xdt", "ten"], writes=["xw"])
                    for g in range(8):
                        pc = P.nextp()
                        self.mmg(P.pb[pc][:, 0:128], [(bT[b][:, g, :], cT[b][:, g, :])], [("bT", b), ("cT", b)], ("pb", pc))
                        T.op("dve", TT(cbm[:, g, :], P.pb[pc][:, 0:128], ucm[:], ALU.mult), reads=[("pb", pc), "ucm"], writes=[("cbm", g)])
                    return dict(b=b, dta=dta, rs_=rs_)

            def groups(ck, ctx):
                    b = ctx['b']; dta = ctx['dta']
                    live = {}

                    def stA(g):
                        hs = slice(g * 8, (g + 1) * 8)
                        rb = rbg[g % 2]
                        T.op("dve", TT(rb[:], dta[:, hs].unsqueeze(2).broadcast_to([128, 8, 128]), ucm[:].unsqueeze(1).broadcast_to([128, 8, 128]), ALU.mult),
                             reads=[("dtt", b), "ucm"], writes=[("rb", g % 2)])
                        for half in range(2):
                            eb = (g * 2 + half) % 4
                            pseg = half
                            self.mmg(P.pb[pseg][:], [(usm[:], rb[:, half * 4:half * 4 + 4, :].rearrange("p h l -> p (h l)"))], ["usm", ("rb", g % 2)], ("pb", pseg))
                            T.op("act", ACTF(ee[eb][:].rearrange("p h l -> p (h l)"), P.pb[pseg][:], AF.Exp), reads=[("pb", pseg)], writes=[("ee", eb)])

                    def stB(g):
                        for half in range(2):
                            eb = (g * 2 + half) % 4
                            h0 = g * 8 + half * 4
                            T.op("pool" if half == 0 else "dve",
                                 TT(mT[:, h0:h0 + 4, :], ee[eb][:], cbm[:, g, :].unsqueeze(1).broadcast_to([128, 4, 128]), ALU.mult),
                                 reads=[("ee", eb), ("cbm", g)], writes=[("mT", g)])
                        pyd = 3 + g % 2
                        for hh in range(8):
                            h = g * 8 + hh
                            T.op("pe", MM(P.pb[pyd][:, hh * 64:(hh + 1) * 64], mT[:, h, :], xdt[:, h * 64:(h + 1) * 64], True, True),
                                 reads=[("mT", g), "xdt"] if hh == 0 else (), writes=[("pb", pyd)], event=(hh == 7))
                        pyo = 5
                        self.mmg(P.pb[pyo][:], [(cT[b][:, g, :], stb[:, g * 512:(g + 1) * 512])], [("cT", b), ("stb", g)], ("pb", pyo))
                        live[g] = (pyd, pyo)

                    def stC(g):
                        hs = slice(g * 8, (g + 1) * 8)
                        pyd, pyo = live.pop(g)
                        gb = g % 2
                        gsl = slice(g * 512, (g + 1) * 512)
                        T.op("dve", TT(t1[gb][:].rearrange("p (h e) -> p h e", e=64), P.pb[pyo][:].rearrange("p (h e) -> p h e", e=64),
                                        eac[:, hs].unsqueeze(2).broadcast_to([128, 8, 64]), ALU.mult), reads=[("pb", pyo), "eac"], writes=[("t1", gb)])
                        T.op("dve", TT(t1[gb][:], t1[gb][:], P.pb[pyd][:], ALU.add), reads=[("t1", gb), ("pb", pyd)], writes=[("t1", gb)])
                        T.op("pool", TT(t2[gb][:].rearrange("p (h e) -> p h e", e=64), xs[b][:, gsl].rearrange("p (h e) -> p h e", e=64),
                                         dsk[:, hs].unsqueeze(2).broadcast_to([128, 8, 64]), ALU.mult), reads=[("xs", b), "dsk"], writes=[("t2", gb)])
                        T.op("pool", TT(t2[gb][:], t2[gb][:], t1[gb][:], ALU.add), reads=[("t2", gb), ("t1", gb)], writes=[("t2", gb)])
                        T.op("pool", TT(yg[:, gsl], t2[gb][:], zs[b][:, gsl], ALU.mult), reads=[("t2", gb), ("zs", b)], writes=[("yg", g)])
                        pst = 2
                        self.mmg(P.pb[pst][:], [(btm[b][:, g * 128:(g + 1) * 128], xw[:, gsl])], [("btm", b), "xw"], ("pb", pst))
                        T.op("dve", TT(st32[:, gsl].rearrange("p (h e) -> p h e", e=64), st32[:, gsl].rearrange("p (h e) -> p h e", e=64),
                                        edec[:, hs].unsqueeze(2).broadcast_to([128, 8, 64]), ALU.mult), reads=[("st32", g), "edec"], writes=[("st32", g)])
                        T.op("dve", TT(st32[:, gsl], st32[:, gsl], P.pb[pst][:], ALU.add), reads=[("st32", g), ("pb", pst)], writes=[("st32", g)])
                        T.op("act", ACP(stb[:, gsl], st32[:, gsl]), reads=[("st32", g)], writes=[("stb", g)])

                    for step in range(8 + 2):
                        if step < 8:
                            stA(step)
                        if 0 <= step - 2 < 8:
                            stC(step - 2)
                        if 0 <= step - 1 < 8:
                            stB(step - 1)

            def epilogue(ck, ctx):
                    nonlocal tp
                    rs_ = ctx['rs_']
                    T.op("act", ACTF(ynb[:], yg[:], AF.Square, accum_out=ss[:, 0:1]), reads=[("yg", g) for g in range(8)], writes=["ss", ("ynb", 0), ("ynb", 1)])
                    T.op("act", ACTF(sd[:], ss[:], AF.Sqrt, scale=1.0 / 4096, bias=EPS), reads=["ss"], writes=["sd"])
                    T.op("dve", RECIP(rstd[:], sd[:]), reads=["sd"], writes=["rstd"])
                    for hf in range(2):
                        fs = slice(hf * 2048, (hf + 1) * 2048)
                        T.op("dve", STT(ynb[:, fs], yg[:, fs], rstd[:, 0:1], ngb[:, fs], ALU.mult, ALU.mult),
                             reads=[("yg", g) for g in range(8)] + ["rstd", "ngb"], writes=[("ynb", hf)])
                    ya = 0
                    for q8 in range(8):
                        pti = tp % 2
                        tp += 1
                        for c4 in range(4):
                            c = q8 * 4 + c4
                            T.op("pe", TR(pt[pti][:, c4 * 128:(c4 + 1) * 128], ynb[:, c * 128:(c + 1) * 128], ident[:]),
                                 reads=[("ynb", q8 // 4), "ident"], writes=[("pt", pti)], event=(c4 == 3))
                        src = pt[pti][:, 0:512].rearrange("p (c t) -> p c t", t=128)
                        if q8 % 2 == 0:
                            T.op("dve", CP(yst[ya][:, q8 * 4:(q8 + 1) * 4, :], src), reads=[("pt", pti)], writes=[("yst", ya, q8)])
                        else:
                            T.op("act", ACP(yst[ya][:, q8 * 4:(q8 + 1) * 4, :], src), reads=[("pt", pti)], writes=[("yst", ya, q8)])
                    for q8 in range(8):
                        T.dma(DMA(D["YT"][q8 * 512:(q8 + 1) * 512, rs_].rearrange("(c p) t -> p c t", p=128), yst[ya][:, q8 * 4:(q8 + 1) * 4, :]),
                              reads=[("yst", ya, q8)])

            nck = getattr(self, "dbg_nck", 32)
            ctx = prologue(0)
            for ck in range(nck):
                groups(ck, ctx)
                nxt = prologue(ck + 1) if ck + 1 < nck else None
                epilogue(ck, ctx)
                ctx = nxt


def _consts():
    c = {}
    pos = np.arange(S, dtype=np.float32)
    inv_r = (np.float32(10000.0) ** (-np.arange(128, dtype=np.float32) / np.float32(128))).astype(np.float32)
    ang = (pos[None, :] * inv_r[:, None]).astype(np.float32)
    c["COSR"] = np.cos(ang).astype(np.float32)
    c["SINR"] = np.sin(ang).astype(np.float32)
    inv_m = (np.float32(10000.0) ** (-np.arange(32, dtype=np.float32) / np.float32(32))).astype(np.float32)
    angm = (pos[None, :] * inv_m[:, None]).astype(np.float32)
    cm, sm = np.cos(angm).astype(np.float32), np.sin(angm).astype(np.float32)
    c["C2"] = np.concatenate([cm, cm, cm, cm], 0)
    c["S2"] = np.concatenate([-sm, sm, -sm, sm], 0)
    lg = np.log1p(-np.exp2(-5.0 - np.arange(4, dtype=np.float64)))
    idx = np.arange(128, dtype=np.float64)
    rel = idx[None, :] - idx[:, None]
    dec = np.where(rel >= 0, np.exp(lg[:, None, None] * np.maximum(rel, 0.0)), 0.0)
    c["DECT"] = np.ascontiguousarray(dec.transpose(1, 0, 2)).astype(np.float32)
    xi = np.exp(lg[:, None] * (idx[None, :] + 1.0))
    c["XI"] = np.ascontiguousarray(np.broadcast_to(xi[None], (128, 4, 128))).astype(np.float32)
    c["ZETA"] = np.ascontiguousarray(np.exp(lg[None, :] * (127.0 - idx[:, None]))).astype(np.float32)
    c["g128"] = np.exp(lg * 128.0)
    k = np.arange(128)
    c["MASKT"] = np.where((k[:, None] >= 64) & (k[None, :] < 64), 0.0, 1.0).astype(ml_dtypes.bfloat16)
    c["UCM"] = (k[:, None] <= k[None, :]).astype(np.float32)
    c["USM"] = (k[:, None] > k[None, :]).astype(np.float32)
    c["ident"] = np.eye(128, dtype=np.float32).astype(ml_dtypes.bfloat16)
    return c


def _prep_weights(inp):
    w = {}
    f = lambda a: np.ascontiguousarray(a, dtype=np.float32)
    w["ada_w"] = f(inp["ada_w"])
    w["ada_b"] = f(inp["ada_b"]).reshape(4, 1, 12288)
    w["norm_mix_g"] = f(inp["norm_mix_g"]).reshape(4, 1, 2048)
    w["norm_ffn_g"] = f(inp["norm_ffn_g"]).reshape(4, 1, 2048)
    win = inp["hyb_w_in"]
    kr = win[:, :, 7168:7232]
    kra = np.concatenate([kr, kr], -1)
    krb = np.concatenate([kr[:, :, 32:64], kr[:, :, 0:32], kr[:, :, 32:64], kr[:, :, 0:32]], -1)
    w["hyb_win"] = f(np.concatenate([win[:, :, 0:7168], kra, krb], -1))
    uq = inp["hyb_w_uq"].reshape(2, 512, 16, 192)
    qn = uq[:, :, :, 0:128].reshape(2, 512, 2048)
    ra = uq[:, :, :, 128:192].reshape(2, 512, 1024)
    rb = np.concatenate([uq[:, :, :, 160:192], uq[:, :, :, 128:160]], -1).reshape(2, 512, 1024)
    w["hyb_wuq"] = f(np.concatenate([qn, ra, rb], -1))
    ukv = inp["hyb_w_ukv"].reshape(2, 512, 16, 256)
    w["hyb_wukv"] = f(np.concatenate([ukv[:, :, :, 0:128].reshape(2, 512, 2048), ukv[:, :, :, 128:256].reshape(2, 512, 2048)], -1))
    w["hyb_qg"] = f(inp["hyb_q_norm_g"].reshape(2, 4, 128).transpose(0, 2, 1))
    w["hyb_kvg"] = f(inp["hyb_kv_norm_g"].reshape(2, 4, 128).transpose(0, 2, 1))
    w["hyb_gn"] = f(inp["hyb_ret_gn_g"]).reshape(2, 1, 2048)
    w["hyb_wout"] = f(inp["hyb_w_out"])
    w["ssd_win"] = f(inp["ssd_w_in"])
    w["ssd_cw"] = f(inp["ssd_conv_w"].transpose(0, 2, 1).reshape(2, 48, 128, 4).transpose(0, 2, 1, 3))
    w["ssd_cb"] = f(inp["ssd_conv_b"].reshape(2, 48, 128).transpose(0, 2, 1))
    w["ssd_dtb"] = f(inp["ssd_dt_bias"]).reshape(2, 1, 64)
    w["ssd_alog"] = f(inp["ssd_a_log"]).reshape(2, 1, 64)
    w["ssd_d"] = f(inp["ssd_d"]).reshape(2, 1, 64)
    w["ssd_ng"] = f(inp["ssd_norm_g"]).reshape(2, 1, 4096)
    w["ssd_wout"] = f(inp["ssd_w_out"])
    w["ffn_wup"] = f(inp["ffn_w_up"])
    w["ffn_cw"] = f(inp["ffn_conv_w"].transpose(0, 2, 1).reshape(4, 44, 128, 3).transpose(0, 2, 1, 3))
    w["ffn_cb"] = f(inp["ffn_conv_b"].reshape(4, 44, 128).transpose(0, 2, 1))
    w["ffn_wdown"] = f(inp["ffn_w_down"])
    w["final_norm_g"] = f(inp["final_norm_g"]).reshape(1, 2048)
    return w


def build(plan=None, dbg=False, only=None):
    B = Builder(dbg=dbg, only=only)
    nc, D = B.nc, B.D
    consts = _consts()
    B.consts = consts
    B.din("x", [S, DM])
    B.din("cT", [128, 16])
    B.din("ada_w", [4, 2048, 12288]); B.din("ada_b", [4, 1, 12288])
    B.din("norm_mix_g", [4, 1, 2048]); B.din("norm_ffn_g", [4, 1, 2048])
    B.din("hyb_win", [2, 2048, 7424]); B.din("hyb_wuq", [2, 512, 4096]); B.din("hyb_wukv", [2, 512, 4096])
    B.din("hyb_qg", [2, 128, 4]); B.din("hyb_kvg", [2, 128, 4]); B.din("hyb_gn", [2, 1, 2048]); B.din("hyb_wout", [2, 4096, 2048])
    B.din("ssd_win", [2, 2048, 10304]); B.din("ssd_cw", [2, 128, 48, 4]); B.din("ssd_cb", [2, 128, 48])
    B.din("ssd_dtb", [2, 1, 64]); B.din("ssd_alog", [2, 1, 64]); B.din("ssd_d", [2, 1, 64]); B.din("ssd_ng", [2, 1, 4096])
    B.din("ssd_wout", [2, 4096, 2048])
    B.din("ffn_wup", [4, 2048, 2 * FH]); B.din("ffn_cw", [4, 128, 44, 3]); B.din("ffn_cb", [4, 128, 44]); B.din("ffn_wdown", [4, FH, 2048])
    B.din("final_norm_g", [1, 2048])
    for k in ("COSR", "SINR", "C2", "S2"):
        B.din(k, [128, S])
    B.din("DECT", [128, 4, 128]); B.din("XI", [128, 4, 128]); B.din("ZETA", [128, 4])
    B.din("MASKT", [128, 128], BF16); B.din("UCM", [128, 128]); B.din("USM", [128, 128]); B.din("ident", [128, 128], BF16)
    B.dscr("out", [S, DM], F32, out=True)
    B.dscr("xres", [S, DM], F32)
    B.dscr("modsb", [4, 6, 128, 2048], F32)
    B.dscr("b_hyb_win", [2048, 7424], BF16); B.dscr("b_hyb_wuq", [512, 4096], BF16); B.dscr("b_hyb_wukv", [512, 4096], BF16)
    B.dscr("b_wout", [4096, 2048], BF16); B.dscr("b_ssd_win", [2048, 10304], BF16)
    B.dscr("b_ffn_wup", [2048, 2 * FH], BF16); B.dscr("b_ffn_wdown", [FH, 2048], BF16)
    B.dscr("HMT", [2048, S], BF16); B.dscr("YT", [4096, S], BF16); B.dscr("HT", [FH, S], BF16)
    B.dscr("RQT", [1024, S], BF16); B.dscr("RQXT", [1024, S], BF16); B.dscr("RKT", [1024, S], BF16)
    B.dscr("RV", [S, 2048], BF16); B.dscr("RG", [S, 2048], BF16)
    B.dscr("CQT", [512, S], BF16); B.dscr("CKVT", [512, S], BF16); B.dscr("KRT", [128, S], BF16)
    B.dscr("ZS", [S, 4096], BF16); B.dscr("XS", [S, 4096], BF16); B.dscr("BTM", [S, 1024], BF16)
    B.dscr("BT", [1024, S], BF16); B.dscr("CT", [1024, S], BF16); B.dscr("DT", [S, 2, 64], F32)
    with contextlib.ExitStack() as es:
        sems = [es.enter_context(nc.semaphore("s%d" % i)) for i in range(len(ENGS) + N_DMA_SLOTS)]
        B.T = Tracker(nc, sems)
        if plan is None:
            plan = [("ada", 0)]
            for l in range(4):
                plan += ([("cast", l)] if l > 0 else []) + [("mixer", l), ("ffn", l)]
            plan += [("final",)]
        xcur = D.get("x")
        for ph in plan:
            if ph[0] == "ada":
                B.phase_ada(ph[1] if len(ph) > 1 else None)
            elif ph[0] == "cast":
                B.phase_cast(ph[1])
            elif ph[0] == "mixer":
                l = ph[1]
                B.phase_norm(xcur, l, 1, 0, D["HMT"])
                if l % 2 == 0:
                    B.phase_hybproj(l)
                    B.phase_ret(l, consts)
                    B.phase_mla(l)
                else:
                    B.phase_ssdproj(l)
                    B.phase_ssdcore(l)
                B.phase_outproj(D["YT"], 32, D["b_wout"], l, 2, xcur, D["xres"])
                xcur = D["xres"]
            elif ph[0] == "sub":
                getattr(B, "phase_" + ph[1])(*[xcur if a == "X" else (D[a] if isinstance(a, str) else a) for a in ph[2:]])
            elif ph[0] == "ffn":
                l = ph[1]
                B.phase_norm(xcur, l, 4, 3, D["HMT"])
                B.phase_ffnup(l)
                B.phase_outproj(D["HT"], 44, D["b_ffn_wdown"], l, 5, xcur, D["xres"])
                xcur = D["xres"]
            elif ph[0] == "final":
                B.phase_final(xcur)
    return B


def make_in_maps(inputs, cores):
    w = _prep_weights(inputs)
    c = _consts()
    shared = dict(w)
    for k in ("COSR", "SINR", "C2", "S2", "DECT", "XI", "ZETA", "MASKT", "UCM", "USM", "ident"):
        shared[k] = c[k]
    maps = []
    for b in cores:
        m = dict(shared)
        m["x"] = np.ascontiguousarray(inputs["x"][b], dtype=np.float32)
        m["cT"] = np.ascontiguousarray(np.asarray(inputs["c"][b], dtype=np.float32).reshape(16, 128).T)
        maps.append(m)
    return maps


def kernel(**inputs):
    inputs = {k: np.asarray(v) for k, v in inputs.items()}
    B = build()
    maps = make_in_maps(inputs, list(range(N_CORES)))
    res = run_bass_kernel_spmd(B.nc, maps, core_ids=list(range(N_CORES)))
    out = np.stack([np.asarray(r["out"], dtype=np.float32) for r in res.results], 0)
    return out
```

```python
import contextlib
import math
import numpy as np
import ml_dtypes
import concourse.bass as bass
import concourse.mybir as mybir
from concourse.bass_utils import run_bass_kernel_spmd

F32 = mybir.dt.float32
BF16 = mybir.dt.bfloat16
ALU = mybir.AluOpType
AF = mybir.ActivationFunctionType

S = 4096
DM = 2048
NT = 8
EPS = 1e-6
FH = 5632
N_CORES = 8

ENGS = ("pe", "act", "dve", "pool", "sp")
N_DMA_SLOTS = 14


class Tracker:
    def __init__(self, nc, sems):
        self.nc = nc
        self.sem = {}
        self.count = {}
        it = iter(sems)
        for e in ENGS:
            self.sem[e] = next(it)
            self.count[e] = 0
        self.slots = []
        for i in range(N_DMA_SLOTS):
            n = "dma%d" % i
            self.sem[n] = next(it)
            self.count[n] = 0
            self.slots.append(n)
        self.slot_rr = 0
        self.seen = {e: {} for e in ENGS}
        self.ops = {e: [] for e in ENGS}
        self.lw = {}
        self.rd = {}
        self.n_ops = 0

    def _deps(self, reads, writes):
        deps = {}
        lw = self.lw
        for r in reads:
            ev = lw.get(r)
            if ev is not None and deps.get(ev[0], 0) < ev[1]:
                deps[ev[0]] = ev[1]
        for w in writes:
            ev = lw.get(w)
            if ev is not None and deps.get(ev[0], 0) < ev[1]:
                deps[ev[0]] = ev[1]
            d = self.rd.get(w)
            if d:
                for p, c in d.items():
                    if deps.get(p, 0) < c:
                        deps[p] = c
        return deps

    def _emit_waits(self, eng, deps):
        seen = self.seen[eng]
        for p, c in deps.items():
            if p == eng and eng in ("pe", "sp"):
                continue
            if seen.get(p, 0) >= c:
                continue
            seen[p] = c
            self.ops[eng].append((0, (p, c)))

    def _commit(self, ev, reads, writes):
        p, c = ev
        for r in reads:
            d = self.rd.setdefault(r, {})
            if d.get(p, 0) < c:
                d[p] = c
        for w in writes:
            self.lw[w] = ev
            self.rd[w] = {}

    max_ops = 10 ** 9

    def op(self, eng, fn, reads=(), writes=(), event=True):
        if self.n_ops >= self.max_ops:
            return
        self._emit_waits(eng, self._deps(reads, writes))
        if event:
            self.count[eng] += 1
            ev = (eng, self.count[eng])
            self.ops[eng].append((1, fn))
        else:
            ev = (eng, self.count[eng] + 1)
            self.ops[eng].append((2, fn))
        self._commit(ev, reads, writes)
        self.n_ops += 1

    def dma(self, fn, reads=(), writes=(), q="sp"):
        if self.n_ops >= self.max_ops:
            return
        deps = self._deps(reads, writes)
        slot = self.slots[self.slot_rr]
        self.slot_rr = (self.slot_rr + 1) % len(self.slots)
        if deps.get(slot, 0) < self.count[slot]:
            deps[slot] = self.count[slot]
        self._emit_waits(q, deps)
        self.count[slot] += 16
        ev = (slot, self.count[slot])
        self.ops[q].append((3, (fn, slot)))
        self._commit(ev, reads, writes)
        self.n_ops += 1

    def barrier(self):
        deps = {p: c for p, c in self.count.items() if c > 0}
        for e in ENGS:
            self._emit_waits(e, dict(deps))
        self.lw = {}
        self.rd = {}

    def _replay(self, eng, e):
        sem = self.sem
        for kind, pl in self.ops[eng]:
            if kind == 0:
                e.wait_ge(sem[pl[0]], pl[1])
            elif kind == 1:
                pl(e).then_inc(sem[eng], 1)
            elif kind == 2:
                pl(e)
            else:
                pl[0](e).then_inc(sem[pl[1]], 16)
        self.ops[eng] = []

    def emit(self):
        with self.nc.Block() as block:
            @block.tensor
            def _(e):
                self._replay("pe", e)

            @block.scalar
            def _(e):
                self._replay("act", e)

            @block.vector
            def _(e):
                self._replay("dve", e)

            @block.gpsimd
            def _(e):
                self._replay("pool", e)

            @block.sync
            def _(e):
                self._replay("sp", e)


def MM(out, lhsT, rhs, st, sp):
    return lambda e: e.matmul(out, lhsT=lhsT, rhs=rhs, start=st, stop=sp)

def TR(out, in_, ident):
    return lambda e: e.transpose(out=out, in_=in_, identity=ident)

def ACTF(out, in_, func, **kw):
    return lambda e: e.activation(out=out, in_=in_, func=func, **kw)

def TT(out, a, b, op):
    return lambda e: e.tensor_tensor(out=out, in0=a, in1=b, op=op)

def TS(out, a, s1, s2, op0, op1):
    return lambda e: e.tensor_scalar(out=out, in0=a, scalar1=s1, scalar2=s2, op0=op0, op1=op1)

def STT(out, a, s, b, op0, op1):
    return lambda e: e.scalar_tensor_tensor(out=out, in0=a, scalar=s, in1=b, op0=op0, op1=op1)

def CP(out, in_):
    return lambda e: e.tensor_copy(out=out, in_=in_)

def ACP(out, in_):
    return lambda e: e.copy(out=out, in_=in_)

def RECIP(out, in_):
    return lambda e: e.reciprocal(out=out, in_=in_)

def MSET(ap, v):
    return lambda e: e.memset(ap, v)

def DMA(out, in_):
    return lambda e: e.dma_start(out=out, in_=in_)


class Phase:
    def __init__(self, B, name):
        self.B = B
        self.name = name
        self.es = contextlib.ExitStack()
        self.np_ = 0
        self.rr = 0

    def __enter__(self):
        self.es.__enter__()
        self.es.enter_context(self.B.nc.named_scope(self.name))
        return self

    def __exit__(self, *a):
        self.B.T.barrier()
        self.B.T.emit()
        return self.es.__exit__(*a)

    def sb(self, name, shape, dt):
        return self.es.enter_context(self.B.nc.sbuf_tensor(self.name + "_" + name, shape, dt))

    def ps(self, name, shape=(128, 512), dt=F32):
        return self.es.enter_context(self.B.nc.psum_tensor(self.name + "_" + name, list(shape), dt))

    def banks(self, n):
        self.pb = [self.ps("pb%d" % i) for i in range(n)]
        self.npb = n

    def nextp(self):
        i = self.rr
        self.rr = (self.rr + 1) % self.npb
        return i


class Builder:
    def __init__(self, dbg=False, only=None):
        self.nc = bass.Bass("TRN2", target_bir_lowering=False)
        self.D = {}
        self.dbg = dbg
        self.only = only
        self.uid = 0

    def din(self, name, shape, dt=F32):
        if self.only is not None and name not in self.only:
            return
        self.D[name] = self.nc.dram_tensor(name, list(shape), dt, kind="ExternalInput").ap()

    def dscr(self, name, shape, dt, out=False):
        kind = "ExternalOutput" if (out or (self.dbg and name in self.dbg)) else "Internal"
        self.D[name] = self.nc.dram_tensor(name, list(shape), dt, kind=kind).ap()

    def phase(self, name):
        self.uid += 1
        return Phase(self, "%s%d" % (name, self.uid))

    def mmg(self, out, pairs, reads, wkey):
        n = len(pairs)
        for k, (l, r) in enumerate(pairs):
            self.T.op("pe", MM(out, l, r, k == 0, k == n - 1), reads=reads if k == 0 else (), writes=[wkey], event=(k == n - 1))

    def cast(self, src, dst, rows, blk=256):
        for r0 in range(0, rows, blk):
            self.T.dma(DMA(dst[r0:r0 + blk, :], src[r0:r0 + blk, :]), q="pool")

    def cast_layer(self, l):
        D = self.D
        i = l // 2
        if True:
            if l % 2 == 0:
                self.cast(D["hyb_win"][i], D["b_hyb_win"], 2048)
                self.cast(D["hyb_wuq"][i], D["b_hyb_wuq"], 512)
                self.cast(D["hyb_wukv"][i], D["b_hyb_wukv"], 512)
                self.cast(D["hyb_wout"][i], D["b_wout"], 4096)
            else:
                self.cast(D["ssd_win"][i], D["b_ssd_win"], 2048)
                self.cast(D["ssd_wout"][i], D["b_wout"], 4096)
            self.cast(D["ffn_wup"][l], D["b_ffn_wup"], 2048)
            self.cast(D["ffn_wdown"][l], D["b_ffn_wdown"], FH)

    def phase_cast(self, l):
        with self.phase("cast"):
            self.cast_layer(l)

    def phase_ada(self, cast_l=None):
        D, T = self.D, self.T
        with self.phase("ada") as P:
            if cast_l is not None:
                self.cast_layer(cast_l)
            sc = P.sb("sc", [128, 16], F32)
            scs = P.sb("scs", [128, 16], F32)
            scb = P.sb("scb", [128, 16, 128], F32)
            ones1 = P.sb("ones1", [1, 128], F32)
            wp = [P.sb("wp%d" % i, [128, 16, 512], F32) for i in range(2)]
            br = [P.sb("br%d" % i, [1, 512], F32) for i in range(2)]
            gr = [P.sb("gr%d" % i, [1, 512], F32) for i in range(2)]
            gb = [P.sb("gb%d" % i, [128, 512], F32) for i in range(2)]
            rs = [P.sb("rs%d" % i, [128, 512], F32) for i in range(2)]
            pm = [P.ps("pm%d" % i) for i in range(2)]
            pg = [P.ps("pg%d" % i) for i in range(2)]
            T.dma(DMA(sc[:], D["cT"]), writes=["sc"])
            T.op("pool", MSET(ones1[:], 1.0), writes=["ones1"])
            T.op("act", ACTF(scs[:], sc[:], AF.Silu), reads=["sc"], writes=["scs"])
            T.op("dve", CP(scb[:], scs[:].unsqueeze(2).broadcast_to([128, 16, 128])), reads=["scs"], writes=["scb"])
            it = 0
            for l in range(4):
                for j in range(6):
                    for n in range(4):
                        i = it % 2
                        it += 1
                        c0 = j * 2048 + n * 512
                        T.dma(DMA(wp[i][:], D["ada_w"][l, :, c0:c0 + 512].rearrange("(kc p) n -> p kc n", p=128)), writes=[("wp", i)])
                        T.dma(DMA(br[i][:], D["ada_b"][l, :, c0:c0 + 512]), writes=[("br", i)])
                        pairs = [(scb[:, kc, :], wp[i][:, kc, :]) for kc in range(16)] + [(ones1[0:1, :], br[i][0:1, :])]
                        self.mmg(pm[i][:], pairs, ["scb", "ones1", ("wp", i), ("br", i)], ("pm", i))
                        if j in (1, 4):
                            gsrc = D["norm_mix_g"] if j == 1 else D["norm_ffn_g"]
                            T.dma(DMA(gr[i][:], gsrc[l, :, n * 512:(n + 1) * 512]), writes=[("gr", i)])
                            self.mmg(pg[i][:], [(ones1[0:1, :], gr[i][0:1, :])], ["ones1", ("gr", i)], ("pg", i))
                            T.op("act", ACP(gb[i][:], pg[i][:]), reads=[("pg", i)], writes=[("gb", i)])
                            T.op("dve", STT(rs[i][:], pm[i][:], 1.0, gb[i][:], ALU.add, ALU.mult), reads=[("pm", i), ("gb", i)], writes=[("rs", i)])
                        else:
                            T.op("act", ACP(rs[i][:], pm[i][:]), reads=[("pm", i)], writes=[("rs", i)])
                        T.dma(DMA(D["modsb"][l, j, :, n * 512:(n + 1) * 512], rs[i][:]), reads=[("rs", i)])

    def phase_norm(self, x_src, l, j_gs, j_sh, dst):
        D, T = self.D, self.T
        with self.phase("norm") as P:
            gs = P.sb("gs", [128, 2048], F32)
            sh = P.sb("sh", [128, 2048], F32)
            ident = P.sb("ident", [128, 128], BF16)
            xt = [P.sb("xt%d" % i, [128, 4, 2048], F32) for i in range(2)]
            tmp = [P.sb("tmp%d" % i, [128, 2048], F32) for i in range(2)]
            hmb = [P.sb("hmb%d" % i, [128, 4, 2048], BF16) for i in range(2)]
            hT = [P.sb("hT%d" % i, [128, 16, 512], BF16) for i in range(2)]
            junk = P.sb("junk", [128, 2048], BF16)
            ss = P.sb("ss", [128, 8], F32)
            sd = P.sb("sd", [128, 8], F32)
            rstd = P.sb("rstd", [128, 8], F32)
            pt = [P.ps("pt%d" % i, [128, 1024], BF16) for i in range(4)]
            T.dma(DMA(gs[:], D["modsb"][l, j_gs]), writes=["gs"])
            T.dma(DMA(sh[:], D["modsb"][l, j_sh]), writes=["sh"])
            T.dma(DMA(ident[:], D["ident"]), writes=["ident"])
            g = 0
            for tt in range(NT):
                i = tt % 2
                T.dma(DMA(xt[i][:], x_src[tt * 512:(tt + 1) * 512, :].rearrange("(s p) d -> p s d", p=128)), writes=[("xt", i)])
                for s in range(4):
                    c = i * 4 + s
                    T.op("act", ACTF(junk[:], xt[i][:, s, :], AF.Square, accum_out=ss[:, c:c + 1]), reads=[("xt", i)], writes=[("ss", c)])
                    T.op("act", ACTF(sd[:, c:c + 1], ss[:, c:c + 1], AF.Sqrt, scale=1.0 / DM, bias=EPS), reads=[("ss", c)], writes=[("sd", c)])
                    T.op("dve", RECIP(rstd[:, c:c + 1], sd[:, c:c + 1]), reads=[("sd", c)], writes=[("rstd", c)])
                    T.op("dve", STT(tmp[s % 2][:], xt[i][:, s, :], rstd[:, c:c + 1], gs[:], ALU.mult, ALU.mult),
                         reads=[("xt", i), ("rstd", c), "gs"], writes=[("tmp", s % 2)])
                    T.op("pool", TT(hmb[i][:, s, :], tmp[s % 2][:], sh[:], ALU.add), reads=[("tmp", s % 2), "sh"], writes=[("hmb", i, s)])
                for kc in range(16):
                    pi = g % 4
                    g += 1
                    for s in range(4):
                        T.op("pe", TR(pt[pi][:, s * 128:(s + 1) * 128], hmb[i][:, s, kc * 128:(kc + 1) * 128], ident[:]),
                             reads=[("hmb", i, s), "ident"], writes=[("pt", pi)], event=(s == 3))
                    if kc % 2 == 0:
                        T.op("dve", CP(hT[i][:, kc, :], pt[pi][:, 0:512]), reads=[("pt", pi)], writes=[("hT", i, kc)])
                    else:
                        T.op("act", ACP(hT[i][:, kc, :], pt[pi][:, 0:512]), reads=[("pt", pi)], writes=[("hT", i, kc)])
                T.dma(DMA(dst[:, tt * 512:(tt + 1) * 512].rearrange("(kc p) t -> p kc t", p=128), hT[i][:]),
                      reads=[("hT", i, kc) for kc in range(16)])

    def phase_final(self, x_src):
        D, T = self.D, self.T
        with self.phase("fin") as P:
            gsb = P.sb("gsb", [128, 2048], F32)
            grow = P.sb("grow", [1, 2048], F32)
            ones1 = P.sb("ones1", [1, 128], F32)
            xt = [P.sb("xt%d" % i, [128, 4, 2048], F32) for i in range(2)]
            ot = [P.sb("ot%d" % i, [128, 4, 2048], F32) for i in range(2)]
            junk = P.sb("junk", [128, 2048], BF16)
            ss = P.sb("ss", [128, 8], F32)
            sd = P.sb("sd", [128, 8], F32)
            rstd = P.sb("rstd", [128, 8], F32)
            P.banks(4)
            T.dma(DMA(grow[:], D["final_norm_g"]), writes=["grow"])
            T.op("pool", MSET(ones1[:], 1.0), writes=["ones1"])
            for n in range(4):
                self.mmg(P.pb[n][:], [(ones1[0:1, :], grow[0:1, n * 512:(n + 1) * 512])], ["ones1", "grow"], ("pb", n))
                T.op("act", ACP(gsb[:, n * 512:(n + 1) * 512], P.pb[n][:]), reads=[("pb", n)], writes=["gsb"])
            for tt in range(NT):
                i = tt % 2
                T.dma(DMA(xt[i][:], x_src[tt * 512:(tt + 1) * 512, :].rearrange("(s p) d -> p s d", p=128)), writes=[("xt", i)])
                for s in range(4):
                    c = i * 4 + s
                    T.op("act", ACTF(junk[:], xt[i][:, s, :], AF.Square, accum_out=ss[:, c:c + 1]), reads=[("xt", i)], writes=[("ss", c)])
                    T.op("act", ACTF(sd[:, c:c + 1], ss[:, c:c + 1], AF.Sqrt, scale=1.0 / DM, bias=EPS), reads=[("ss", c)], writes=[("sd", c)])
                    T.op("dve", RECIP(rstd[:, c:c + 1], sd[:, c:c + 1]), reads=[("sd", c)], writes=[("rstd", c)])
                    T.op("dve", STT(ot[i][:, s, :], xt[i][:, s, :], rstd[:, c:c + 1], gsb[:], ALU.mult, ALU.mult),
                         reads=[("xt", i), ("rstd", c), "gsb"], writes=[("ot", i, s)])
                T.dma(DMA(D["out"][tt * 512:(tt + 1) * 512, :].rearrange("(s p) d -> p s d", p=128), ot[i][:]),
                      reads=[("ot", i, s) for s in range(4)])

    def phase_outproj(self, AT, KC, W, l, j_gate, x_src, x_dst):
        if KC > 32:
            return self.phase_outproj_split(AT, KC, W, l, j_gate, x_src, x_dst)
        return self.phase_outproj_simple(AT, KC, W, l, j_gate, x_src, x_dst)

    def phase_outproj_split(self, AT, KC, W, l, j_gate, x_src, x_dst):
        D, T = self.D, self.T
        TB = 1024
        KH = KC // 2
        with self.phase("oproj") as P:
            aT = P.sb("aT", [128, KC, TB], BF16)
            wh = [P.sb("wh%d" % i, [128, KH, 512], BF16) for i in range(3)]
            gate = P.sb("gate", [128, 2048], F32)
            xp = [P.sb("xp%d" % i, [128, 512], F32) for i in range(3)]
            tm = [P.sb("tm%d" % i, [128, 512], F32) for i in range(3)]
            P.banks(8)
            T.dma(DMA(gate[:], D["modsb"][l, j_gate]), writes=["gate"])
            g = 0
            wi = 0
            for tt in range(S // TB):
                T.dma(DMA(aT[:], AT[:, tt * TB:(tt + 1) * TB].rearrange("(kc p) t -> p kc t", p=128)), writes=["aT"])
                for n in range(4):
                    cs = slice(n * 512, (n + 1) * 512)
                    for hf in range(2):
                        w = wi % 3
                        wi += 1
                        T.dma(DMA(wh[w][:], W[hf * KH * 128:(hf + 1) * KH * 128, cs].rearrange("(kc p) n -> p kc n", p=128)), writes=[("wh", w)])
                        for s_ in range(8):
                            for k in range(KH):
                                kc = hf * KH + k
                                T.op("pe", MM(P.pb[s_][:], aT[:, kc, s_ * 128:(s_ + 1) * 128], wh[w][:, k, :], hf == 0 and k == 0, hf == 1 and k == KH - 1),
                                     reads=["aT", ("wh", w)] if k == 0 else (), writes=[("pb", s_)], event=(k == KH - 1))
                            if hf == 1:
                                b = g % 3
                                g += 1
                                r0 = tt * TB + s_ * 128
                                T.dma(DMA(xp[b][:], x_src[r0:r0 + 128, cs]), writes=[("xp", b)])
                                T.op("dve", TT(tm[b][:], P.pb[s_][:], gate[:, cs], ALU.mult), reads=[("pb", s_), "gate"], writes=[("tm", b)])
                                T.op("pool", TT(xp[b][:], xp[b][:], tm[b][:], ALU.add), reads=[("xp", b), ("tm", b)], writes=[("xp", b)])
                                T.dma(DMA(x_dst[r0:r0 + 128, cs], xp[b][:]), reads=[("xp", b)])

    def phase_outproj_simple(self, AT, KC, W, l, j_gate, x_src, x_dst):
        D, T = self.D, self.T
        TB = 1024
        PW = 512 if KC <= 32 else 256
        with self.phase("oproj") as P:
            aT = P.sb("aT", [128, KC, TB], BF16)
            wpan = [P.sb("wpan%d" % i, [128, KC, PW], BF16) for i in range(2)]
            gate = P.sb("gate", [128, 2048], F32)
            xp = [P.sb("xp%d" % i, [128, PW], F32) for i in range(4)]
            tm = [P.sb("tm%d" % i, [128, PW], F32) for i in range(4)]
            P.banks(6)
            T.dma(DMA(gate[:], D["modsb"][l, j_gate]), writes=["gate"])
            g = 0
            wi = 0
            for tt in range(S // TB):
                T.dma(DMA(aT[:], AT[:, tt * TB:(tt + 1) * TB].rearrange("(kc p) t -> p kc t", p=128)), writes=["aT"])
                for n in range(2048 // PW):
                    w = wi % 2
                    wi += 1
                    cs = slice(n * PW, (n + 1) * PW)
                    T.dma(DMA(wpan[w][:], W[:, cs].rearrange("(kc p) n -> p kc n", p=128)), writes=[("wpan", w)])
                    for s in range(TB // 128):
                        pi = P.nextp()
                        b = g % 4
                        g += 1
                        r0 = tt * TB + s * 128
                        T.dma(DMA(xp[b][:], x_src[r0:r0 + 128, cs]), writes=[("xp", b)])
                        self.mmg(P.pb[pi][:, 0:PW], [(aT[:, kc, s * 128:(s + 1) * 128], wpan[w][:, kc, :]) for kc in range(KC)],
                                 ["aT", ("wpan", w)], ("pb", pi))
                        T.op("dve", TT(tm[b][:], P.pb[pi][:, 0:PW], gate[:, cs], ALU.mult), reads=[("pb", pi), "gate"], writes=[("tm", b)])
                        T.op("pool", TT(xp[b][:], xp[b][:], tm[b][:], ALU.add), reads=[("xp", b), ("tm", b)], writes=[("xp", b)])
                        T.dma(DMA(x_dst[r0:r0 + 128, cs], xp[b][:]), reads=[("xp", b)])

    def phase_ffnup(self, l):
        D, T = self.D, self.T
        W = D["b_ffn_wup"]
        with self.phase("ffnup") as P:
            aTb = P.sb("aTb", [128, 16, 2048], BF16)
            wu = [P.sb("wu%d" % i, [128, 16, 512], BF16) for i in range(2)]
            wg = [P.sb("wg%d" % i, [128, 16, 512], BF16) for i in range(2)]
            taps = P.sb("taps", [128, 44, 3], F32)
            cb = P.sb("cb", [128, 44], F32)
            halo = P.sb("halo", [128, 44, 2], F32)
            gext = [P.sb("gext%d" % i, [128, 514], F32) for i in range(4)]
            acc = [P.sb("acc%d" % i, [128, 512], F32) for i in range(4)]
            sg = [P.sb("sg%d" % i, [128, 512], F32) for i in range(4)]
            hst = [P.sb("hst%d" % i, [128, 4, 512], BF16) for i in range(3)]
            P.banks(8)
            T.dma(DMA(taps[:], D["ffn_cw"][l]), writes=["taps"])
            T.dma(DMA(cb[:], D["ffn_cb"][l]), writes=["cb"])
            T.op("pool", MSET(halo[:], 0.0), writes=[("halo", j) for j in range(44)])
            g = 0
            wi = 0
            hi = 0
            pend = []
            for TT_ in range(2):
                T.dma(DMA(aTb[:], D["HMT"][:, TT_ * 2048:(TT_ + 1) * 2048].rearrange("(kc p) t -> p kc t", p=128)), writes=["aT"])
                for pn in range(11):
                    w = wi % 2
                    wi += 1
                    T.dma(DMA(wu[w][:], W[:, pn * 512:(pn + 1) * 512].rearrange("(kc p) n -> p kc n", p=128)), writes=[("wu", w)])
                    T.dma(DMA(wg[w][:], W[:, FH + pn * 512:FH + (pn + 1) * 512].rearrange("(kc p) n -> p kc n", p=128)), writes=[("wg", w)])
                    for sub in range(4):
                        tt = TT_ * 4 + sub
                        ss_ = slice(sub * 512, (sub + 1) * 512)
                        hb = hi % 3
                        hi += 1
                        for m in range(4):
                            j = pn * 4 + m
                            b = g % 4
                            g += 1
                            pu = P.nextp()
                            pg_ = P.nextp()
                            self.mmg(P.pb[pg_][:], [(wg[w][:, kc, m * 128:(m + 1) * 128], aTb[:, kc, ss_]) for kc in range(16)],
                                     ["aT", ("wg", w)], ("pb", pg_))
                            self.mmg(P.pb[pu][:], [(wu[w][:, kc, m * 128:(m + 1) * 128], aTb[:, kc, ss_]) for kc in range(16)],
                                     ["aT", ("wu", w)], ("pb", pu))
                            ge = gext[b]
                            T.op("pool", CP(ge[:, 0:2], halo[:, j, :]), reads=[("halo", j)], writes=[("gext", b)])
                            T.op("act", ACP(ge[:, 2:514], P.pb[pg_][:]), reads=[("pb", pg_)], writes=[("gext", b)])
                            T.op("pool", CP(halo[:, j, :], ge[:, 512:514]), reads=[("gext", b)], writes=[("halo", j)])
                            T.op("dve", TS(acc[b][:], ge[:, 0:512], taps[:, j, 0:1], cb[:, j:j + 1], ALU.mult, ALU.add),
                                 reads=[("gext", b), "taps", "cb"], writes=[("acc", b)])
                            T.op("dve", STT(acc[b][:], ge[:, 1:513], taps[:, j, 1:2], acc[b][:], ALU.mult, ALU.add),
                                 reads=[("gext", b), ("acc", b), "taps"], writes=[("acc", b)])
                            T.op("dve", STT(acc[b][:], ge[:, 2:514], taps[:, j, 2:3], acc[b][:], ALU.mult, ALU.add),
                                 reads=[("gext", b), ("acc", b), "taps"], writes=[("acc", b)])
                            for fn in pend:
                                fn()
                            pend = []

                            def back(b=b, pu=pu, hb=hb, m=m):
                                T.op("act", ACTF(sg[b][:], acc[b][:], AF.Silu), reads=[("acc", b)], writes=[("sg", b)])
                                T.op("dve", TT(hst[hb][:, m, :], P.pb[pu][:], sg[b][:], ALU.mult), reads=[("pb", pu), ("sg", b)], writes=[("hst", hb, m)])
                            pend.append(back)
                            if m == 3:
                                def store(pn=pn, tt=tt, hb=hb):
                                    T.dma(DMA(D["HT"][pn * 512:(pn + 1) * 512, tt * 512:(tt + 1) * 512].rearrange("(m p) t -> p m t", p=128), hst[hb][:]),
                                          reads=[("hst", hb, m_) for m_ in range(4)])
                                pend.append(store)
            for fn in pend:
                fn()

    def phase_hybproj(self, l):
        D, T = self.D, self.T
        i_ = l // 2
        W = D["b_hyb_win"]
        TB = 1024
        NSUB = TB // 512
        with self.phase("hproj") as P:
            aTb = P.sb("aTb", [128, 16, TB], BF16)
            wpan = [P.sb("wpan%d" % i, [128, 16, 512], BF16) for i in range(2)]
            cosr = [P.sb("cosr%d" % i, [128, 512], F32) for i in range(NSUB)]
            sinr = [P.sb("sinr%d" % i, [128, 512], F32) for i in range(NSUB)]
            c2 = [P.sb("c2%d" % i, [128, 512], F32) for i in range(NSUB)]
            s2 = [P.sb("s2%d" % i, [128, 512], F32) for i in range(NSUB)]
            xi = P.sb("xi", [128, 4, 128], F32)
            gq = P.sb("gq", [128, 4], F32)
            gkv = P.sb("gkv", [128, 4], F32)
            onesf = P.sb("onesf", [128, 128], F32)
            mt8 = [P.sb("mt%d" % i, [128, 512], F32) for i in range(8)]
            mt = mt8[0:4]
            st4 = [P.sb("st4%d" % i, [128, 4, 512], BF16) for i in range(2)]
            sx4 = [P.sb("sx4%d" % i, [128, 4, 512], BF16) for i in range(2)]
            craw = P.sb("craw", [128, 4, 512], F32)
            sq = P.sb("sq", [128, 4, 512], F32)
            sdt = P.sb("sdt", [128, 512], F32)
            rst = P.sb("rst", [128, 512], F32)
            cst = [P.sb("cst%d" % i, [128, 4, 512], BF16) for i in range(2)]
            krst = [P.sb("krst%d" % i, [128, 512], BF16) for i in range(2)]
            tst = [P.sb("tst%d" % i, [128, 512], BF16) for i in range(4)]
            P.banks(8)
            T.dma(DMA(xi[:], D["XI"]), writes=["xi"])
            T.dma(DMA(gq[:], D["hyb_qg"][i_]), writes=["gq"])
            T.dma(DMA(gkv[:], D["hyb_kvg"][i_]), writes=["gkv"])
            T.op("pool", MSET(onesf[:], 1.0), writes=["onesf"])
            wi = 0
            tg = 0
            sti = 0
            csi = 0
            for TT_ in range(S // TB):
                T.dma(DMA(aTb[:], D["HMT"][:, TT_ * TB:(TT_ + 1) * TB].rearrange("(kc p) t -> p kc t", p=128)), writes=["aT"])
                for sub in range(NSUB):
                    ts_ = slice(TT_ * TB + sub * 512, TT_ * TB + (sub + 1) * 512)
                    T.dma(DMA(cosr[sub][:], D["COSR"][:, ts_]), writes=[("cosr", sub)])
                    T.dma(DMA(sinr[sub][:], D["SINR"][:, ts_]), writes=[("sinr", sub)])
                    T.dma(DMA(c2[sub][:], D["C2"][:, ts_]), writes=[("c2", sub)])
                    T.dma(DMA(s2[sub][:], D["S2"][:, ts_]), writes=[("s2", sub)])

                def load_panel(c0, ncols=512):
                    nonlocal wi
                    w = wi % 2
                    wi += 1
                    T.dma(DMA(wpan[w][:, :, 0:ncols], W[:, c0:c0 + ncols].rearrange("(kc p) n -> p kc n", p=128)), writes=[("wpan", w)])
                    return w

                def fm(w, m, sub):
                    pi = P.nextp()
                    self.mmg(P.pb[pi][:], [(wpan[w][:, kc, m * 128:(m + 1) * 128], aTb[:, kc, sub * 512:(sub + 1) * 512]) for kc in range(16)],
                             ["aT", ("wpan", w)], ("pb", pi))
                    return pi

                def tm_(w, s, sub):
                    pi = P.nextp()
                    t0 = sub * 512 + s * 128
                    self.mmg(P.pb[pi][:], [(aTb[:, kc, t0:t0 + 128], wpan[w][:, kc, :]) for kc in range(16)],
                             ["aT", ("wpan", w)], ("pb", pi))
                    return pi

                for which in range(2):
                    for pn in range(2):
                        w = load_panel(which * 1024 + pn * 512)
                        for sub in range(NSUB):
                            ts_ = slice(TT_ * TB + sub * 512, TT_ * TB + (sub + 1) * 512)
                            sb_ = sti % 2
                            sti += 1
                            st = st4[sb_]
                            sx = sx4[sb_]
                            for hh in range(2):
                                h = pn * 2 + hh
                                p1 = fm(w, hh * 2, sub)
                                p2 = fm(w, hh * 2 + 1, sub)
                                rd1 = [("pb", p1), ("cosr", sub), ("sinr", sub)]
                                rd2 = [("pb", p2), ("cosr", sub), ("sinr", sub)]
                                mo = 4 * hh
                                ma, mb_, mc, md = mt8[mo], mt8[mo + 1], mt8[mo + 2], mt8[mo + 3]
                                T.op("dve", TT(ma[:], P.pb[p1][:], cosr[sub][:], ALU.mult), reads=rd1, writes=[("mt", mo)])
                                T.op("dve", TT(mb_[:], P.pb[p2][:], sinr[sub][:], ALU.mult), reads=rd2, writes=[("mt", mo + 1)])
                                T.op("dve", TT(mc[:], P.pb[p1][:], sinr[sub][:], ALU.mult), reads=rd1, writes=[("mt", mo + 2)])
                                T.op("dve", TT(md[:], P.pb[p2][:], cosr[sub][:], ALU.mult), reads=rd2, writes=[("mt", mo + 3)])
                                T.op("pool", TT(st[:, 2 * hh, :], ma[:], mb_[:], ALU.subtract), reads=[("mt", mo), ("mt", mo + 1)], writes=[("st", sb_, 2 * hh)])
                                T.op("pool", TT(st[:, 2 * hh + 1, :], mc[:], md[:], ALU.add), reads=[("mt", mo + 2), ("mt", mo + 3)], writes=[("st", sb_, 2 * hh + 1)])
                                for cc in (2 * hh, 2 * hh + 1):
                                    if which == 1:
                                        T.op("act", ACTF(st[:, cc, :], st[:, cc, :], AF.Copy, scale=1.0 / 16.0), reads=[("st", sb_, cc)], writes=[("st", sb_, cc)])
                                    else:
                                        T.op("pool", TT(sx[:, cc, :].rearrange("p (c l) -> p c l", l=128), st[:, cc, :].rearrange("p (c l) -> p c l", l=128),
                                                        xi[:, h, :].unsqueeze(1).broadcast_to([128, 4, 128]), ALU.mult),
                                             reads=[("st", sb_, cc), "xi"], writes=[("sx", sb_, cc)])
                            dst = D["RQT"] if which == 0 else D["RKT"]
                            rows = slice(pn * 512, (pn + 1) * 512)
                            T.dma(DMA(dst[rows, ts_].rearrange("(c p) t -> p c t", p=128), st[:]), reads=[("st", sb_, cc) for cc in range(4)])
                            if which == 0:
                                T.dma(DMA(D["RQXT"][rows, ts_].rearrange("(c p) t -> p c t", p=128), sx[:]), reads=[("sx", sb_, cc) for cc in range(4)])
                for which in range(2):
                    dst = D["RV"] if which == 0 else D["RG"]
                    for pn in range(4):
                        w = load_panel(2048 + which * 2048 + pn * 512)
                        for sub in range(NSUB):
                            for s in range(4):
                                pi = tm_(w, s, sub)
                                b = tg % 4
                                tg += 1
                                if which == 0:
                                    T.op("act", ACP(tst[b][:], P.pb[pi][:]), reads=[("pb", pi)], writes=[("tst", b)])
                                else:
                                    T.op("act", ACTF(tst[b][:], P.pb[pi][:], AF.Silu), reads=[("pb", pi)], writes=[("tst", b)])
                                r0 = TT_ * TB + sub * 512 + s * 128
                                T.dma(DMA(dst[r0:r0 + 128, pn * 512:(pn + 1) * 512], tst[b][:]), reads=[("tst", b)])
                for which in range(2):
                    w = load_panel(6144 + which * 512)
                    gcol = gq if which == 0 else gkv
                    dst = D["CQT"] if which == 0 else D["CKVT"]
                    for sub in range(NSUB):
                        ts_ = slice(TT_ * TB + sub * 512, TT_ * TB + (sub + 1) * 512)
                        cb_ = csi % 2
                        csi += 1
                        for c in range(4):
                            pi = fm(w, c, sub)
                            T.op("act", ACP(craw[:, c, :], P.pb[pi][:]), reads=[("pb", pi)], writes=[("craw", c)])
                            T.op("act", ACTF(sq[:, c, :], P.pb[pi][:], AF.Square), reads=[("pb", pi)], writes=[("sq", c)])
                        pi = P.nextp()
                        self.mmg(P.pb[pi][:], [(onesf[:], sq[:, c, :]) for c in range(4)], ["onesf"] + [("sq", c) for c in range(4)], ("pb", pi))
                        T.op("act", ACTF(sdt[:], P.pb[pi][:], AF.Sqrt, scale=1.0 / 512, bias=EPS), reads=[("pb", pi)], writes=["sdt"])
                        T.op("dve", RECIP(rst[:], sdt[:]), reads=["sdt"], writes=["rst"])
                        for c in range(4):
                            T.op("dve", STT(cst[cb_][:, c, :], craw[:, c, :], gcol[:, c:c + 1], rst[:], ALU.mult, ALU.mult),
                                 reads=[("craw", c), "rst", "gq", "gkv"], writes=[("cst", cb_, c)])
                        T.dma(DMA(dst[:, ts_].rearrange("(c p) t -> p c t", p=128), cst[cb_][:]), reads=[("cst", cb_, c) for c in range(4)])
                w = load_panel(7168, 256)
                for sub in range(NSUB):
                    ts_ = slice(TT_ * TB + sub * 512, TT_ * TB + (sub + 1) * 512)
                    pa = fm(w, 0, sub)
                    pb_ = fm(w, 1, sub)
                    T.op("dve", TT(mt[0][:], P.pb[pa][:], c2[sub][:], ALU.mult), reads=[("pb", pa), ("c2", sub)], writes=[("mt", 0)])
                    T.op("dve", TT(mt[1][:], P.pb[pb_][:], s2[sub][:], ALU.mult), reads=[("pb", pb_), ("s2", sub)], writes=[("mt", 1)])
                    T.op("pool", TT(krst[sub][:], mt[0][:], mt[1][:], ALU.add), reads=[("mt", 0), ("mt", 1)], writes=[("krst", sub)])
                    T.dma(DMA(D["KRT"][:, ts_], krst[sub][:]), reads=[("krst", sub)])

    def phase_ret(self, l, consts):
        D, T = self.D, self.T
        i_ = l // 2
        g128 = consts["g128"]
        with self.phase("ret") as P:
            ident = P.sb("ident", [128, 128], BF16)
            decT = P.sb("decT", [128, 4, 128], F32)
            zeta = P.sb("zeta", [128, 4], F32)
            gnb = P.sb("gnb", [128, 2048], F32)
            grow = P.sb("grow", [1, 2048], F32)
            ones1 = P.sb("ones1", [1, 128], F32)
            qT = [P.sb("qT%d" % i, [128, 8, 512], BF16) for i in range(2)]
            qxT = [P.sb("qxT%d" % i, [128, 8, 512], BF16) for i in range(2)]
            kT = [P.sb("kT%d" % i, [128, 8, 512], BF16) for i in range(2)]
            v = [P.sb("v%d" % i, [128, 2048], BF16) for i in range(2)]
            rg = [P.sb("rg%d" % i, [128, 2048], BF16) for i in range(2)]
            ktm = [P.sb("ktm%d" % i, [128, 1024], BF16) for i in range(2)]
            R32 = P.sb("R32", [128, 8, 512], F32)
            Rb = P.sb("Rb", [128, 8, 512], BF16)
            A = [P.sb("A%d" % i, [128, 128], BF16) for i in range(2)]
            st6 = [P.sb("st6%d" % i, [128, 6], F32) for i in range(2)]
            mv = [P.sb("mv%d" % i, [128, 2], F32) for i in range(2)]
            sd = [P.sb("sd%d" % i, [128, 1], F32) for i in range(2)]
            rs = [P.sb("rs%d" % i, [128, 1], F32) for i in range(2)]
            nm = [P.sb("nm%d" % i, [128, 1], F32) for i in range(2)]
            yn = [P.sb("yn%d" % i, [128, 512], F32) for i in range(2)]
            yg = [P.sb("yg%d" % i, [128, 512], F32) for i in range(2)]
            yr = [P.sb("yr%d" % i, [128, 2048], BF16) for i in range(2)]
            yst = [P.sb("yst%d" % i, [128, 16, 512], BF16) for i in range(2)]
            P.banks(6)
            pt = [P.ps("pt%d" % i, [128, 1024], BF16) for i in range(2)]
            T.dma(DMA(ident[:], D["ident"]), writes=["ident"])
            T.dma(DMA(decT[:], D["DECT"]), writes=["decT"])
            T.dma(DMA(zeta[:], D["ZETA"]), writes=["zeta"])
            T.dma(DMA(grow[:], D["hyb_gn"][i_]), writes=["grow"])
            T.op("pool", MSET(ones1[:], 1.0), writes=["ones1"])
            T.op("pool", MSET(R32[:], 0.0), writes=[("R32", c) for c in range(8)])
            T.op("pool", MSET(Rb[:], 0.0), writes=[("Rb", c) for c in range(8)])
            for n in range(4):
                pi = P.nextp()
                self.mmg(P.pb[pi][:], [(ones1[0:1, :], grow[0:1, n * 512:(n + 1) * 512])], ["ones1", "grow"], ("pb", pi))
                T.op("act", ACP(gnb[:, n * 512:(n + 1) * 512], P.pb[pi][:]), reads=[("pb", pi)], writes=["gnb"])
            tp = 0
            for tt in range(NT):
                a = tt % 2
                ts_ = slice(tt * 512, (tt + 1) * 512)
                T.dma(DMA(qT[a][:], D["RQT"][:, ts_].rearrange("(c p) t -> p c t", p=128)), writes=[("qT", a)])
                T.dma(DMA(qxT[a][:], D["RQXT"][:, ts_].rearrange("(c p) t -> p c t", p=128)), writes=[("qxT", a)])
                T.dma(DMA(kT[a][:], D["RKT"][:, ts_].rearrange("(c p) t -> p c t", p=128)), writes=[("kT", a)])
                for cs in range(4):
                    ck = tt * 4 + cs
                    b = ck % 2
                    cs_ = slice(cs * 128, (cs + 1) * 128)
                    r0 = ck * 128
                    T.dma(DMA(v[b][:], D["RV"][r0:r0 + 128, :]), writes=[("v", b)])
                    T.dma(DMA(rg[b][:], D["RG"][r0:r0 + 128, :]), writes=[("rg", b)])
                    for half in range(2):
                        pi = tp % 2
                        tp += 1
                        for c4 in range(4):
                            c = half * 4 + c4
                            T.op("pe", TR(pt[pi][:, c4 * 128:(c4 + 1) * 128], kT[a][:, c, cs_], ident[:]), reads=[("kT", a), "ident"], writes=[("pt", pi)], event=(c4 == 3))
                        for hh in range(2):
                            h = half * 2 + hh
                            T.op("act", ACTF(ktm[b][:, h * 256:(h + 1) * 256], pt[pi][:, hh * 256:(hh + 1) * 256], AF.Copy, scale=zeta[:, h:h + 1]),
                                 reads=[("pt", pi), "zeta"], writes=[("ktm", b, h)])
                    for h in range(4):
                        hb = h % 2
                        pi = P.nextp()
                        self.mmg(P.pb[pi][:, 0:128], [(kT[a][:, 2 * h + dc, cs_], qT[a][:, 2 * h + dc, cs_]) for dc in range(2)],
                                 [("kT", a), ("qT", a)], ("pb", pi))
                        T.op("dve", TT(A[hb][:], P.pb[pi][:, 0:128], decT[:, h, :], ALU.mult), reads=[("pb", pi), "decT"], writes=[("A", hb)])
                        py = P.nextp()
                        pairs = [(A[hb][:], v[b][:, h * 512:(h + 1) * 512])] + [(qxT[a][:, 2 * h + dc, cs_], Rb[:, 2 * h + dc, :]) for dc in range(2)]
                        self.mmg(P.pb[py][:], pairs, [("A", hb), ("v", b), ("qxT", a), ("Rb", 2 * h), ("Rb", 2 * h + 1)], ("pb", py))
                        for dc in range(2):
                            c = 2 * h + dc
                            pr = P.nextp()
                            self.mmg(P.pb[pr][:], [(ktm[b][:, c * 128:(c + 1) * 128], v[b][:, h * 512:(h + 1) * 512])], [("ktm", b, h), ("v", b)], ("pb", pr))
                            T.op("dve", STT(R32[:, c, :], R32[:, c, :], float(g128[h]), P.pb[pr][:], ALU.mult, ALU.add),
                                 reads=[("R32", c), ("pb", pr)], writes=[("R32", c)])
                            T.op("act", ACP(Rb[:, c, :], R32[:, c, :]), reads=[("R32", c)], writes=[("Rb", c)])
                        T.op("dve", lambda e, o=st6[hb], i=P.pb[py]: e.bn_stats(out=o[:], in_=i[:]), reads=[("pb", py)], writes=[("st6", hb)])
                        T.op("dve", lambda e, o=mv[hb], i=st6[hb]: e.bn_aggr(out=o[:], in_=i[:]), reads=[("st6", hb)], writes=[("mv", hb)])
                        T.op("act", ACTF(sd[hb][:], mv[hb][:, 1:2], AF.Sqrt, scale=1.0, bias=EPS), reads=[("mv", hb)], writes=[("sd", hb)])
                        T.op("dve", RECIP(rs[hb][:], sd[hb][:]), reads=[("sd", hb)], writes=[("rs", hb)])
                        T.op("dve", STT(nm[hb][:], mv[hb][:, 0:1], -1.0, rs[hb][:], ALU.mult, ALU.mult), reads=[("mv", hb), ("rs", hb)], writes=[("nm", hb)])
                        T.op("act", ACTF(yn[hb][:], P.pb[py][:], AF.Identity, scale=rs[hb][:, 0:1], bias=nm[hb][:, 0:1]),
                             reads=[("pb", py), ("rs", hb), ("nm", hb)], writes=[("yn", hb)])
                        T.op("pool", TT(yg[hb][:], yn[hb][:], gnb[:, h * 512:(h + 1) * 512], ALU.mult), reads=[("yn", hb), "gnb"], writes=[("yg", hb)])
                        T.op("pool", TT(yr[b][:, h * 512:(h + 1) * 512], yg[hb][:], rg[b][:, h * 512:(h + 1) * 512], ALU.mult),
                             reads=[("yg", hb), ("rg", b)], writes=[("yr", b, h)])
                    for q4 in range(4):
                        pi = tp % 2
                        tp += 1
                        for c4 in range(4):
                            c = q4 * 4 + c4
                            T.op("pe", TR(pt[pi][:, c4 * 128:(c4 + 1) * 128], yr[b][:, c * 128:(c + 1) * 128], ident[:]),
                                 reads=[("yr", b, q4), "ident"], writes=[("pt", pi)], event=(c4 == 3))
                        src = pt[pi][:, 0:512].rearrange("p (c t) -> p c t", t=128)
                        if q4 % 2 == 0:
                            T.op("dve", CP(yst[a][:, q4 * 4:(q4 + 1) * 4, cs_], src), reads=[("pt", pi)], writes=[("yst", a, cs, q4)])
                        else:
                            T.op("act", ACP(yst[a][:, q4 * 4:(q4 + 1) * 4, cs_], src), reads=[("pt", pi)], writes=[("yst", a, cs, q4)])
                T.dma(DMA(D["YT"][0:2048, ts_].rearrange("(c p) t -> p c t", p=128), yst[a][:]),
                      reads=[("yst", a, cs, q4) for cs in range(4) for q4 in range(4)])

    def phase_mla(self, l):
        D, T = self.D, self.T
        Wq = D["b_hyb_wuq"]
        Wkv = D["b_hyb_wukv"]
        scale = (128 + 64) ** -0.5
        with self.phase("mla") as P:
            krt = P.sb("krt", [128, S], BF16)
            ones = P.sb("ones", [128, 128], BF16)
            maskT = P.sb("maskT", [128, 128], BF16)
            cq = [P.sb("cq%d" % i, [128, 4, 512], BF16) for i in range(2)]
            ckv = [P.sb("ckv%d" % i, [128, 4, 512], BF16) for i in range(2)]
            c2 = [P.sb("c2%d" % i, [128, 512], F32) for i in range(2)]
            s2 = [P.sb("s2%d" % i, [128, 512], F32) for i in range(2)]
            wq = [P.sb("wq%d" % i, [128, 4, 512], BF16) for i in range(2)]
            wkv = [P.sb("wkv%d" % i, [128, 4, 512], BF16) for i in range(2)]
            qn = [P.sb("qn%d" % i, [128, 2, S], BF16) for i in range(2)]
            qr = [P.sb("qr%d" % i, [128, S], BF16) for i in range(2)]
            kn = [P.sb("kn%d" % i, [128, 2, S], BF16) for i in range(2)]
            vv = [P.sb("vv%d" % i, [128, 32, 256], BF16) for i in range(2)]
            mt = [P.sb("mt%d" % i, [128, 512], F32) for i in range(2)]
            pp = [P.sb("pp%d" % i, [128, 512], BF16) for i in range(4)]
            rden = [P.sb("rden%d" % i, [128, 512], F32) for i in range(2)]
            ost = [P.sb("ost%d" % i, [128, 512], BF16) for i in range(2)]
            P.banks(8)
            T.dma(DMA(krt[:], D["KRT"]), writes=["krt"])
            T.dma(DMA(maskT[:], D["MASKT"]), writes=["maskT"])
            T.op("pool", MSET(ones[:], 1.0), writes=["ones"])
            prep_rr = 0
            sc_rr = 0
            pp_rr = 0
            for hp in range(8):
                x = hp % 2
                T.dma(DMA(wq[x][:, :, 0:256], Wq[:, hp * 256:(hp + 1) * 256].rearrange("(kc p) n -> p kc n", p=128)), writes=[("wq", x)])
                T.dma(DMA(wq[x][:, :, 256:384], Wq[:, 2048 + hp * 128:2048 + (hp + 1) * 128].rearrange("(kc p) n -> p kc n", p=128)), writes=[("wq", x)])
                T.dma(DMA(wq[x][:, :, 384:512], Wq[:, 3072 + hp * 128:3072 + (hp + 1) * 128].rearrange("(kc p) n -> p kc n", p=128)), writes=[("wq", x)])
                T.dma(DMA(wkv[x][:, :, 0:256], Wkv[:, hp * 256:(hp + 1) * 256].rearrange("(kc p) n -> p kc n", p=128)), writes=[("wkv", x)])
                T.dma(DMA(wkv[x][:, :, 256:512], Wkv[:, 2048 + hp * 256:2048 + (hp + 1) * 256].rearrange("(kc p) n -> p kc n", p=128)), writes=[("wkv", x)])
                for tt in range(NT):
                    a = tt % 2
                    ts_ = slice(tt * 512, (tt + 1) * 512)
                    T.dma(DMA(cq[a][:], D["CQT"][:, ts_].rearrange("(c p) t -> p c t", p=128)), writes=[("cq", a)])
                    T.dma(DMA(ckv[a][:], D["CKVT"][:, ts_].rearrange("(c p) t -> p c t", p=128)), writes=[("ckv", a)])
                    T.dma(DMA(c2[a][:], D["C2"][:, ts_]), writes=[("c2", a)])
                    T.dma(DMA(s2[a][:], D["S2"][:, ts_]), writes=[("s2", a)])

                    def prep(wt, wname, m, src, sname):
                        nonlocal prep_rr
                        pi = 7
                        prep_rr += 1
                        self.mmg(P.pb[pi][:], [(wt[:, kc, m * 128:(m + 1) * 128], src[:, kc, :]) for kc in range(4)],
                                 [(wname, x), (sname, a)], ("pb", pi))
                        return pi
                    for hh in range(2):
                        pi = prep(wq[x], "wq", hh, cq[a], "cq")
                        T.op("act", ACP(qn[x][:, hh, ts_], P.pb[pi][:]), reads=[("pb", pi)], writes=[("qn", x, tt)])
                        pi = prep(wkv[x], "wkv", hh, ckv[a], "ckv")
                        T.op("dve", CP(kn[x][:, hh, ts_], P.pb[pi][:]), reads=[("pb", pi)], writes=[("kn", x, tt)])
                    pa = prep(wq[x], "wq", 2, cq[a], "cq")
                    T.op("dve", TT(mt[0][:], P.pb[pa][:], c2[a][:], ALU.mult), reads=[("pb", pa), ("c2", a)], writes=[("mt", 0)])
                    pb_ = prep(wq[x], "wq", 3, cq[a], "cq")
                    T.op("dve", TT(mt[1][:], P.pb[pb_][:], s2[a][:], ALU.mult), reads=[("pb", pb_), ("s2", a)], writes=[("mt", 1)])
                    T.op("pool", TT(qr[x][:, ts_], mt[0][:], mt[1][:], ALU.add), reads=[("mt", 0), ("mt", 1)], writes=[("qr", x, tt)])
                    for s in range(4):
                        pi = 7
                        prep_rr += 1
                        self.mmg(P.pb[pi][:, 0:256], [(ckv[a][:, kc, s * 128:(s + 1) * 128], wkv[x][:, kc, 256:512]) for kc in range(4)],
                                 [("wkv", x), ("ckv", a)], ("pb", pi))
                        T.op("act", ACP(vv[x][:, tt * 4 + s, :], P.pb[pi][:, 0:256]), reads=[("pb", pi)], writes=[("vv", x, tt)])
                LOOK = 2
                for hh in range(2):
                    h = hp * 2 + hh
                    po = slice(hh * 64, (hh + 1) * 64)
                    tiles = [(qt, kt) for qt in range(NT) for kt in range(4 * qt + 4)]
                    info = {}

                    def front(i):
                        nonlocal sc_rr, pp_rr
                        qt, kt = tiles[i]
                        ks = slice(kt * 128, (kt + 1) * 128)
                        j = kt - 4 * qt
                        q0 = 0 if j <= 0 else j * 128
                        si = sc_rr % 3
                        sc_rr += 1
                        qa = slice(qt * 512 + q0, (qt + 1) * 512)
                        rdq = [("qn", x, qt), ("qr", x, qt), ("kn", x, kt // 4), "krt"]
                        self.mmg(P.pb[si][:, q0:512], [(kn[x][:, hh, ks], qn[x][:, hh, qa]), (krt[po, ks], qr[x][po, qa])], rdq, ("pb", si))
                        pb_i = pp_rr % 4
                        pp_rr += 1
                        T.op("act", ACTF(pp[pb_i][:, q0:512], P.pb[si][:, q0:512], AF.Exp, scale=scale), reads=[("pb", si)], writes=[("pp", pb_i)])
                        if j >= 0:
                            T.op("pool", TT(pp[pb_i][:, j * 128:(j + 1) * 128], pp[pb_i][:, j * 128:(j + 1) * 128], maskT[:], ALU.mult),
                                 reads=[("pp", pb_i), "maskT"], writes=[("pp", pb_i)])
                        info[i] = (pb_i, q0)

                    def back(i):
                        qt, kt = tiles[i]
                        pb_i, q0 = info.pop(i)
                        nk = 4 * qt + 4
                        ob = 3 + (h * NT + qt) % 2
                        db = 5 + (h * NT + qt) % 2
                        first = (kt == 0)
                        last = (kt == nk - 1)
                        T.op("pe", MM(P.pb[ob][:, q0:512], vv[x][:, kt, hh * 128:(hh + 1) * 128], pp[pb_i][:, q0:512], first, last),
                             reads=[("pp", pb_i), ("vv", x, kt // 4)], writes=[("pb", ob)], event=last)
                        T.op("pe", MM(P.pb[db][:, q0:512], ones[:], pp[pb_i][:, q0:512], first, last),
                             reads=[("pp", pb_i), "ones"], writes=[("pb", db)], event=True)
                        if last:
                            qs = slice(qt * 512, (qt + 1) * 512)
                            r = (h * NT + qt) % 2
                            T.op("dve", RECIP(rden[r][:], P.pb[db][:]), reads=[("pb", db)], writes=[("rden", r)])
                            T.op("dve", TT(ost[r][:], P.pb[ob][:], rden[r][:], ALU.mult), reads=[("pb", ob), ("rden", r)], writes=[("ost", r)])
                            T.dma(DMA(D["YT"][2048 + h * 128:2048 + (h + 1) * 128, qs], ost[r][:]), reads=[("ost", r)])

                    for i in range(len(tiles) + LOOK):
                        if i < len(tiles):
                            front(i)
                        if i >= LOOK:
                            back(i - LOOK)

    def phase_ssdproj(self, l):
        D, T = self.D, self.T
        i_ = l // 2
        W = D["b_ssd_win"]
        TB = 2048
        NSUB = TB // 512
        with self.phase("sproj") as P:
            ident = P.sb("ident", [128, 128], BF16)
            aTb = P.sb("aTb", [128, 16, TB], BF16)
            wpan = [P.sb("wpan%d" % i, [128, 16, 512], BF16) for i in range(2)]
            taps = P.sb("taps", [128, 48, 4], F32)
            cb = P.sb("cb", [128, 48], F32)
            halo = P.sb("halo", [128, 48, 3], F32)
            gext = [P.sb("gext%d" % i, [128, 515], F32) for i in range(4)]
            acc = [P.sb("acc%d" % i, [128, 512], F32) for i in range(4)]
            sgb = [P.sb("sgb%d" % i, [128, 512], BF16) for i in range(4)]
            tst = [P.sb("tst%d" % i, [128, 512], BF16) for i in range(4)]
            xst = [P.sb("xst%d" % i, [128, 4, 512], BF16) for i in range(3)]
            bcst = [P.sb("bcst%d" % i, [128, 4, 512], BF16) for i in range(3)]
            dtb = P.sb("dtb", [128, 64], F32)
            ab = P.sb("ab", [128, 64], F32)
            row = P.sb("row", [1, 128], F32)
            ones1 = P.sb("ones1", [1, 128], F32)
            dt1 = [P.sb("dt1%d" % i, [128, 64], F32) for i in range(2)]
            dt2 = [P.sb("dt2%d" % i, [128, 2, 64], F32) for i in range(2)]
            P.banks(6)
            pt = [P.ps("pt%d" % i, [128, 1024], BF16) for i in range(2)]
            T.dma(DMA(ident[:], D["ident"]), writes=["ident"])
            T.dma(DMA(taps[:], D["ssd_cw"][i_]), writes=["taps"])
            T.dma(DMA(cb[:], D["ssd_cb"][i_]), writes=["cb"])
            T.dma(DMA(row[:, 0:64], D["ssd_dtb"][i_]), writes=["row"])
            T.dma(DMA(row[:, 64:128], D["ssd_alog"][i_]), writes=["row"])
            T.op("pool", MSET(ones1[:], 1.0), writes=["ones1"])
            T.op("pool", MSET(halo[:], 0.0), writes=[("halo", j) for j in range(48)])
            pi = P.nextp()
            self.mmg(P.pb[pi][:, 0:128], [(ones1[0:1, :], row[0:1, :])], ["ones1", "row"], ("pb", pi))
            T.op("act", ACP(dtb[:], P.pb[pi][:, 0:64]), reads=[("pb", pi)], writes=["dtb"])
            T.op("act", ACTF(ab[:], P.pb[pi][:, 64:128], AF.Exp), reads=[("pb", pi)], writes=["ab"])
            T.op("dve", (lambda e: e.tensor_scalar_mul(out=ab[:], in0=ab[:], scalar1=-1.0)), reads=["ab"], writes=["ab"])
            wi = 0
            tg = 0
            g = 0
            tp = 0
            xi_ = 0
            pend = []
            for TT_ in range(S // TB):
                T.dma(DMA(aTb[:], D["HMT"][:, TT_ * TB:(TT_ + 1) * TB].rearrange("(kc p) t -> p kc t", p=128)), writes=["aT"])

                def load_panel(c0, ncols=512):
                    nonlocal wi
                    w = wi % 2
                    wi += 1
                    T.dma(DMA(wpan[w][:, :, 0:ncols], W[:, c0:c0 + ncols].rearrange("(kc p) n -> p kc n", p=128)), writes=[("wpan", w)])
                    return w
                for pn in range(8):
                    w = load_panel(pn * 512)
                    for sub in range(NSUB):
                        for s in range(4):
                            t0 = sub * 512 + s * 128
                            pi = P.nextp()
                            self.mmg(P.pb[pi][:], [(aTb[:, kc, t0:t0 + 128], wpan[w][:, kc, :]) for kc in range(16)],
                                     ["aT", ("wpan", w)], ("pb", pi))
                            b = tg % 4
                            tg += 1
                            T.op("act", ACTF(tst[b][:], P.pb[pi][:], AF.Silu), reads=[("pb", pi)], writes=[("tst", b)])
                            r0 = TT_ * TB + t0
                            T.dma(DMA(D["ZS"][r0:r0 + 128, pn * 512:(pn + 1) * 512], tst[b][:]), reads=[("tst", b)])
                w = load_panel(10240, 64)
                for sub in range(NSUB):
                    for s in range(4):
                        t0 = sub * 512 + s * 128
                        pi = P.nextp()
                        self.mmg(P.pb[pi][:, 0:64], [(aTb[:, kc, t0:t0 + 128], wpan[w][:, kc, 0:64]) for kc in range(16)],
                                 ["aT", ("wpan", w)], ("pb", pi))
                        b = s % 2
                        T.op("dve", TT(dt1[b][:], P.pb[pi][:, 0:64], dtb[:], ALU.add), reads=[("pb", pi), "dtb"], writes=[("dt1", b)])
                        T.op("act", ACTF(dt1[b][:], dt1[b][:], AF.Exp), reads=[("dt1", b)], writes=[("dt1", b)])
                        T.op("act", ACTF(dt2[b][:, 0, :], dt1[b][:], AF.Ln, bias=1.0), reads=[("dt1", b)], writes=[("dt2", b)])
                        T.op("dve", TT(dt2[b][:, 1, :], dt2[b][:, 0, :], ab[:], ALU.mult), reads=[("dt2", b), "ab"], writes=[("dt2", b)])
                        r0 = TT_ * TB + t0
                        T.dma(DMA(D["DT"][r0:r0 + 128, :, :], dt2[b][:]), reads=[("dt2", b)])
                for pn in range(12):
                    w = load_panel(4096 + pn * 512)
                    for sub in range(NSUB):
                        ts_ = slice(TT_ * TB + sub * 512, TT_ * TB + (sub + 1) * 512)
                        xb = xi_ % 3
                        xi_ += 1
                        for m in range(4):
                            j = pn * 4 + m
                            b = g % 4
                            g += 1
                            pi = P.nextp()
                            self.mmg(P.pb[pi][:], [(wpan[w][:, kc, m * 128:(m + 1) * 128], aTb[:, kc, sub * 512:(sub + 1) * 512]) for kc in range(16)],
                                     ["aT", ("wpan", w)], ("pb", pi))
                            ge = gext[b]
                            T.op("pool", CP(ge[:, 0:3], halo[:, j, :]), reads=[("halo", j)], writes=[("gext", b)])
                            T.op("act", ACP(ge[:, 3:515], P.pb[pi][:]), reads=[("pb", pi)], writes=[("gext", b)])
                            T.op("pool", CP(halo[:, j, :], ge[:, 512:515]), reads=[("gext", b)], writes=[("halo", j)])
                            T.op("dve", TS(acc[b][:], ge[:, 0:512], taps[:, j, 0:1], cb[:, j:j + 1], ALU.mult, ALU.add),
                                 reads=[("gext", b), "taps", "cb"], writes=[("acc", b)])
                            for k in range(1, 4):
                                T.op("dve", STT(acc[b][:], ge[:, k:k + 512], taps[:, j, k:k + 1], acc[b][:], ALU.mult, ALU.add),
                                     reads=[("gext", b), ("acc", b), "taps"], writes=[("acc", b)])
                            for fn in pend:
                                fn()
                            pend = []

                            def back(j=j, b=b, m=m, xb=xb):
                                nonlocal tp
                                if j < 40:
                                    T.op("act", ACTF(sgb[b][:], acc[b][:], AF.Silu), reads=[("acc", b)], writes=[("sgb", b)])
                                    pti = tp % 2
                                    tp += 1
                                    for s in range(4):
                                        T.op("pe", TR(pt[pti][:, s * 128:(s + 1) * 128], sgb[b][:, s * 128:(s + 1) * 128], ident[:]),
                                             reads=[("sgb", b), "ident"], writes=[("pt", pti)], event=(s == 3))
                                    src = pt[pti][:, 0:512].rearrange("p (s f) -> p s f", f=128)
                                    T.op("dve" if m % 2 == 0 else "act", (CP if m % 2 == 0 else ACP)(xst[xb][:, :, m * 128:(m + 1) * 128], src),
                                         reads=[("pt", pti)], writes=[("xst", xb, m)])
                                    if j >= 32:
                                        T.op("pool", CP(bcst[xb][:, m, :], sgb[b][:]), reads=[("sgb", b)], writes=[("bcst", xb, m)])
                                else:
                                    T.op("act", ACTF(bcst[xb][:, m, :], acc[b][:], AF.Silu), reads=[("acc", b)], writes=[("bcst", xb, m)])
                            pend.append(back)
                            if m == 3:
                                def store(pn=pn, ts_=ts_, xb=xb):
                                    if pn < 8:
                                        T.dma(DMA(D["XS"][ts_, pn * 512:(pn + 1) * 512].rearrange("(s p) f -> p s f", p=128), xst[xb][:]),
                                              reads=[("xst", xb, m_) for m_ in range(4)])
                                    elif pn < 10:
                                        T.dma(DMA(D["BTM"][ts_, (pn - 8) * 512:(pn - 7) * 512].rearrange("(s p) f -> p s f", p=128), xst[xb][:]),
                                              reads=[("xst", xb, m_) for m_ in range(4)])
                                        T.dma(DMA(D["BT"][(pn - 8) * 512:(pn - 7) * 512, ts_].rearrange("(m p) t -> p m t", p=128), bcst[xb][:]),
                                              reads=[("bcst", xb, m_) for m_ in range(4)])
                                    else:
                                        T.dma(DMA(D["CT"][(pn - 10) * 512:(pn - 9) * 512, ts_].rearrange("(m p) t -> p m t", p=128), bcst[xb][:]),
                                              reads=[("bcst", xb, m_) for m_ in range(4)])
                                pend.append(store)
            for fn in pend:
                fn()

    def phase_ssdcore(self, l):
        D, T = self.D, self.T
        i_ = l // 2
        with self.phase("score") as P:
            ident = P.sb("ident", [128, 128], BF16)
            ucm = P.sb("ucm", [128, 128], F32)
            usm = P.sb("usm", [128, 128], F32)
            onesf = P.sb("onesf", [128, 128], F32)
            ones1 = P.sb("ones1", [1, 128], F32)
            row = P.sb("row", [1, 64], F32)
            dsk = P.sb("dsk", [128, 64], F32)
            ngb = P.sb("ngb", [128, 4096], F32)
            nrw = [P.sb("nrw%d" % i, [1, 512], F32) for i in range(2)]
            xs = [P.sb("xs%d" % i, [128, 4096], BF16) for i in range(1)] * 2
            zs = [P.sb("zs%d" % i, [128, 4096], BF16) for i in range(1)] * 2
            btm = [P.sb("btm%d" % i, [128, 1024], BF16) for i in range(2)]
            bT = [P.sb("bT%d" % i, [128, 8, 128], BF16) for i in range(2)]
            cT = [P.sb("cT%d" % i, [128, 8, 128], BF16) for i in range(2)]
            dtt = [P.sb("dtt%d" % i, [128, 2, 64], F32) for i in range(2)]
            rbg = [P.sb("rbg%d" % i, [128, 8, 128], F32) for i in range(2)]
            cbm = P.sb("cbm", [128, 8, 128], F32)
            ee = [P.sb("ee%d" % i, [128, 4, 128], F32) for i in range(4)]
            mT = P.sb("mT", [128, 64, 128], BF16)
            xdt = P.sb("xdt", [128, 4096], BF16)
            xw = P.sb("xw", [128, 4096], BF16)
            eac = P.sb("eac", [128, 64], F32)
            tot = P.sb("tot", [128, 64], F32)
            edec = P.sb("edec", [128, 64], F32)
            ten = P.sb("ten", [128, 64], F32)
            acs = P.sb("acs", [128, 64], F32)
            st32 = P.sb("st32", [128, 4096], F32)
            stb = P.sb("stb", [128, 4096], BF16)
            t1 = [P.sb("t1%d" % i, [128, 512], F32) for i in range(2)]
            t2 = [P.sb("t2%d" % i, [128, 512], F32) for i in range(2)]
            yg = P.sb("yg", [128, 4096], F32)
            ynb = P.sb("ynb", [128, 4096], BF16)
            ss = P.sb("ss", [128, 1], F32)
            sd = P.sb("sd", [128, 1], F32)
            rstd = P.sb("rstd", [128, 1], F32)
            yst = [P.sb("yst%d" % i, [128, 32, 128], BF16) for i in range(1)] * 2
            P.banks(6)
            pt = [P.ps("pt%d" % i, [128, 1024], BF16) for i in range(2)]
            T.dma(DMA(ident[:], D["ident"]), writes=["ident"])
            T.dma(DMA(ucm[:], D["UCM"]), writes=["ucm"])
            T.dma(DMA(usm[:], D["USM"]), writes=["usm"])
            T.dma(DMA(row[:], D["ssd_d"][i_]), writes=["row"])
            T.op("pool", MSET(ones1[:], 1.0), writes=["ones1"])
            T.op("pool", MSET(onesf[:], 1.0), writes=["onesf"])
            T.op("pool", MSET(st32[:], 0.0), writes=[("st32", g) for g in range(8)])
            T.op("pool", MSET(stb[:], 0.0), writes=[("stb", g) for g in range(8)])
            pi = P.nextp()
            self.mmg(P.pb[pi][:, 0:64], [(ones1[0:1, :], row[0:1, :])], ["ones1", "row"], ("pb", pi))
            T.op("act", ACP(dsk[:], P.pb[pi][:, 0:64]), reads=[("pb", pi)], writes=["dsk"])
            for n in range(8):
                pi = P.nextp()
                T.dma(DMA(nrw[n % 2][:], D["ssd_ng"][i_][:, n * 512:(n + 1) * 512]), writes=[("nrw", n % 2)])
                self.mmg(P.pb[pi][:], [(ones1[0:1, :], nrw[n % 2][0:1, :])], ["ones1", ("nrw", n % 2)], ("pb", pi))
                T.op("act", ACP(ngb[:, n * 512:(n + 1) * 512], P.pb[pi][:]), reads=[("pb", pi)], writes=["ngb"])
            tp = 0
            def prologue(ck):
                    b = ck % 2
                    r0 = ck * 128
                    rs_ = slice(r0, r0 + 128)
                    T.dma(DMA(xs[b][:], D["XS"][rs_, :]), writes=[("xs", 0), ("xs", 1)])
                    T.dma(DMA(zs[b][:], D["ZS"][rs_, :]), writes=[("zs", 0), ("zs", 1)])
                    T.dma(DMA(btm[b][:], D["BTM"][rs_, :]), writes=[("btm", b)])
                    T.dma(DMA(bT[b][:], D["BT"][:, rs_].rearrange("(g p) t -> p g t", p=128)), writes=[("bT", b)])
                    T.dma(DMA(cT[b][:], D["CT"][:, rs_].rearrange("(g p) t -> p g t", p=128)), writes=[("cT", b)])
                    T.dma(DMA(dtt[b][:], D["DT"][rs_, :, :]), writes=[("dtt", b)])
                    dt_ = dtt[b][:, 0, :]
                    dta = dtt[b][:, 1, :]
                    pa = P.nextp()
                    dboth = dtt[b][:].rearrange("p a h -> p (a h)")
                    self.mmg(P.pb[pa][:, 0:128], [(ucm[:], dboth)], ["ucm", ("dtt", b)], ("pb", pa))
                    T.op("act", ACTF(eac[:], P.pb[pa][:, 64:128], AF.Exp), reads=[("pb", pa)], writes=["eac"])
                    T.op("act", ACP(acs[:], P.pb[pa][:, 64:128]), reads=[("pb", pa)], writes=["acs"])
                    ptot = P.nextp()
                    self.mmg(P.pb[ptot][:, 0:128], [(onesf[:], dboth)], ["onesf", ("dtt", b)], ("pb", ptot))
                    T.op("act", ACTF(edec[:], P.pb[ptot][:, 64:128], AF.Exp), reads=[("pb", ptot)], writes=["edec"])
                    T.op("act", ACP(tot[:], P.pb[ptot][:, 64:128]), reads=[("pb", ptot)], writes=["tot"])
                    T.op("dve", TT(ten[:], tot[:], acs[:], ALU.subtract), reads=["tot", "acs"], writes=["ten"])
                    T.op("act", ACTF(ten[:], ten[:], AF.Exp), reads=["ten"], writes=["ten"])
                    T.op("pool", TT(xdt[:].rearrange("p (h e) -> p h e", e=64), xs[b][:].rearrange("p (h e) -> p h e", e=64),
                                    dt_.unsqueeze(2).broadcast_to([128, 64, 64]), ALU.mult), reads=[("xs", b), ("dtt", b)], writes=["xdt"])
                    T.op("pool", TT(xw[:].rearrange("p (h e) -> p h e", e=64), xdt[:].rearrange("p (h e) -> p h e", e=64),
                                    ten[:].unsqueeze(2).broadcast_to([128, 64, 64]), ALU.mult), reads=["xdt", "ten"], writes=["xw"])
                    for g in range(8):
                        pc = P.nextp()
                        self.mmg(P.pb[pc][:, 0:128], [(bT[b][:, g, :], cT[b][:, g, :])], [("bT", b), ("cT", b)], ("pb", pc))
                        T.op("dve", TT(cbm[:, g, :], P.pb[pc][:, 0:128], ucm[:], ALU.mult), reads=[("pb", pc), "ucm"], writes=[("cbm", g)])
                    return dict(b=b, dta=dta, rs_=rs_)

            def groups(ck, ctx):
                    b = ctx['b']; dta = ctx['dta']
                    live = {}

                    def stA(g):
                        hs = slice(g * 8, (g + 1) * 8)
                        rb = rbg[g % 2]
                        T.op("dve", TT(rb[:], dta[:, hs].unsqueeze(2).broadcast_to([128, 8, 128]), ucm[:].unsqueeze(1).broadcast_to([128, 8, 128]), ALU.mult),
                             reads=[("dtt", b), "ucm"], writes=[("rb", g % 2)])
                        for half in range(2):
                            eb = (g * 2 + half) % 4
                            pseg = half
                            self.mmg(P.pb[pseg][:], [(usm[:], rb[:, half * 4:half * 4 + 4, :].rearrange("p h l -> p (h l)"))], ["usm", ("rb", g % 2)], ("pb", pseg))
                            T.op("act", ACTF(ee[eb][:].rearrange("p h l -> p (h l)"), P.pb[pseg][:], AF.Exp), reads=[("pb", pseg)], writes=[("ee", eb)])

                    def stB(g):
                        for half in range(2):
                            eb = (g * 2 + half) % 4
                            h0 = g * 8 + half * 4
                            T.op("pool" if half == 0 else "dve",
                                 TT(mT[:, h0:h0 + 4, :], ee[eb][:], cbm[:, g, :].unsqueeze(1).broadcast_to([128, 4, 128]), ALU.mult),
                                 reads=[("ee", eb), ("cbm", g)], writes=[("mT", g)])
                        pyd = 3 + g % 2
                        for hh in range(8):
                            h = g * 8 + hh
                            T.op("pe", MM(P.pb[pyd][:, hh * 64:(hh + 1) * 64], mT[:, h, :], xdt[:, h * 64:(h + 1) * 64], True, True),
                                 reads=[("mT", g), "xdt"] if hh == 0 else (), writes=[("pb", pyd)], event=(hh == 7))
                        pyo = 5
                        self.mmg(P.pb[pyo][:], [(cT[b][:, g, :], stb[:, g * 512:(g + 1) * 512])], [("cT", b), ("stb", g)], ("pb", pyo))
                        live[g] = (pyd, pyo)

                    def stC(g):
                        hs = slice(g * 8, (g + 1) * 8)
                        pyd, pyo = live.pop(g)
                        gb = g % 2
                        gsl = slice(g * 512, (g + 1) * 512)
                        T.op("dve", TT(t1[gb][:].rearrange("p (h e) -> p h e", e=64), P.pb[pyo][:].rearrange("p (h e) -> p h e", e=64),
                                        eac[:, hs].unsqueeze(2).broadcast_to([128, 8, 64]), ALU.mult), reads=[("pb", pyo), "eac"], writes=[("t1", gb)])
                        T.op("dve", TT(t1[gb][:], t1[gb][:], P.pb[pyd][:], ALU.add), reads=[("t1", gb), ("pb", pyd)], writes=[("t1", gb)])
                        T.op("pool", TT(t2[gb][:].rearrange("p (h e) -> p h e", e=64), xs[b][:, gsl].rearrange("p (h e) -> p h e", e=64),
                                         dsk[:, hs].unsqueeze(2).broadcast_to([128, 8, 64]), ALU.mult), reads=[("xs", b), "dsk"], writes=[("t2", gb)])
                        T.op("pool", TT(t2[gb][:], t2[gb][:], t1[gb][:], ALU.add), reads=[("t2", gb), ("t1", gb)], writes=[("t2", gb)])
                        T.op("pool", TT(yg[:, gsl], t2[gb][:], zs[b][:, gsl], ALU.mult), reads=[("t2", gb), ("zs", b)], writes=[("yg", g)])
                        pst = 2
                        self.mmg(P.pb[pst][:], [(btm[b][:, g * 128:(g + 1) * 128], xw[:, gsl])], [("btm", b), "xw"], ("pb", pst))
                        T.op("dve", TT(st32[:, gsl].rearrange("p (h e) -> p h e", e=64), st32[:, gsl].rearrange("p (h e) -> p h e", e=64),
                                        edec[:, hs].unsqueeze(2).broadcast_to([128, 8, 64]), ALU.mult), reads=[("st32", g), "edec"], writes=[("st32", g)])
                        T.op("dve", TT(st32[:, gsl], st32[:, gsl], P.pb[pst][:], ALU.add), reads=[("st32", g), ("pb", pst)], writes=[("st32", g)])
                        T.op("act", ACP(stb[:, gsl], st32[:, gsl]), reads=[("st32", g)], writes=[("stb", g)])

                    for step in range(8 + 2):
                        if step < 8:
                            stA(step)
                        if 0 <= step - 2 < 8:
                            stC(step - 2)
                        if 0 <= step - 1 < 8:
                            stB(step - 1)

            def epilogue(ck, ctx):
                    nonlocal tp
                    rs_ = ctx['rs_']
                    T.op("act", ACTF(ynb[:], yg[:], AF.Square, accum_out=ss[:, 0:1]), reads=[("yg", g) for g in range(8)], writes=["ss", ("ynb", 0), ("ynb", 1)])
                    T.op("act", ACTF(sd[:], ss[:], AF.Sqrt, scale=1.0 / 4096, bias=EPS), reads=["ss"], writes=["sd"])
                    T.op("dve", RECIP(rstd[:], sd[:]), reads=["sd"], writes=["rstd"])
                    for hf in range(2):
                        fs = slice(hf * 2048, (hf + 1) * 2048)
                        T.op("dve", STT(ynb[:, fs], yg[:, fs], rstd[:, 0:1], ngb[:, fs], ALU.mult, ALU.mult),
                             reads=[("yg", g) for g in range(8)] + ["rstd", "ngb"], writes=[("ynb", hf)])
                    ya = 0
                    for q8 in range(8):
                        pti = tp % 2
                        tp += 1
                        for c4 in range(4):
                            c = q8 * 4 + c4
                            T.op("pe", TR(pt[pti][:, c4 * 128:(c4 + 1) * 128], ynb[:, c * 128:(c + 1) * 128], ident[:]),
                                 reads=[("ynb", q8 // 4), "ident"], writes=[("pt", pti)], event=(c4 == 3))
                        src = pt[pti][:, 0:512].rearrange("p (c t) -> p c t", t=128)
                        if q8 % 2 == 0:
                            T.op("dve", CP(yst[ya][:, q8 * 4:(q8 + 1) * 4, :], src), reads=[("pt", pti)], writes=[("yst", ya, q8)])
                        else:
                            T.op("act", ACP(yst[ya][:, q8 * 4:(q8 + 1) * 4, :], src), reads=[("pt", pti)], writes=[("yst", ya, q8)])
                    for q8 in range(8):
                        T.dma(DMA(D["YT"][q8 * 512:(q8 + 1) * 512, rs_].rearrange("(c p) t -> p c t", p=128), yst[ya][:, q8 * 4:(q8 + 1) * 4, :]),
                              reads=[("yst", ya, q8)])

            nck = getattr(self, "dbg_nck", 32)
            ctx = prologue(0)
            for ck in range(nck):
                groups(ck, ctx)
                nxt = prologue(ck + 1) if ck + 1 < nck else None
                epilogue(ck, ctx)
                ctx = nxt


def _consts():
    c = {}
    pos = np.arange(S, dtype=np.float32)
    inv_r = (np.float32(10000.0) ** (-np.arange(128, dtype=np.float32) / np.float32(128))).astype(np.float32)
    ang = (pos[None, :] * inv_r[:, None]).astype(np.float32)
    c["COSR"] = np.cos(ang).astype(np.float32)
    c["SINR"] = np.sin(ang).astype(np.float32)
    inv_m = (np.float32(10000.0) ** (-np.arange(32, dtype=np.float32) / np.float32(32))).astype(np.float32)
    angm = (pos[None, :] * inv_m[:, None]).astype(np.float32)
    cm, sm = np.cos(angm).astype(np.float32), np.sin(angm).astype(np.float32)
    c["C2"] = np.concatenate([cm, cm, cm, cm], 0)
    c["S2"] = np.concatenate([-sm, sm, -sm, sm], 0)
    lg = np.log1p(-np.exp2(-5.0 - np.arange(4, dtype=np.float64)))
    idx = np.arange(128, dtype=np.float64)
    rel = idx[None, :] - idx[:, None]
    dec = np.where(rel >= 0, np.exp(lg[:, None, None] * np.maximum(rel, 0.0)), 0.0)
    c["DECT"] = np.ascontiguousarray(dec.transpose(1, 0, 2)).astype(np.float32)
    xi = np.exp(lg[:, None] * (idx[None, :] + 1.0))
    c["XI"] = np.ascontiguousarray(np.broadcast_to(xi[None], (128, 4, 128))).astype(np.float32)
    c["ZETA"] = np.ascontiguousarray(np.exp(lg[None, :] * (127.0 - idx[:, None]))).astype(np.float32)
    c["g128"] = np.exp(lg * 128.0)
    k = np.arange(128)
    c["MASKT"] = np.where((k[:, None] >= 64) & (k[None, :] < 64), 0.0, 1.0).astype(ml_dtypes.bfloat16)
    c["UCM"] = (k[:, None] <= k[None, :]).astype(np.float32)
    c["USM"] = (k[:, None] > k[None, :]).astype(np.float32)
    c["ident"] = np.eye(128, dtype=np.float32).astype(ml_dtypes.bfloat16)
    return c


def _prep_weights(inp):
    w = {}
    f = lambda a: np.ascontiguousarray(a, dtype=np.float32)
    w["ada_w"] = f(inp["ada_w"])
    w["ada_b"] = f(inp["ada_b"]).reshape(4, 1, 12288)
    w["norm_mix_g"] = f(inp["norm_mix_g"]).reshape(4, 1, 2048)
    w["norm_ffn_g"] = f(inp["norm_ffn_g"]).reshape(4, 1, 2048)
    win = inp["hyb_w_in"]
    kr = win[:, :, 7168:7232]
    kra = np.concatenate([kr, kr], -1)
    krb = np.concatenate([kr[:, :, 32:64], kr[:, :, 0:32], kr[:, :, 32:64], kr[:, :, 0:32]], -1)
    w["hyb_win"] = f(np.concatenate([win[:, :, 0:7168], kra, krb], -1))
    uq = inp["hyb_w_uq"].reshape(2, 512, 16, 192)
    qn = uq[:, :, :, 0:128].reshape(2, 512, 2048)
    ra = uq[:, :, :, 128:192].reshape(2, 512, 1024)
    rb = np.concatenate([uq[:, :, :, 160:192], uq[:, :, :, 128:160]], -1).reshape(2, 512, 1024)
    w["hyb_wuq"] = f(np.concatenate([qn, ra, rb], -1))
    ukv = inp["hyb_w_ukv"].reshape(2, 512, 16, 256)
    w["hyb_wukv"] = f(np.concatenate([ukv[:, :, :, 0:128].reshape(2, 512, 2048), ukv[:, :, :, 128:256].reshape(2, 512, 2048)], -1))
    w["hyb_qg"] = f(inp["hyb_q_norm_g"].reshape(2, 4, 128).transpose(0, 2, 1))
    w["hyb_kvg"] = f(inp["hyb_kv_norm_g"].reshape(2, 4, 128).transpose(0, 2, 1))
    w["hyb_gn"] = f(inp["hyb_ret_gn_g"]).reshape(2, 1, 2048)
    w["hyb_wout"] = f(inp["hyb_w_out"])
    w["ssd_win"] = f(inp["ssd_w_in"])
    w["ssd_cw"] = f(inp["ssd_conv_w"].transpose(0, 2, 1).reshape(2, 48, 128, 4).transpose(0, 2, 1, 3))
    w["ssd_cb"] = f(inp["ssd_conv_b"].reshape(2, 48, 128).transpose(0, 2, 1))
    w["ssd_dtb"] = f(inp["ssd_dt_bias"]).reshape(2, 1, 64)
    w["ssd_alog"] = f(inp["ssd_a_log"]).reshape(2, 1, 64)
    w["ssd_d"] = f(inp["ssd_d"]).reshape(2, 1, 64)
    w["ssd_ng"] = f(inp["ssd_norm_g"]).reshape(2, 1, 4096)
    w["ssd_wout"] = f(inp["ssd_w_out"])
    w["ffn_wup"] = f(inp["ffn_w_up"])
    w["ffn_cw"] = f(inp["ffn_conv_w"].transpose(0, 2, 1).reshape(4, 44, 128, 3).transpose(0, 2, 1, 3))
    w["ffn_cb"] = f(inp["ffn_conv_b"].reshape(4, 44, 128).transpose(0, 2, 1))
    w["ffn_wdown"] = f(inp["ffn_w_down"])
    w["final_norm_g"] = f(inp["final_norm_g"]).reshape(1, 2048)
    return w


def build(plan=None, dbg=False, only=None):
    B = Builder(dbg=dbg, only=only)
    nc, D = B.nc, B.D
    consts = _consts()
    B.consts = consts
    B.din("x", [S, DM])
    B.din("cT", [128, 16])
    B.din("ada_w", [4, 2048, 12288]); B.din("ada_b", [4, 1, 12288])
    B.din("norm_mix_g", [4, 1, 2048]); B.din("norm_ffn_g", [4, 1, 2048])
    B.din("hyb_win", [2, 2048, 7424]); B.din("hyb_wuq", [2, 512, 4096]); B.din("hyb_wukv", [2, 512, 4096])
    B.din("hyb_qg", [2, 128, 4]); B.din("hyb_kvg", [2, 128, 4]); B.din("hyb_gn", [2, 1, 2048]); B.din("hyb_wout", [2, 4096, 2048])
    B.din("ssd_win", [2, 2048, 10304]); B.din("ssd_cw", [2, 128, 48, 4]); B.din("ssd_cb", [2, 128, 48])
    B.din("ssd_dtb", [2, 1, 64]); B.din("ssd_alog", [2, 1, 64]); B.din("ssd_d", [2, 1, 64]); B.din("ssd_ng", [2, 1, 4096])
    B.din("ssd_wout", [2, 4096, 2048])
    B.din("ffn_wup", [4, 2048, 2 * FH]); B.din("ffn_cw", [4, 128, 44, 3]); B.din("ffn_cb", [4, 128, 44]); B.din("ffn_wdown", [4, FH, 2048])
    B.din("final_norm_g", [1, 2048])
    for k in ("COSR", "SINR", "C2", "S2"):
        B.din(k, [128, S])
    B.din("DECT", [128, 4, 128]); B.din("XI", [128, 4, 128]); B.din("ZETA", [128, 4])
    B.din("MASKT", [128, 128], BF16); B.din("UCM", [128, 128]); B.din("USM", [128, 128]); B.din("ident", [128, 128], BF16)
    B.dscr("out", [S, DM], F32, out=True)
    B.dscr("xres", [S, DM], F32)
    B.dscr("modsb", [4, 6, 128, 2048], F32)
    B.dscr("b_hyb_win", [2048, 7424], BF16); B.dscr("b_hyb_wuq", [512, 4096], BF16); B.dscr("b_hyb_wukv", [512, 4096], BF16)
    B.dscr("b_wout", [4096, 2048], BF16); B.dscr("b_ssd_win", [2048, 10304], BF16)
    B.dscr("b_ffn_wup", [2048, 2 * FH], BF16); B.dscr("b_ffn_wdown", [FH, 2048], BF16)
    B.dscr("HMT", [2048, S], BF16); B.dscr("YT", [4096, S], BF16); B.dscr("HT", [FH, S], BF16)
    B.dscr("RQT", [1024, S], BF16); B.dscr("RQXT", [1024, S], BF16); B.dscr("RKT", [1024, S], BF16)
    B.dscr("RV", [S, 2048], BF16); B.dscr("RG", [S, 2048], BF16)
    B.dscr("CQT", [512, S], BF16); B.dscr("CKVT", [512, S], BF16); B.dscr("KRT", [128, S], BF16)
    B.dscr("ZS", [S, 4096], BF16); B.dscr("XS", [S, 4096], BF16); B.dscr("BTM", [S, 1024], BF16)
    B.dscr("BT", [1024, S], BF16); B.dscr("CT", [1024, S], BF16); B.dscr("DT", [S, 2, 64], F32)
    with contextlib.ExitStack() as es:
        sems = [es.enter_context(nc.semaphore("s%d" % i)) for i in range(len(ENGS) + N_DMA_SLOTS)]
        B.T = Tracker(nc, sems)
        if plan is None:
            plan = [("ada", 0)]
            for l in range(4):
                plan += ([("cast", l)] if l > 0 else []) + [("mixer", l), ("ffn", l)]
            plan += [("final",)]
        xcur = D.get("x")
        for ph in plan:
            if ph[0] == "ada":
                B.phase_ada(ph[1] if len(ph) > 1 else None)
            elif ph[0] == "cast":
                B.phase_cast(ph[1])
            elif ph[0] == "mixer":
                l = ph[1]
                B.phase_norm(xcur, l, 1, 0, D["HMT"])
                if l % 2 == 0:
                    B.phase_hybproj(l)
                    B.phase_ret(l, consts)
                    B.phase_mla(l)
                else:
                    B.phase_ssdproj(l)
                    B.phase_ssdcore(l)
                B.phase_outproj(D["YT"], 32, D["b_wout"], l, 2, xcur, D["xres"])
                xcur = D["xres"]
            elif ph[0] == "sub":
                getattr(B, "phase_" + ph[1])(*[xcur if a == "X" else (D[a] if isinstance(a, str) else a) for a in ph[2:]])
            elif ph[0] == "ffn":
                l = ph[1]
                B.phase_norm(xcur, l, 4, 3, D["HMT"])
                B.phase_ffnup(l)
                B.phase_outproj(D["HT"], 44, D["b_ffn_wdown"], l, 5, xcur, D["xres"])
                xcur = D["xres"]
            elif ph[0] == "final":
                B.phase_final(xcur)
    return B


def make_in_maps(inputs, cores):
    w = _prep_weights(inputs)
    c = _consts()
    shared = dict(w)
    for k in ("COSR", "SINR", "C2", "S2", "DECT", "XI", "ZETA", "MASKT", "UCM", "USM", "ident"):
        shared[k] = c[k]
    maps = []
    for b in cores:
        m = dict(shared)
        m["x"] = np.ascontiguousarray(inputs["x"][b], dtype=np.float32)
        m["cT"] = np.ascontiguousarray(np.asarray(inputs["c"][b], dtype=np.float32).reshape(16, 128).T)
        maps.append(m)
    return maps


def kernel(**inputs):
    inputs = {k: np.asarray(v) for k, v in inputs.items()}
    B = build()
    maps = make_in_maps(inputs, list(range(N_CORES)))
    res = run_bass_kernel_spmd(B.nc, maps, core_ids=list(range(N_CORES)))
    out = np.stack([np.asarray(r["out"], dtype=np.float32) for r in res.results], 0)
    return out
```

```python
import contextlib
import math
import numpy as np
import ml_dtypes
import concourse.bass as bass
import concourse.mybir as mybir
from concourse.bass_utils import run_bass_kernel_spmd

F32 = mybir.dt.float32
BF16 = mybir.dt.bfloat16
ALU = mybir.AluOpType
AF = mybir.ActivationFunctionType

S = 4096
DM = 2048
NT = 8
EPS = 1e-6
FH = 5632
N_CORES = 8

ENGS = ("pe", "act", "dve", "pool", "sp")
N_DMA_SLOTS = 14


class Tracker:
    def __init__(self, nc, sems):
        self.nc = nc
        self.sem = {}
        self.count = {}
        it = iter(sems)
        for e in ENGS:
            self.sem[e] = next(it)
            self.count[e] = 0
        self.slots = []
        for i in range(N_DMA_SLOTS):
            n = "dma%d" % i
            self.sem[n] = next(it)
            self.count[n] = 0
            self.slots.append(n)
        self.slot_rr = 0
        self.seen = {e: {} for e in ENGS}
        self.ops = {e: [] for e in ENGS}
        self.lw = {}
        self.rd = {}
        self.n_ops = 0

    def _deps(self, reads, writes):
        deps = {}
        lw = self.lw
        for r in reads:
            ev = lw.get(r)
            if ev is not None and deps.get(ev[0], 0) < ev[1]:
                deps[ev[0]] = ev[1]
        for w in writes:
            ev = lw.get(w)
            if ev is not None and deps.get(ev[0], 0) < ev[1]:
                deps[ev[0]] = ev[1]
            d = self.rd.get(w)
            if d:
                for p, c in d.items():
                    if deps.get(p, 0) < c:
                        deps[p] = c
        return deps

    def _emit_waits(self, eng, deps):
        seen = self.seen[eng]
        for p, c in deps.items():
            if p == eng and eng in ("pe", "sp"):
                continue
            if seen.get(p, 0) >= c:
                continue
            seen[p] = c
            self.ops[eng].append((0, (p, c)))

    def _commit(self, ev, reads, writes):
        p, c = ev
        for r in reads:
            d = self.rd.setdefault(r, {})
            if d.get(p, 0) < c:
                d[p] = c
        for w in writes:
            self.lw[w] = ev
            self.rd[w] = {}

    max_ops = 10 ** 9

    def op(self, eng, fn, reads=(), writes=(), event=True):
        if self.n_ops >= self.max_ops:
            return
        self._emit_waits(eng, self._deps(reads, writes))
        if event:
            self.count[eng] += 1
            ev = (eng, self.count[eng])
            self.ops[eng].append((1, fn))
        else:
            ev = (eng, self.count[eng] + 1)
            self.ops[eng].append((2, fn))
        self._commit(ev, reads, writes)
        self.n_ops += 1

    def dma(self, fn, reads=(), writes=(), q="sp"):
        if self.n_ops >= self.max_ops:
            return
        deps = self._deps(reads, writes)
        slot = self.slots[self.slot_rr]
        self.slot_rr = (self.slot_rr + 1) % len(self.slots)
        if deps.get(slot, 0) < self.count[slot]:
            deps[slot] = self.count[slot]
        self._emit_waits(q, deps)
        self.count[slot] += 16
        ev = (slot, self.count[slot])
        self.ops[q].append((3, (fn, slot)))
        self._commit(ev, reads, writes)
        self.n_ops += 1

    def barrier(self):
        deps = {p: c for p, c in self.count.items() if c > 0}
        for e in ENGS:
            self._emit_waits(e, dict(deps))
        self.lw = {}
        self.rd = {}

    def _replay(self, eng, e):
        sem = self.sem
        for kind, pl in self.ops[eng]:
            if kind == 0:
                e.wait_ge(sem[pl[0]], pl[1])
            elif kind == 1:
                pl(e).then_inc(sem[eng], 1)
            elif kind == 2:
                pl(e)
            else:
                pl[0](e).then_inc(sem[pl[1]], 16)
        self.ops[eng] = []

    def emit(self):
        with self.nc.Block() as block:
            @block.tensor
            def _(e):
                self._replay("pe", e)

            @block.scalar
            def _(e):
                self._replay("act", e)

            @block.vector
            def _(e):
                self._replay("dve", e)

            @block.gpsimd
            def _(e):
                self._replay("pool", e)

            @block.sync
            def _(e):
                self._replay("sp", e)


def MM(out, lhsT, rhs, st, sp):
    return lambda e: e.matmul(out, lhsT=lhsT, rhs=rhs, start=st, stop=sp)

def TR(out, in_, ident):
    return lambda e: e.transpose(out=out, in_=in_, identity=ident)

def ACTF(out, in_, func, **kw):
    return lambda e: e.activation(out=out, in_=in_, func=func, **kw)

def TT(out, a, b, op):
    return lambda e: e.tensor_tensor(out=out, in0=a, in1=b, op=op)

def TS(out, a, s1, s2, op0, op1):
    return lambda e: e.tensor_scalar(out=out, in0=a, scalar1=s1, scalar2=s2, op0=op0, op1=op1)

def STT(out, a, s, b, op0, op1):
    return lambda e: e.scalar_tensor_tensor(out=out, in0=a, scalar=s, in1=b, op0=op0, op1=op1)

def CP(out, in_):
    return lambda e: e.tensor_copy(out=out, in_=in_)

def ACP(out, in_):
    return lambda e: e.copy(out=out, in_=in_)

def RECIP(out, in_):
    return lambda e: e.reciprocal(out=out, in_=in_)

def MSET(ap, v):
    return lambda e: e.memset(ap, v)

def DMA(out, in_):
    return lambda e: e.dma_start(out=out, in_=in_)


class Phase:
    def __init__(self, B, name):
        self.B = B
        self.name = name
        self.es = contextlib.ExitStack()
        self.np_ = 0
        self.rr = 0

    def __enter__(self):
        self.es.__enter__()
        self.es.enter_context(self.B.nc.named_scope(self.name))
        return self

    def __exit__(self, *a):
        self.B.T.barrier()
        self.B.T.emit()
        return self.es.__exit__(*a)

    def sb(self, name, shape, dt):
        return self.es.enter_context(self.B.nc.sbuf_tensor(self.name + "_" + name, shape, dt))

    def ps(self, name, shape=(128, 512), dt=F32):
        return self.es.enter_context(self.B.nc.psum_tensor(self.name + "_" + name, list(shape), dt))

    def banks(self, n):
        self.pb = [self.ps("pb%d" % i) for i in range(n)]
        self.npb = n

    def nextp(self):
        i = self.rr
        self.rr = (self.rr + 1) % self.npb
        return i


class Builder:
    def __init__(self, dbg=False, only=None):
        self.nc = bass.Bass("TRN2", target_bir_lowering=False)
        self.D = {}
        self.dbg = dbg
        self.only = only
        self.uid = 0

    def din(self, name, shape, dt=F32):
        if self.only is not None and name not in self.only:
            return
        self.D[name] = self.nc.dram_tensor(name, list(shape), dt, kind="ExternalInput").ap()

    def dscr(self, name, shape, dt, out=False):
        kind = "ExternalOutput" if (out or (self.dbg and name in self.dbg)) else "Internal"
        self.D[name] = self.nc.dram_tensor(name, list(shape), dt, kind=kind).ap()

    def phase(self, name):
        self.uid += 1
        return Phase(self, "%s%d" % (name, self.uid))

    def mmg(self, out, pairs, reads, wkey):
        n = len(pairs)
        for k, (l, r) in enumerate(pairs):
            self.T.op("pe", MM(out, l, r, k == 0, k == n - 1), reads=reads if k == 0 else (), writes=[wkey], event=(k == n - 1))

    def cast(self, src, dst, rows, blk=256):
        for r0 in range(0, rows, blk):
            self.T.dma(DMA(dst[r0:r0 + blk, :], src[r0:r0 + blk, :]), q="pool")

    def phase_cast(self, l):
        D = self.D
        i = l // 2
        with self.phase("cast"):
            if l % 2 == 0:
                self.cast(D["hyb_win"][i], D["b_hyb_win"], 2048)
                self.cast(D["hyb_wuq"][i], D["b_hyb_wuq"], 512)
                self.cast(D["hyb_wukv"][i], D["b_hyb_wukv"], 512)
                self.cast(D["hyb_wout"][i], D["b_wout"], 4096)
            else:
                self.cast(D["ssd_win"][i], D["b_ssd_win"], 2048)
                self.cast(D["ssd_wout"][i], D["b_wout"], 4096)
            self.cast(D["ffn_wup"][l], D["b_ffn_wup"], 2048)
            self.cast(D["ffn_wdown"][l], D["b_ffn_wdown"], FH)

    def phase_ada(self):
        D, T = self.D, self.T
        with self.phase("ada") as P:
            sc = P.sb("sc", [128, 16], F32)
            scs = P.sb("scs", [128, 16], F32)
            scb = P.sb("scb", [128, 16, 128], F32)
            ones1 = P.sb("ones1", [1, 128], F32)
            wp = [P.sb("wp%d" % i, [128, 16, 512], F32) for i in range(3)]
            br = [P.sb("br%d" % i, [1, 512], F32) for i in range(2)]
            gr = [P.sb("gr%d" % i, [1, 512], F32) for i in range(2)]
            gb = [P.sb("gb%d" % i, [128, 512], F32) for i in range(2)]
            rs = [P.sb("rs%d" % i, [128, 512], F32) for i in range(2)]
            pm = [P.ps("pm%d" % i) for i in range(2)]
            pg = [P.ps("pg%d" % i) for i in range(2)]
            T.dma(DMA(sc[:], D["cT"]), writes=["sc"])
            T.op("pool", MSET(ones1[:], 1.0), writes=["ones1"])
            T.op("act", ACTF(scs[:], sc[:], AF.Silu), reads=["sc"], writes=["scs"])
            T.op("dve", CP(scb[:], scs[:].unsqueeze(2).broadcast_to([128, 16, 128])), reads=["scs"], writes=["scb"])
            it = 0
            for l in range(4):
                for j in range(6):
                    for n in range(4):
                        i = it % 2
                        i3 = it % 3
                        it += 1
                        c0 = j * 2048 + n * 512
                        T.dma(DMA(wp[i3][:], D["ada_w"][l, :, c0:c0 + 512].rearrange("(kc p) n -> p kc n", p=128)), writes=[("wp", i3)])
                        T.dma(DMA(br[i][:], D["ada_b"][l, :, c0:c0 + 512]), writes=[("br", i)])
                        pairs = [(scb[:, kc, :], wp[i3][:, kc, :]) for kc in range(16)] + [(ones1[0:1, :], br[i][0:1, :])]
                        self.mmg(pm[i][:], pairs, ["scb", "ones1", ("wp", i3), ("br", i)], ("pm", i))
                        if j in (1, 4):
                            gsrc = D["norm_mix_g"] if j == 1 else D["norm_ffn_g"]
                            T.dma(DMA(gr[i][:], gsrc[l, :, n * 512:(n + 1) * 512]), writes=[("gr", i)])
                            self.mmg(pg[i][:], [(ones1[0:1, :], gr[i][0:1, :])], ["ones1", ("gr", i)], ("pg", i))
                            T.op("act", ACP(gb[i][:], pg[i][:]), reads=[("pg", i)], writes=[("gb", i)])
                            T.op("dve", STT(rs[i][:], pm[i][:], 1.0, gb[i][:], ALU.add, ALU.mult), reads=[("pm", i), ("gb", i)], writes=[("rs", i)])
                        else:
                            T.op("act", ACP(rs[i][:], pm[i][:]), reads=[("pm", i)], writes=[("rs", i)])
                        T.dma(DMA(D["modsb"][l, j, :, n * 512:(n + 1) * 512], rs[i][:]), reads=[("rs", i)])

    def phase_norm(self, x_src, l, j_gs, j_sh, dst):
        D, T = self.D, self.T
        with self.phase("norm") as P:
            gs = P.sb("gs", [128, 2048], F32)
            sh = P.sb("sh", [128, 2048], F32)
            ident = P.sb("ident", [128, 128], BF16)
            xt = [P.sb("xt%d" % i, [128, 4, 2048], F32) for i in range(2)]
            tmp = [P.sb("tmp%d" % i, [128, 2048], F32) for i in range(2)]
            hmb = [P.sb("hmb%d" % i, [128, 4, 2048], BF16) for i in range(2)]
            hT = [P.sb("hT%d" % i, [128, 16, 512], BF16) for i in range(2)]
            junk = P.sb("junk", [128, 2048], BF16)
            ss = P.sb("ss", [128, 8], F32)
            sd = P.sb("sd", [128, 8], F32)
            rstd = P.sb("rstd", [128, 8], F32)
            pt = [P.ps("pt%d" % i, [128, 1024], BF16) for i in range(4)]
            T.dma(DMA(gs[:], D["modsb"][l, j_gs]), writes=["gs"])
            T.dma(DMA(sh[:], D["modsb"][l, j_sh]), writes=["sh"])
            T.dma(DMA(ident[:], D["ident"]), writes=["ident"])
            g = 0
            for tt in range(NT):
                i = tt % 2
                T.dma(DMA(xt[i][:], x_src[tt * 512:(tt + 1) * 512, :].rearrange("(s p) d -> p s d", p=128)), writes=[("xt", i)])
                for s in range(4):
                    c = i * 4 + s
                    T.op("act", ACTF(junk[:], xt[i][:, s, :], AF.Square, accum_out=ss[:, c:c + 1]), reads=[("xt", i)], writes=[("ss", c)])
                    T.op("act", ACTF(sd[:, c:c + 1], ss[:, c:c + 1], AF.Sqrt, scale=1.0 / DM, bias=EPS), reads=[("ss", c)], writes=[("sd", c)])
                    T.op("dve", RECIP(rstd[:, c:c + 1], sd[:, c:c + 1]), reads=[("sd", c)], writes=[("rstd", c)])
                    T.op("dve", STT(tmp[s % 2][:], xt[i][:, s, :], rstd[:, c:c + 1], gs[:], ALU.mult, ALU.mult),
                         reads=[("xt", i), ("rstd", c), "gs"], writes=[("tmp", s % 2)])
                    T.op("pool", TT(hmb[i][:, s, :], tmp[s % 2][:], sh[:], ALU.add), reads=[("tmp", s % 2), "sh"], writes=[("hmb", i, s)])
                for kc in range(16):
                    pi = g % 4
                    g += 1
                    for s in range(4):
                        T.op("pe", TR(pt[pi][:, s * 128:(s + 1) * 128], hmb[i][:, s, kc * 128:(kc + 1) * 128], ident[:]),
                             reads=[("hmb", i, s), "ident"], writes=[("pt", pi)], event=(s == 3))
                    if kc % 2 == 0:
                        T.op("dve", CP(hT[i][:, kc, :], pt[pi][:, 0:512]), reads=[("pt", pi)], writes=[("hT", i, kc)])
                    else:
                        T.op("act", ACP(hT[i][:, kc, :], pt[pi][:, 0:512]), reads=[("pt", pi)], writes=[("hT", i, kc)])
                T.dma(DMA(dst[:, tt * 512:(tt + 1) * 512].rearrange("(kc p) t -> p kc t", p=128), hT[i][:]),
                      reads=[("hT", i, kc) for kc in range(16)])

    def phase_final(self, x_src):
        D, T = self.D, self.T
        with self.phase("fin") as P:
            gsb = P.sb("gsb", [128, 2048], F32)
            grow = P.sb("grow", [1, 2048], F32)
            ones1 = P.sb("ones1", [1, 128], F32)
            xt = [P.sb("xt%d" % i, [128, 4, 2048], F32) for i in range(2)]
            ot = [P.sb("ot%d" % i, [128, 4, 2048], F32) for i in range(2)]
            junk = P.sb("junk", [128, 2048], BF16)
            ss = P.sb("ss", [128, 8], F32)
            sd = P.sb("sd", [128, 8], F32)
            rstd = P.sb("rstd", [128, 8], F32)
            P.banks(4)
            T.dma(DMA(grow[:], D["final_norm_g"]), writes=["grow"])
            T.op("pool", MSET(ones1[:], 1.0), writes=["ones1"])
            for n in range(4):
                self.mmg(P.pb[n][:], [(ones1[0:1, :], grow[0:1, n * 512:(n + 1) * 512])], ["ones1", "grow"], ("pb", n))
                T.op("act", ACP(gsb[:, n * 512:(n + 1) * 512], P.pb[n][:]), reads=[("pb", n)], writes=["gsb"])
            for tt in range(NT):
                i = tt % 2
                T.dma(DMA(xt[i][:], x_src[tt * 512:(tt + 1) * 512, :].rearrange("(s p) d -> p s d", p=128)), writes=[("xt", i)])
                for s in range(4):
                    c = i * 4 + s
                    T.op("act", ACTF(junk[:], xt[i][:, s, :], AF.Square, accum_out=ss[:, c:c + 1]), reads=[("xt", i)], writes=[("ss", c)])
                    T.op("act", ACTF(sd[:, c:c + 1], ss[:, c:c + 1], AF.Sqrt, scale=1.0 / DM, bias=EPS), reads=[("ss", c)], writes=[("sd", c)])
                    T.op("dve", RECIP(rstd[:, c:c + 1], sd[:, c:c + 1]), reads=[("sd", c)], writes=[("rstd", c)])
                    T.op("dve", STT(ot[i][:, s, :], xt[i][:, s, :], rstd[:, c:c + 1], gsb[:], ALU.mult, ALU.mult),
                         reads=[("xt", i), ("rstd", c), "gsb"], writes=[("ot", i, s)])
                T.dma(DMA(D["out"][tt * 512:(tt + 1) * 512, :].rearrange("(s p) d -> p s d", p=128), ot[i][:]),
                      reads=[("ot", i, s) for s in range(4)])

    def phase_outproj(self, AT, KC, W, l, j_gate, x_src, x_dst):
        D, T = self.D, self.T
        TB = 1024
        PW = 512 if KC <= 32 else 256
        with self.phase("oproj") as P:
            aT = P.sb("aT", [128, KC, TB], BF16)
            wpan = [P.sb("wpan%d" % i, [128, KC, PW], BF16) for i in range(2)]
            gate = P.sb("gate", [128, 2048], F32)
            xp = [P.sb("xp%d" % i, [128, PW], F32) for i in range(4)]
            tm = [P.sb("tm%d" % i, [128, PW], F32) for i in range(4)]
            P.banks(6)
            T.dma(DMA(gate[:], D["modsb"][l, j_gate]), writes=["gate"])
            g = 0
            wi = 0
            for tt in range(S // TB):
                T.dma(DMA(aT[:], AT[:, tt * TB:(tt + 1) * TB].rearrange("(kc p) t -> p kc t", p=128)), writes=["aT"])
                for n in range(2048 // PW):
                    w = wi % 2
                    wi += 1
                    cs = slice(n * PW, (n + 1) * PW)
                    T.dma(DMA(wpan[w][:], W[:, cs].rearrange("(kc p) n -> p kc n", p=128)), writes=[("wpan", w)])
                    for s in range(TB // 128):
                        pi = P.nextp()
                        b = g % 4
                        g += 1
                        r0 = tt * TB + s * 128
                        T.dma(DMA(xp[b][:], x_src[r0:r0 + 128, cs]), writes=[("xp", b)])
                        self.mmg(P.pb[pi][:, 0:PW], [(aT[:, kc, s * 128:(s + 1) * 128], wpan[w][:, kc, :]) for kc in range(KC)],
                                 ["aT", ("wpan", w)], ("pb", pi))
                        T.op("dve", TT(tm[b][:], P.pb[pi][:, 0:PW], gate[:, cs], ALU.mult), reads=[("pb", pi), "gate"], writes=[("tm", b)])
                        T.op("pool", TT(xp[b][:], xp[b][:], tm[b][:], ALU.add), reads=[("xp", b), ("tm", b)], writes=[("xp", b)])
                        T.dma(DMA(x_dst[r0:r0 + 128, cs], xp[b][:]), reads=[("xp", b)])

    def phase_ffnup(self, l):
        D, T = self.D, self.T
        W = D["b_ffn_wup"]
        with self.phase("ffnup") as P:
            aTb = P.sb("aTb", [128, 16, 2048], BF16)
            wu = [P.sb("wu%d" % i, [128, 16, 512], BF16) for i in range(2)]
            wg = [P.sb("wg%d" % i, [128, 16, 512], BF16) for i in range(2)]
            taps = P.sb("taps", [128, 44, 3], F32)
            cb = P.sb("cb", [128, 44], F32)
            halo = P.sb("halo", [128, 44, 2], F32)
            gext = [P.sb("gext%d" % i, [128, 514], F32) for i in range(4)]
            acc = [P.sb("acc%d" % i, [128, 512], F32) for i in range(4)]
            sg = [P.sb("sg%d" % i, [128, 512], F32) for i in range(4)]
            hst = [P.sb("hst%d" % i, [128, 4, 512], BF16) for i in range(3)]
            P.banks(8)
            T.dma(DMA(taps[:], D["ffn_cw"][l]), writes=["taps"])
            T.dma(DMA(cb[:], D["ffn_cb"][l]), writes=["cb"])
            T.op("pool", MSET(halo[:], 0.0), writes=[("halo", j) for j in range(44)])
            g = 0
            wi = 0
            hi = 0
            pend = []
            for TT_ in range(2):
                T.dma(DMA(aTb[:], D["HMT"][:, TT_ * 2048:(TT_ + 1) * 2048].rearrange("(kc p) t -> p kc t", p=128)), writes=["aT"])
                for pn in range(11):
                    w = wi % 2
                    wi += 1
                    T.dma(DMA(wu[w][:], W[:, pn * 512:(pn + 1) * 512].rearrange("(kc p) n -> p kc n", p=128)), writes=[("wu", w)])
                    T.dma(DMA(wg[w][:], W[:, FH + pn * 512:FH + (pn + 1) * 512].rearrange("(kc p) n -> p kc n", p=128)), writes=[("wg", w)])
                    for sub in range(4):
                        tt = TT_ * 4 + sub
                        ss_ = slice(sub * 512, (sub + 1) * 512)
                        hb = hi % 3
                        hi += 1
                        for m in range(4):
                            j = pn * 4 + m
                            b = g % 4
                            g += 1
                            pu = P.nextp()
                            pg_ = P.nextp()
                            self.mmg(P.pb[pg_][:], [(wg[w][:, kc, m * 128:(m + 1) * 128], aTb[:, kc, ss_]) for kc in range(16)],
                                     ["aT", ("wg", w)], ("pb", pg_))
                            self.mmg(P.pb[pu][:], [(wu[w][:, kc, m * 128:(m + 1) * 128], aTb[:, kc, ss_]) for kc in range(16)],
                                     ["aT", ("wu", w)], ("pb", pu))
                            ge = gext[b]
                            T.op("pool", CP(ge[:, 0:2], halo[:, j, :]), reads=[("halo", j)], writes=[("gext", b)])
                            T.op("act", ACP(ge[:, 2:514], P.pb[pg_][:]), reads=[("pb", pg_)], writes=[("gext", b)])
                            T.op("pool", CP(halo[:, j, :], ge[:, 512:514]), reads=[("gext", b)], writes=[("halo", j)])
                            T.op("dve", TS(acc[b][:], ge[:, 0:512], taps[:, j, 0:1], cb[:, j:j + 1], ALU.mult, ALU.add),
                                 reads=[("gext", b), "taps", "cb"], writes=[("acc", b)])
                            T.op("dve", STT(acc[b][:], ge[:, 1:513], taps[:, j, 1:2], acc[b][:], ALU.mult, ALU.add),
                                 reads=[("gext", b), ("acc", b), "taps"], writes=[("acc", b)])
                            T.op("dve", STT(acc[b][:], ge[:, 2:514], taps[:, j, 2:3], acc[b][:], ALU.mult, ALU.add),
                                 reads=[("gext", b), ("acc", b), "taps"], writes=[("acc", b)])
                            for fn in pend:
                                fn()
                            pend = []

                            def back(b=b, pu=pu, hb=hb, m=m):
                                T.op("act", ACTF(sg[b][:], acc[b][:], AF.Silu), reads=[("acc", b)], writes=[("sg", b)])
                                T.op("dve", TT(hst[hb][:, m, :], P.pb[pu][:], sg[b][:], ALU.mult), reads=[("pb", pu), ("sg", b)], writes=[("hst", hb, m)])
                            pend.append(back)
                            if m == 3:
                                def store(pn=pn, tt=tt, hb=hb):
                                    T.dma(DMA(D["HT"][pn * 512:(pn + 1) * 512, tt * 512:(tt + 1) * 512].rearrange("(m p) t -> p m t", p=128), hst[hb][:]),
                                          reads=[("hst", hb, m_) for m_ in range(4)])
                                pend.append(store)
            for fn in pend:
                fn()

    def phase_hybproj(self, l):
        D, T = self.D, self.T
        i_ = l // 2
        W = D["b_hyb_win"]
        TB = 1024
        NSUB = TB // 512
        with self.phase("hproj") as P:
            aTb = P.sb("aTb", [128, 16, TB], BF16)
            wpan = [P.sb("wpan%d" % i, [128, 16, 512], BF16) for i in range(3)]
            cosr = [P.sb("cosr%d" % i, [128, 512], F32) for i in range(NSUB)]
            sinr = [P.sb("sinr%d" % i, [128, 512], F32) for i in range(NSUB)]
            c2 = [P.sb("c2%d" % i, [128, 512], F32) for i in range(NSUB)]
            s2 = [P.sb("s2%d" % i, [128, 512], F32) for i in range(NSUB)]
            xi = P.sb("xi", [128, 4, 128], F32)
            gq = P.sb("gq", [128, 4], F32)
            gkv = P.sb("gkv", [128, 4], F32)
            onesf = P.sb("onesf", [128, 128], F32)
            mt8 = [P.sb("mt%d" % i, [128, 512], F32) for i in range(8)]
            mt = mt8[0:4]
            st4 = [P.sb("st4%d" % i, [128, 4, 512], BF16) for i in range(2)]
            sx4 = [P.sb("sx4%d" % i, [128, 4, 512], BF16) for i in range(2)]
            craw = P.sb("craw", [128, 4, 512], F32)
            sq = P.sb("sq", [128, 4, 512], F32)
            sdt = P.sb("sdt", [128, 512], F32)
            rst = P.sb("rst", [128, 512], F32)
            cst = [P.sb("cst%d" % i, [128, 4, 512], BF16) for i in range(2)]
            krst = [P.sb("krst%d" % i, [128, 512], BF16) for i in range(2)]
            tst = [P.sb("tst%d" % i, [128, 512], BF16) for i in range(4)]
            P.banks(8)
            T.dma(DMA(xi[:], D["XI"]), writes=["xi"])
            T.dma(DMA(gq[:], D["hyb_qg"][i_]), writes=["gq"])
            T.dma(DMA(gkv[:], D["hyb_kvg"][i_]), writes=["gkv"])
            T.op("pool", MSET(onesf[:], 1.0), writes=["onesf"])
            wi = 0
            tg = 0
            sti = 0
            csi = 0
            for TT_ in range(S // TB):
                T.dma(DMA(aTb[:], D["HMT"][:, TT_ * TB:(TT_ + 1) * TB].rearrange("(kc p) t -> p kc t", p=128)), writes=["aT"])
                for sub in range(NSUB):
                    ts_ = slice(TT_ * TB + sub * 512, TT_ * TB + (sub + 1) * 512)
                    T.dma(DMA(cosr[sub][:], D["COSR"][:, ts_]), writes=[("cosr", sub)])
                    T.dma(DMA(sinr[sub][:], D["SINR"][:, ts_]), writes=[("sinr", sub)])
                    T.dma(DMA(c2[sub][:], D["C2"][:, ts_]), writes=[("c2", sub)])
                    T.dma(DMA(s2[sub][:], D["S2"][:, ts_]), writes=[("s2", sub)])

                def load_panel(c0, ncols=512):
                    nonlocal wi
                    w = wi % 3
                    wi += 1
                    T.dma(DMA(wpan[w][:, :, 0:ncols], W[:, c0:c0 + ncols].rearrange("(kc p) n -> p kc n", p=128)), writes=[("wpan", w)])
                    return w

                def fm(w, m, sub):
                    pi = P.nextp()
                    self.mmg(P.pb[pi][:], [(wpan[w][:, kc, m * 128:(m + 1) * 128], aTb[:, kc, sub * 512:(sub + 1) * 512]) for kc in range(16)],
                             ["aT", ("wpan", w)], ("pb", pi))
                    return pi

                def tm_(w, s, sub):
                    pi = P.nextp()
                    t0 = sub * 512 + s * 128
                    self.mmg(P.pb[pi][:], [(aTb[:, kc, t0:t0 + 128], wpan[w][:, kc, :]) for kc in range(16)],
                             ["aT", ("wpan", w)], ("pb", pi))
                    return pi

                for which in range(2):
                    for pn in range(2):
                        w = load_panel(which * 1024 + pn * 512)
                        for sub in range(NSUB):
                            ts_ = slice(TT_ * TB + sub * 512, TT_ * TB + (sub + 1) * 512)
                            sb_ = sti % 2
                            sti += 1
                            st = st4[sb_]
                            sx = sx4[sb_]
                            for hh in range(2):
                                h = pn * 2 + hh
                                p1 = fm(w, hh * 2, sub)
                                p2 = fm(w, hh * 2 + 1, sub)
                                rd1 = [("pb", p1), ("cosr", sub), ("sinr", sub)]
                                rd2 = [("pb", p2), ("cosr", sub), ("sinr", sub)]
                                mo = 4 * hh
                                ma, mb_, mc, md = mt8[mo], mt8[mo + 1], mt8[mo + 2], mt8[mo + 3]
                                T.op("dve", TT(ma[:], P.pb[p1][:], cosr[sub][:], ALU.mult), reads=rd1, writes=[("mt", mo)])
                                T.op("dve", TT(mb_[:], P.pb[p2][:], sinr[sub][:], ALU.mult), reads=rd2, writes=[("mt", mo + 1)])
                                T.op("dve", TT(mc[:], P.pb[p1][:], sinr[sub][:], ALU.mult), reads=rd1, writes=[("mt", mo + 2)])
                                T.op("dve", TT(md[:], P.pb[p2][:], cosr[sub][:], ALU.mult), reads=rd2, writes=[("mt", mo + 3)])
                                T.op("pool", TT(st[:, 2 * hh, :], ma[:], mb_[:], ALU.subtract), reads=[("mt", mo), ("mt", mo + 1)], writes=[("st", sb_, 2 * hh)])
                                T.op("pool", TT(st[:, 2 * hh + 1, :], mc[:], md[:], ALU.add), reads=[("mt", mo + 2), ("mt", mo + 3)], writes=[("st", sb_, 2 * hh + 1)])
                                for cc in (2 * hh, 2 * hh + 1):
                                    if which == 1:
                                        T.op("act", ACTF(st[:, cc, :], st[:, cc, :], AF.Copy, scale=1.0 / 16.0), reads=[("st", sb_, cc)], writes=[("st", sb_, cc)])
                                    else:
                                        T.op("pool", TT(sx[:, cc, :].rearrange("p (c l) -> p c l", l=128), st[:, cc, :].rearrange("p (c l) -> p c l", l=128),
                                                        xi[:, h, :].unsqueeze(1).broadcast_to([128, 4, 128]), ALU.mult),
                                             reads=[("st", sb_, cc), "xi"], writes=[("sx", sb_, cc)])
                            dst = D["RQT"] if which == 0 else D["RKT"]
                            rows = slice(pn * 512, (pn + 1) * 512)
                            T.dma(DMA(dst[rows, ts_].rearrange("(c p) t -> p c t", p=128), st[:]), reads=[("st", sb_, cc) for cc in range(4)])
                            if which == 0:
                                T.dma(DMA(D["RQXT"][rows, ts_].rearrange("(c p) t -> p c t", p=128), sx[:]), reads=[("sx", sb_, cc) for cc in range(4)])
                for which in range(2):
                    dst = D["RV"] if which == 0 else D["RG"]
                    for pn in range(4):
                        w = load_panel(2048 + which * 2048 + pn * 512)
                        for sub in range(NSUB):
                            for s in range(4):
                                pi = tm_(w, s, sub)
                                b = tg % 4
                                tg += 1
                                if which == 0:
                                    T.op("act", ACP(tst[b][:], P.pb[pi][:]), reads=[("pb", pi)], writes=[("tst", b)])
                                else:
                                    T.op("act", ACTF(tst[b][:], P.pb[pi][:], AF.Silu), reads=[("pb", pi)], writes=[("tst", b)])
                                r0 = TT_ * TB + sub * 512 + s * 128
                                T.dma(DMA(dst[r0:r0 + 128, pn * 512:(pn + 1) * 512], tst[b][:]), reads=[("tst", b)])
                for which in range(2):
                    w = load_panel(6144 + which * 512)
                    gcol = gq if which == 0 else gkv
                    dst = D["CQT"] if which == 0 else D["CKVT"]
                    for sub in range(NSUB):
                        ts_ = slice(TT_ * TB + sub * 512, TT_ * TB + (sub + 1) * 512)
                        cb_ = csi % 2
                        csi += 1
                        for c in range(4):
                            pi = fm(w, c, sub)
                            T.op("act", ACP(craw[:, c, :], P.pb[pi][:]), reads=[("pb", pi)], writes=[("craw", c)])
                            T.op("act", ACTF(sq[:, c, :], P.pb[pi][:], AF.Square), reads=[("pb", pi)], writes=[("sq", c)])
                        pi = P.nextp()
                        self.mmg(P.pb[pi][:], [(onesf[:], sq[:, c, :]) for c in range(4)], ["onesf"] + [("sq", c) for c in range(4)], ("pb", pi))
                        T.op("act", ACTF(sdt[:], P.pb[pi][:], AF.Sqrt, scale=1.0 / 512, bias=EPS), reads=[("pb", pi)], writes=["sdt"])
                        T.op("dve", RECIP(rst[:], sdt[:]), reads=["sdt"], writes=["rst"])
                        for c in range(4):
                            T.op("dve", STT(cst[cb_][:, c, :], craw[:, c, :], gcol[:, c:c + 1], rst[:], ALU.mult, ALU.mult),
                                 reads=[("craw", c), "rst", "gq", "gkv"], writes=[("cst", cb_, c)])
                        T.dma(DMA(dst[:, ts_].rearrange("(c p) t -> p c t", p=128), cst[cb_][:]), reads=[("cst", cb_, c) for c in range(4)])
                w = load_panel(7168, 256)
                for sub in range(NSUB):
                    ts_ = slice(TT_ * TB + sub * 512, TT_ * TB + (sub + 1) * 512)
                    pa = fm(w, 0, sub)
                    pb_ = fm(w, 1, sub)
                    T.op("dve", TT(mt[0][:], P.pb[pa][:], c2[sub][:], ALU.mult), reads=[("pb", pa), ("c2", sub)], writes=[("mt", 0)])
                    T.op("dve", TT(mt[1][:], P.pb[pb_][:], s2[sub][:], ALU.mult), reads=[("pb", pb_), ("s2", sub)], writes=[("mt", 1)])
                    T.op("pool", TT(krst[sub][:], mt[0][:], mt[1][:], ALU.add), reads=[("mt", 0), ("mt", 1)], writes=[("krst", sub)])
                    T.dma(DMA(D["KRT"][:, ts_], krst[sub][:]), reads=[("krst", sub)])

    def phase_ret(self, l, consts):
        D, T = self.D, self.T
        i_ = l // 2
        g128 = consts["g128"]
        with self.phase("ret") as P:
            ident = P.sb("ident", [128, 128], BF16)
            decT = P.sb("decT", [128, 4, 128], F32)
            zeta = P.sb("zeta", [128, 4], F32)
            gnb = P.sb("gnb", [128, 2048], F32)
            grow = P.sb("grow", [1, 2048], F32)
            ones1 = P.sb("ones1", [1, 128], F32)
            qT = [P.sb("qT%d" % i, [128, 8, 512], BF16) for i in range(2)]
            qxT = [P.sb("qxT%d" % i, [128, 8, 512], BF16) for i in range(2)]
            kT = [P.sb("kT%d" % i, [128, 8, 512], BF16) for i in range(2)]
            v = [P.sb("v%d" % i, [128, 2048], BF16) for i in range(2)]
            rg = [P.sb("rg%d" % i, [128, 2048], BF16) for i in range(2)]
            ktm = [P.sb("ktm%d" % i, [128, 1024], BF16) for i in range(2)]
            R32 = P.sb("R32", [128, 8, 512], F32)
            Rb = P.sb("Rb", [128, 8, 512], BF16)
            A = [P.sb("A%d" % i, [128, 128], BF16) for i in range(2)]
            st6 = [P.sb("st6%d" % i, [128, 6], F32) for i in range(2)]
            mv = [P.sb("mv%d" % i, [128, 2], F32) for i in range(2)]
            sd = [P.sb("sd%d" % i, [128, 1], F32) for i in range(2)]
            rs = [P.sb("rs%d" % i, [128, 1], F32) for i in range(2)]
            nm = [P.sb("nm%d" % i, [128, 1], F32) for i in range(2)]
            yn = [P.sb("yn%d" % i, [128, 512], F32) for i in range(2)]
            yg = [P.sb("yg%d" % i, [128, 512], F32) for i in range(2)]
            yr = [P.sb("yr%d" % i, [128, 2048], BF16) for i in range(2)]
            yst = [P.sb("yst%d" % i, [128, 16, 512], BF16) for i in range(2)]
            P.banks(6)
            pt = [P.ps("pt%d" % i, [128, 1024], BF16) for i in range(2)]
            T.dma(DMA(ident[:], D["ident"]), writes=["ident"])
            T.dma(DMA(decT[:], D["DECT"]), writes=["decT"])
            T.dma(DMA(zeta[:], D["ZETA"]), writes=["zeta"])
            T.dma(DMA(grow[:], D["hyb_gn"][i_]), writes=["grow"])
            T.op("pool", MSET(ones1[:], 1.0), writes=["ones1"])
            T.op("pool", MSET(R32[:], 0.0), writes=[("R32", c) for c in range(8)])
            T.op("pool", MSET(Rb[:], 0.0), writes=[("Rb", c) for c in range(8)])
            for n in range(4):
                pi = P.nextp()
                self.mmg(P.pb[pi][:], [(ones1[0:1, :], grow[0:1, n * 512:(n + 1) * 512])], ["ones1", "grow"], ("pb", pi))
                T.op("act", ACP(gnb[:, n * 512:(n + 1) * 512], P.pb[pi][:]), reads=[("pb", pi)], writes=["gnb"])
            tp = 0
            for tt in range(NT):
                a = tt % 2
                ts_ = slice(tt * 512, (tt + 1) * 512)
                T.dma(DMA(qT[a][:], D["RQT"][:, ts_].rearrange("(c p) t -> p c t", p=128)), writes=[("qT", a)])
                T.dma(DMA(qxT[a][:], D["RQXT"][:, ts_].rearrange("(c p) t -> p c t", p=128)), writes=[("qxT", a)])
                T.dma(DMA(kT[a][:], D["RKT"][:, ts_].rearrange("(c p) t -> p c t", p=128)), writes=[("kT", a)])
                for cs in range(4):
                    ck = tt * 4 + cs
                    b = ck % 2
                    cs_ = slice(cs * 128, (cs + 1) * 128)
                    r0 = ck * 128
                    T.dma(DMA(v[b][:], D["RV"][r0:r0 + 128, :]), writes=[("v", b)])
                    T.dma(DMA(rg[b][:], D["RG"][r0:r0 + 128, :]), writes=[("rg", b)])
                    for half in range(2):
                        pi = tp % 2
                        tp += 1
                        for c4 in range(4):
                            c = half * 4 + c4
                            T.op("pe", TR(pt[pi][:, c4 * 128:(c4 + 1) * 128], kT[a][:, c, cs_], ident[:]), reads=[("kT", a), "ident"], writes=[("pt", pi)], event=(c4 == 3))
                        for hh in range(2):
                            h = half * 2 + hh
                            T.op("act", ACTF(ktm[b][:, h * 256:(h + 1) * 256], pt[pi][:, hh * 256:(hh + 1) * 256], AF.Copy, scale=zeta[:, h:h + 1]),
                                 reads=[("pt", pi), "zeta"], writes=[("ktm", b, h)])
                    for h in range(4):
                        hb = h % 2
                        pi = P.nextp()
                        self.mmg(P.pb[pi][:, 0:128], [(kT[a][:, 2 * h + dc, cs_], qT[a][:, 2 * h + dc, cs_]) for dc in range(2)],
                                 [("kT", a), ("qT", a)], ("pb", pi))
                        T.op("dve", TT(A[hb][:], P.pb[pi][:, 0:128], decT[:, h, :], ALU.mult), reads=[("pb", pi), "decT"], writes=[("A", hb)])
                        py = P.nextp()
                        pairs = [(A[hb][:], v[b][:, h * 512:(h + 1) * 512])] + [(qxT[a][:, 2 * h + dc, cs_], Rb[:, 2 * h + dc, :]) for dc in range(2)]
                        self.mmg(P.pb[py][:], pairs, [("A", hb), ("v", b), ("qxT", a), ("Rb", 2 * h), ("Rb", 2 * h + 1)], ("pb", py))
                        for dc in range(2):
                            c = 2 * h + dc
                            pr = P.nextp()
                            self.mmg(P.pb[pr][:], [(ktm[b][:, c * 128:(c + 1) * 128], v[b][:, h * 512:(h + 1) * 512])], [("ktm", b, h), ("v", b)], ("pb", pr))
                            T.op("dve", STT(R32[:, c, :], R32[:, c, :], float(g128[h]), P.pb[pr][:], ALU.mult, ALU.add),
                                 reads=[("R32", c), ("pb", pr)], writes=[("R32", c)])
                            T.op("act", ACP(Rb[:, c, :], R32[:, c, :]), reads=[("R32", c)], writes=[("Rb", c)])
                        T.op("dve", lambda e, o=st6[hb], i=P.pb[py]: e.bn_stats(out=o[:], in_=i[:]), reads=[("pb", py)], writes=[("st6", hb)])
                        T.op("dve", lambda e, o=mv[hb], i=st6[hb]: e.bn_aggr(out=o[:], in_=i[:]), reads=[("st6", hb)], writes=[("mv", hb)])
                        T.op("act", ACTF(sd[hb][:], mv[hb][:, 1:2], AF.Sqrt, scale=1.0, bias=EPS), reads=[("mv", hb)], writes=[("sd", hb)])
                        T.op("dve", RECIP(rs[hb][:], sd[hb][:]), reads=[("sd", hb)], writes=[("rs", hb)])
                        T.op("dve", STT(nm[hb][:], mv[hb][:, 0:1], -1.0, rs[hb][:], ALU.mult, ALU.mult), reads=[("mv", hb), ("rs", hb)], writes=[("nm", hb)])
                        T.op("act", ACTF(yn[hb][:], P.pb[py][:], AF.Identity, scale=rs[hb][:, 0:1], bias=nm[hb][:, 0:1]),
                             reads=[("pb", py), ("rs", hb), ("nm", hb)], writes=[("yn", hb)])
                        T.op("pool", TT(yg[hb][:], yn[hb][:], gnb[:, h * 512:(h + 1) * 512], ALU.mult), reads=[("yn", hb), "gnb"], writes=[("yg", hb)])
                        T.op("pool", TT(yr[b][:, h * 512:(h + 1) * 512], yg[hb][:], rg[b][:, h * 512:(h + 1) * 512], ALU.mult),
                             reads=[("yg", hb), ("rg", b)], writes=[("yr", b, h)])
                    for q4 in range(4):
                        pi = tp % 2
                        tp += 1
                        for c4 in range(4):
                            c = q4 * 4 + c4
                            T.op("pe", TR(pt[pi][:, c4 * 128:(c4 + 1) * 128], yr[b][:, c * 128:(c + 1) * 128], ident[:]),
                                 reads=[("yr", b, q4), "ident"], writes=[("pt", pi)], event=(c4 == 3))
                        src = pt[pi][:, 0:512].rearrange("p (c t) -> p c t", t=128)
                        if q4 % 2 == 0:
                            T.op("dve", CP(yst[a][:, q4 * 4:(q4 + 1) * 4, cs_], src), reads=[("pt", pi)], writes=[("yst", a, cs, q4)])
                        else:
                            T.op("act", ACP(yst[a][:, q4 * 4:(q4 + 1) * 4, cs_], src), reads=[("pt", pi)], writes=[("yst", a, cs, q4)])
                T.dma(DMA(D["YT"][0:2048, ts_].rearrange("(c p) t -> p c t", p=128), yst[a][:]),
                      reads=[("yst", a, cs, q4) for cs in range(4) for q4 in range(4)])

    def phase_mla(self, l):
        D, T = self.D, self.T
        Wq = D["b_hyb_wuq"]
        Wkv = D["b_hyb_wukv"]
        scale = (128 + 64) ** -0.5
        with self.phase("mla") as P:
            krt = P.sb("krt", [128, S], BF16)
            ones = P.sb("ones", [128, 128], BF16)
            maskT = P.sb("maskT", [128, 128], BF16)
            cq = [P.sb("cq%d" % i, [128, 4, 512], BF16) for i in range(2)]
            ckv = [P.sb("ckv%d" % i, [128, 4, 512], BF16) for i in range(2)]
            c2 = [P.sb("c2%d" % i, [128, 512], F32) for i in range(2)]
            s2 = [P.sb("s2%d" % i, [128, 512], F32) for i in range(2)]
            wq = [P.sb("wq%d" % i, [128, 4, 512], BF16) for i in range(2)]
            wkv = [P.sb("wkv%d" % i, [128, 4, 512], BF16) for i in range(2)]
            qn = [P.sb("qn%d" % i, [128, 2, S], BF16) for i in range(2)]
            qr = [P.sb("qr%d" % i, [128, S], BF16) for i in range(2)]
            kn = [P.sb("kn%d" % i, [128, 2, S], BF16) for i in range(2)]
            vv = [P.sb("vv%d" % i, [128, 32, 256], BF16) for i in range(2)]
            mt = [P.sb("mt%d" % i, [128, 512], F32) for i in range(2)]
            pp = [P.sb("pp%d" % i, [128, 512], BF16) for i in range(4)]
            rden = [P.sb("rden%d" % i, [128, 512], F32) for i in range(2)]
            ost = [P.sb("ost%d" % i, [128, 512], BF16) for i in range(2)]
            P.banks(8)
            T.dma(DMA(krt[:], D["KRT"]), writes=["krt"])
            T.dma(DMA(maskT[:], D["MASKT"]), writes=["maskT"])
            T.op("pool", MSET(ones[:], 1.0), writes=["ones"])
            prep_rr = 0
            sc_rr = 0
            pp_rr = 0
            for hp in range(8):
                x = hp % 2
                T.dma(DMA(wq[x][:, :, 0:256], Wq[:, hp * 256:(hp + 1) * 256].rearrange("(kc p) n -> p kc n", p=128)), writes=[("wq", x)])
                T.dma(DMA(wq[x][:, :, 256:384], Wq[:, 2048 + hp * 128:2048 + (hp + 1) * 128].rearrange("(kc p) n -> p kc n", p=128)), writes=[("wq", x)])
                T.dma(DMA(wq[x][:, :, 384:512], Wq[:, 3072 + hp * 128:3072 + (hp + 1) * 128].rearrange("(kc p) n -> p kc n", p=128)), writes=[("wq", x)])
                T.dma(DMA(wkv[x][:, :, 0:256], Wkv[:, hp * 256:(hp + 1) * 256].rearrange("(kc p) n -> p kc n", p=128)), writes=[("wkv", x)])
                T.dma(DMA(wkv[x][:, :, 256:512], Wkv[:, 2048 + hp * 256:2048 + (hp + 1) * 256].rearrange("(kc p) n -> p kc n", p=128)), writes=[("wkv", x)])
                for tt in range(NT):
                    a = tt % 2
                    ts_ = slice(tt * 512, (tt + 1) * 512)
                    T.dma(DMA(cq[a][:], D["CQT"][:, ts_].rearrange("(c p) t -> p c t", p=128)), writes=[("cq", a)])
                    T.dma(DMA(ckv[a][:], D["CKVT"][:, ts_].rearrange("(c p) t -> p c t", p=128)), writes=[("ckv", a)])
                    T.dma(DMA(c2[a][:], D["C2"][:, ts_]), writes=[("c2", a)])
                    T.dma(DMA(s2[a][:], D["S2"][:, ts_]), writes=[("s2", a)])

                    def prep(wt, wname, m, src, sname):
                        nonlocal prep_rr
                        pi = 7
                        prep_rr += 1
                        self.mmg(P.pb[pi][:], [(wt[:, kc, m * 128:(m + 1) * 128], src[:, kc, :]) for kc in range(4)],
                                 [(wname, x), (sname, a)], ("pb", pi))
                        return pi
                    for hh in range(2):
                        pi = prep(wq[x], "wq", hh, cq[a], "cq")
                        T.op("act", ACP(qn[x][:, hh, ts_], P.pb[pi][:]), reads=[("pb", pi)], writes=[("qn", x, tt)])
                        pi = prep(wkv[x], "wkv", hh, ckv[a], "ckv")
                        T.op("dve", CP(kn[x][:, hh, ts_], P.pb[pi][:]), reads=[("pb", pi)], writes=[("kn", x, tt)])
                    pa = prep(wq[x], "wq", 2, cq[a], "cq")
                    T.op("dve", TT(mt[0][:], P.pb[pa][:], c2[a][:], ALU.mult), reads=[("pb", pa), ("c2", a)], writes=[("mt", 0)])
                    pb_ = prep(wq[x], "wq", 3, cq[a], "cq")
                    T.op("dve", TT(mt[1][:], P.pb[pb_][:], s2[a][:], ALU.mult), reads=[("pb", pb_), ("s2", a)], writes=[("mt", 1)])
                    T.op("pool", TT(qr[x][:, ts_], mt[0][:], mt[1][:], ALU.add), reads=[("mt", 0), ("mt", 1)], writes=[("qr", x, tt)])
                    for s in range(4):
                        pi = 7
                        prep_rr += 1
                        self.mmg(P.pb[pi][:, 0:256], [(ckv[a][:, kc, s * 128:(s + 1) * 128], wkv[x][:, kc, 256:512]) for kc in range(4)],
                                 [("wkv", x), ("ckv", a)], ("pb", pi))
                        T.op("act", ACP(vv[x][:, tt * 4 + s, :], P.pb[pi][:, 0:256]), reads=[("pb", pi)], writes=[("vv", x, tt)])
                LOOK = 2
                for hh in range(2):
                    h = hp * 2 + hh
                    po = slice(hh * 64, (hh + 1) * 64)
                    tiles = [(qt, kt) for qt in range(NT) for kt in range(4 * qt + 4)]
                    info = {}

                    def front(i):
                        nonlocal sc_rr, pp_rr
                        qt, kt = tiles[i]
                        ks = slice(kt * 128, (kt + 1) * 128)
                        j = kt - 4 * qt
                        q0 = 0 if j <= 0 else j * 128
                        si = sc_rr % 3
                        sc_rr += 1
                        qa = slice(qt * 512 + q0, (qt + 1) * 512)
                        rdq = [("qn", x, qt), ("qr", x, qt), ("kn", x, kt // 4), "krt"]
                        self.mmg(P.pb[si][:, q0:512], [(kn[x][:, hh, ks], qn[x][:, hh, qa]), (krt[po, ks], qr[x][po, qa])], rdq, ("pb", si))
                        pb_i = pp_rr % 4
                        pp_rr += 1
                        T.op("act", ACTF(pp[pb_i][:, q0:512], P.pb[si][:, q0:512], AF.Exp, scale=scale), reads=[("pb", si)], writes=[("pp", pb_i)])
                        if j >= 0:
                            T.op("pool", TT(pp[pb_i][:, j * 128:(j + 1) * 128], pp[pb_i][:, j * 128:(j + 1) * 128], maskT[:], ALU.mult),
                                 reads=[("pp", pb_i), "maskT"], writes=[("pp", pb_i)])
                        info[i] = (pb_i, q0)

                    def back(i):
                        qt, kt = tiles[i]
                        pb_i, q0 = info.pop(i)
                        nk = 4 * qt + 4
                        ob = 3 + (h * NT + qt) % 2
                        db = 5 + (h * NT + qt) % 2
                        first = (kt == 0)
                        last = (kt == nk - 1)
                        T.op("pe", MM(P.pb[ob][:, q0:512], vv[x][:, kt, hh * 128:(hh + 1) * 128], pp[pb_i][:, q0:512], first, last),
                             reads=[("pp", pb_i), ("vv", x, kt // 4)], writes=[("pb", ob)], event=last)
                        T.op("pe", MM(P.pb[db][:, q0:512], ones[:], pp[pb_i][:, q0:512], first, last),
                             reads=[("pp", pb_i), "ones"], writes=[("pb", db)], event=True)
                        if last:
                            qs = slice(qt * 512, (qt + 1) * 512)
                            r = (h * NT + qt) % 2
                            T.op("dve", RECIP(rden[r][:], P.pb[db][:]), reads=[("pb", db)], writes=[("rden", r)])
                            T.op("dve", TT(ost[r][:], P.pb[ob][:], rden[r][:], ALU.mult), reads=[("pb", ob), ("rden", r)], writes=[("ost", r)])
                            T.dma(DMA(D["YT"][2048 + h * 128:2048 + (h + 1) * 128, qs], ost[r][:]), reads=[("ost", r)])

                    for i in range(len(tiles) + LOOK):
                        if i < len(tiles):
                            front(i)
                        if i >= LOOK:
                            back(i - LOOK)

    def phase_ssdproj(self, l):
        D, T = self.D, self.T
        i_ = l // 2
        W = D["b_ssd_win"]
        TB = 2048
        NSUB = TB // 512
        with self.phase("sproj") as P:
            ident = P.sb("ident", [128, 128], BF16)
            aTb = P.sb("aTb", [128, 16, TB], BF16)
            wpan = [P.sb("wpan%d" % i, [128, 16, 512], BF16) for i in range(3)]
            taps = P.sb("taps", [128, 48, 4], F32)
            cb = P.sb("cb", [128, 48], F32)
            halo = P.sb("halo", [128, 48, 3], F32)
            gext = [P.sb("gext%d" % i, [128, 515], F32) for i in range(4)]
            acc = [P.sb("acc%d" % i, [128, 512], F32) for i in range(4)]
            sgb = [P.sb("sgb%d" % i, [128, 512], BF16) for i in range(4)]
            tst = [P.sb("tst%d" % i, [128, 512], BF16) for i in range(4)]
            xst = [P.sb("xst%d" % i, [128, 4, 512], BF16) for i in range(3)]
            bcst = [P.sb("bcst%d" % i, [128, 4, 512], BF16) for i in range(3)]
            dtb = P.sb("dtb", [128, 64], F32)
            ab = P.sb("ab", [128, 64], F32)
            row = P.sb("row", [1, 128], F32)
            ones1 = P.sb("ones1", [1, 128], F32)
            dt1 = [P.sb("dt1%d" % i, [128, 64], F32) for i in range(2)]
            dt2 = [P.sb("dt2%d" % i, [128, 2, 64], F32) for i in range(2)]
            P.banks(6)
            pt = [P.ps("pt%d" % i, [128, 1024], BF16) for i in range(2)]
            T.dma(DMA(ident[:], D["ident"]), writes=["ident"])
            T.dma(DMA(taps[:], D["ssd_cw"][i_]), writes=["taps"])
            T.dma(DMA(cb[:], D["ssd_cb"][i_]), writes=["cb"])
            T.dma(DMA(row[:, 0:64], D["ssd_dtb"][i_]), writes=["row"])
            T.dma(DMA(row[:, 64:128], D["ssd_alog"][i_]), writes=["row"])
            T.op("pool", MSET(ones1[:], 1.0), writes=["ones1"])
            T.op("pool", MSET(halo[:], 0.0), writes=[("halo", j) for j in range(48)])
            pi = P.nextp()
            self.mmg(P.pb[pi][:, 0:128], [(ones1[0:1, :], row[0:1, :])], ["ones1", "row"], ("pb", pi))
            T.op("act", ACP(dtb[:], P.pb[pi][:, 0:64]), reads=[("pb", pi)], writes=["dtb"])
            T.op("act", ACTF(ab[:], P.pb[pi][:, 64:128], AF.Exp), reads=[("pb", pi)], writes=["ab"])
            T.op("dve", (lambda e: e.tensor_scalar_mul(out=ab[:], in0=ab[:], scalar1=-1.0)), reads=["ab"], writes=["ab"])
            wi = 0
            tg = 0
            g = 0
            tp = 0
            xi_ = 0
            pend = []
            for TT_ in range(S // TB):
                T.dma(DMA(aTb[:], D["HMT"][:, TT_ * TB:(TT_ + 1) * TB].rearrange("(kc p) t -> p kc t", p=128)), writes=["aT"])

                def load_panel(c0, ncols=512):
                    nonlocal wi
                    w = wi % 3
                    wi += 1
                    T.dma(DMA(wpan[w][:, :, 0:ncols], W[:, c0:c0 + ncols].rearrange("(kc p) n -> p kc n", p=128)), writes=[("wpan", w)])
                    return w
                for pn in range(8):
                    w = load_panel(pn * 512)
                    for sub in range(NSUB):
                        for s in range(4):
                            t0 = sub * 512 + s * 128
                            pi = P.nextp()
                            self.mmg(P.pb[pi][:], [(aTb[:, kc, t0:t0 + 128], wpan[w][:, kc, :]) for kc in range(16)],
                                     ["aT", ("wpan", w)], ("pb", pi))
                            b = tg % 4
                            tg += 1
                            T.op("act", ACTF(tst[b][:], P.pb[pi][:], AF.Silu), reads=[("pb", pi)], writes=[("tst", b)])
                            r0 = TT_ * TB + t0
                            T.dma(DMA(D["ZS"][r0:r0 + 128, pn * 512:(pn + 1) * 512], tst[b][:]), reads=[("tst", b)])
                w = load_panel(10240, 64)
                for sub in range(NSUB):
                    for s in range(4):
                        t0 = sub * 512 + s * 128
                        pi = P.nextp()
                        self.mmg(P.pb[pi][:, 0:64], [(aTb[:, kc, t0:t0 + 128], wpan[w][:, kc, 0:64]) for kc in range(16)],
                                 ["aT", ("wpan", w)], ("pb", pi))
                        b = s % 2
                        T.op("dve", TT(dt1[b][:], P.pb[pi][:, 0:64], dtb[:], ALU.add), reads=[("pb", pi), "dtb"], writes=[("dt1", b)])
                        T.op("act", ACTF(dt1[b][:], dt1[b][:], AF.Exp), reads=[("dt1", b)], writes=[("dt1", b)])
                        T.op("act", ACTF(dt2[b][:, 0, :], dt1[b][:], AF.Ln, bias=1.0), reads=[("dt1", b)], writes=[("dt2", b)])
                        T.op("dve", TT(dt2[b][:, 1, :], dt2[b][:, 0, :], ab[:], ALU.mult), reads=[("dt2", b), "ab"], writes=[("dt2", b)])
                        r0 = TT_ * TB + t0
                        T.dma(DMA(D["DT"][r0:r0 + 128, :, :], dt2[b][:]), reads=[("dt2", b)])
                for pn in range(12):
                    w = load_panel(4096 + pn * 512)
                    for sub in range(NSUB):
                        ts_ = slice(TT_ * TB + sub * 512, TT_ * TB + (sub + 1) * 512)
                        xb = xi_ % 3
                        xi_ += 1
                        for m in range(4):
                            j = pn * 4 + m
                            b = g % 4
                            g += 1
                            pi = P.nextp()
                            self.mmg(P.pb[pi][:], [(wpan[w][:, kc, m * 128:(m + 1) * 128], aTb[:, kc, sub * 512:(sub + 1) * 512]) for kc in range(16)],
                                     ["aT", ("wpan", w)], ("pb", pi))
                            ge = gext[b]
                            T.op("pool", CP(ge[:, 0:3], halo[:, j, :]), reads=[("halo", j)], writes=[("gext", b)])
                            T.op("act", ACP(ge[:, 3:515], P.pb[pi][:]), reads=[("pb", pi)], writes=[("gext", b)])
                            T.op("pool", CP(halo[:, j, :], ge[:, 512:515]), reads=[("gext", b)], writes=[("halo", j)])
                            T.op("dve", TS(acc[b][:], ge[:, 0:512], taps[:, j, 0:1], cb[:, j:j + 1], ALU.mult, ALU.add),
                                 reads=[("gext", b), "taps", "cb"], writes=[("acc", b)])
                            for k in range(1, 4):
                                T.op("dve", STT(acc[b][:], ge[:, k:k + 512], taps[:, j, k:k + 1], acc[b][:], ALU.mult, ALU.add),
                                     reads=[("gext", b), ("acc", b), "taps"], writes=[("acc", b)])
                            for fn in pend:
                                fn()
                            pend = []

                            def back(j=j, b=b, m=m, xb=xb):
                                nonlocal tp
                                if j < 40:
                                    T.op("act", ACTF(sgb[b][:], acc[b][:], AF.Silu), reads=[("acc", b)], writes=[("sgb", b)])
                                    pti = tp % 2
                                    tp += 1
                                    for s in range(4):
                                        T.op("pe", TR(pt[pti][:, s * 128:(s + 1) * 128], sgb[b][:, s * 128:(s + 1) * 128], ident[:]),
                                             reads=[("sgb", b), "ident"], writes=[("pt", pti)], event=(s == 3))
                                    src = pt[pti][:, 0:512].rearrange("p (s f) -> p s f", f=128)
                                    T.op("dve" if m % 2 == 0 else "act", (CP if m % 2 == 0 else ACP)(xst[xb][:, :, m * 128:(m + 1) * 128], src),
                                         reads=[("pt", pti)], writes=[("xst", xb, m)])
                                    if j >= 32:
                                        T.op("pool", CP(bcst[xb][:, m, :], sgb[b][:]), reads=[("sgb", b)], writes=[("bcst", xb, m)])
                                else:
                                    T.op("act", ACTF(bcst[xb][:, m, :], acc[b][:], AF.Silu), reads=[("acc", b)], writes=[("bcst", xb, m)])
                            pend.append(back)
                            if m == 3:
                                def store(pn=pn, ts_=ts_, xb=xb):
                                    if pn < 8:
                                        T.dma(DMA(D["XS"][ts_, pn * 512:(pn + 1) * 512].rearrange("(s p) f -> p s f", p=128), xst[xb][:]),
                                              reads=[("xst", xb, m_) for m_ in range(4)])
                                    elif pn < 10:
                                        T.dma(DMA(D["BTM"][ts_, (pn - 8) * 512:(pn - 7) * 512].rearrange("(s p) f -> p s f", p=128), xst[xb][:]),
                                              reads=[("xst", xb, m_) for m_ in range(4)])
                                        T.dma(DMA(D["BT"][(pn - 8) * 512:(pn - 7) * 512, ts_].rearrange("(m p) t -> p m t", p=128), bcst[xb][:]),
                                              reads=[("bcst", xb, m_) for m_ in range(4)])
                                    else:
                                        T.dma(DMA(D["CT"][(pn - 10) * 512:(pn - 9) * 512, ts_].rearrange("(m p) t -> p m t", p=128), bcst[xb][:]),
                                              reads=[("bcst", xb, m_) for m_ in range(4)])
                                pend.append(store)
            for fn in pend:
                fn()

    def phase_ssdcore(self, l):
        D, T = self.D, self.T
        i_ = l // 2
        with self.phase("score") as P:
            ident = P.sb("ident", [128, 128], BF16)
            ucm = P.sb("ucm", [128, 128], F32)
            usm = P.sb("usm", [128, 128], F32)
            onesf = P.sb("onesf", [128, 128], F32)
            ones1 = P.sb("ones1", [1, 128], F32)
            row = P.sb("row", [1, 64], F32)
            dsk = P.sb("dsk", [128, 64], F32)
            ngb = P.sb("ngb", [128, 4096], F32)
            nrw = [P.sb("nrw%d" % i, [1, 512], F32) for i in range(2)]
            xs = [P.sb("xs%d" % i, [128, 4096], BF16) for i in range(1)] * 2
            zs = [P.sb("zs%d" % i, [128, 4096], BF16) for i in range(1)] * 2
            btm = [P.sb("btm%d" % i, [128, 1024], BF16) for i in range(2)]
            bT = [P.sb("bT%d" % i, [128, 8, 128], BF16) for i in range(2)]
            cT = [P.sb("cT%d" % i, [128, 8, 128], BF16) for i in range(2)]
            dtt = [P.sb("dtt%d" % i, [128, 2, 64], F32) for i in range(2)]
            rbg = [P.sb("rbg%d" % i, [128, 8, 128], F32) for i in range(2)]
            cbm = P.sb("cbm", [128, 8, 128], F32)
            ee = [P.sb("ee%d" % i, [128, 4, 128], F32) for i in range(4)]
            mT = P.sb("mT", [128, 64, 128], BF16)
            xdt = P.sb("xdt", [128, 4096], BF16)
            xw = P.sb("xw", [128, 4096], BF16)
            eac = P.sb("eac", [128, 64], F32)
            tot = P.sb("tot", [128, 64], F32)
            edec = P.sb("edec", [128, 64], F32)
            ten = P.sb("ten", [128, 64], F32)
            acs = P.sb("acs", [128, 64], F32)
            st32 = P.sb("st32", [128, 4096], F32)
            stb = P.sb("stb", [128, 4096], BF16)
            t1 = [P.sb("t1%d" % i, [128, 512], F32) for i in range(2)]
            t2 = [P.sb("t2%d" % i, [128, 512], F32) for i in range(2)]
            yg = P.sb("yg", [128, 4096], F32)
            ynb = P.sb("ynb", [128, 4096], BF16)
            ss = P.sb("ss", [128, 1], F32)
            sd = P.sb("sd", [128, 1], F32)
            rstd = P.sb("rstd", [128, 1], F32)
            yst = [P.sb("yst%d" % i, [128, 32, 128], BF16) for i in range(1)] * 2
            P.banks(6)
            pt = [P.ps("pt%d" % i, [128, 1024], BF16) for i in range(2)]
            T.dma(DMA(ident[:], D["ident"]), writes=["ident"])
            T.dma(DMA(ucm[:], D["UCM"]), writes=["ucm"])
            T.dma(DMA(usm[:], D["USM"]), writes=["usm"])
            T.dma(DMA(row[:], D["ssd_d"][i_]), writes=["row"])
            T.op("pool", MSET(ones1[:], 1.0), writes=["ones1"])
            T.op("pool", MSET(onesf[:], 1.0), writes=["onesf"])
            T.op("pool", MSET(st32[:], 0.0), writes=[("st32", g) for g in range(8)])
            T.op("pool", MSET(stb[:], 0.0), writes=[("stb", g) for g in range(8)])
            pi = P.nextp()
            self.mmg(P.pb[pi][:, 0:64], [(ones1[0:1, :], row[0:1, :])], ["ones1", "row"], ("pb", pi))
            T.op("act", ACP(dsk[:], P.pb[pi][:, 0:64]), reads=[("pb", pi)], writes=["dsk"])
            for n in range(8):
                pi = P.nextp()
                T.dma(DMA(nrw[n % 2][:], D["ssd_ng"][i_][:, n * 512:(n + 1) * 512]), writes=[("nrw", n % 2)])
                self.mmg(P.pb[pi][:], [(ones1[0:1, :], nrw[n % 2][0:1, :])], ["ones1", ("nrw", n % 2)], ("pb", pi))
                T.op("act", ACP(ngb[:, n * 512:(n + 1) * 512], P.pb[pi][:]), reads=[("pb", pi)], writes=["ngb"])
            tp = 0
            for ck in range(getattr(self, "dbg_nck", 32)):
                b = ck % 2
                r0 = ck * 128
                rs_ = slice(r0, r0 + 128)
                T.dma(DMA(xs[b][:], D["XS"][rs_, :]), writes=[("xs", 0), ("xs", 1)])
                T.dma(DMA(zs[b][:], D["ZS"][rs_, :]), writes=[("zs", 0), ("zs", 1)])
                T.dma(DMA(btm[b][:], D["BTM"][rs_, :]), writes=[("btm", b)])
                T.dma(DMA(bT[b][:], D["BT"][:, rs_].rearrange("(g p) t -> p g t", p=128)), writes=[("bT", b)])
                T.dma(DMA(cT[b][:], D["CT"][:, rs_].rearrange("(g p) t -> p g t", p=128)), writes=[("cT", b)])
                T.dma(DMA(dtt[b][:], D["DT"][rs_, :, :]), writes=[("dtt", b)])
                dt_ = dtt[b][:, 0, :]
                dta = dtt[b][:, 1, :]
                stg = getattr(self, 'dbg_stage', 99)
                if stg < 2: continue
                pa = P.nextp()
                dboth = dtt[b][:].rearrange("p a h -> p (a h)")
                self.mmg(P.pb[pa][:, 0:128], [(ucm[:], dboth)], ["ucm", ("dtt", b)], ("pb", pa))
                T.op("act", ACTF(eac[:], P.pb[pa][:, 64:128], AF.Exp), reads=[("pb", pa)], writes=["eac"])
                T.op("act", ACP(acs[:], P.pb[pa][:, 64:128]), reads=[("pb", pa)], writes=["acs"])
                ptot = P.nextp()
                self.mmg(P.pb[ptot][:, 0:128], [(onesf[:], dboth)], ["onesf", ("dtt", b)], ("pb", ptot))
                T.op("act", ACTF(edec[:], P.pb[ptot][:, 64:128], AF.Exp), reads=[("pb", ptot)], writes=["edec"])
                T.op("act", ACP(tot[:], P.pb[ptot][:, 64:128]), reads=[("pb", ptot)], writes=["tot"])
                T.op("dve", TT(ten[:], tot[:], acs[:], ALU.subtract), reads=["tot", "acs"], writes=["ten"])
                T.op("act", ACTF(ten[:], ten[:], AF.Exp), reads=["ten"], writes=["ten"])
                if stg < 3: continue
                T.op("pool", TT(xdt[:].rearrange("p (h e) -> p h e", e=64), xs[b][:].rearrange("p (h e) -> p h e", e=64),
                                dt_.unsqueeze(2).broadcast_to([128, 64, 64]), ALU.mult), reads=[("xs", b), ("dtt", b)], writes=["xdt"])
                T.op("pool", TT(xw[:].rearrange("p (h e) -> p h e", e=64), xdt[:].rearrange("p (h e) -> p h e", e=64),
                                ten[:].unsqueeze(2).broadcast_to([128, 64, 64]), ALU.mult), reads=["xdt", "ten"], writes=["xw"])
                if stg < 4: continue
                for g in range(8):
                    pc = P.nextp()
                    self.mmg(P.pb[pc][:, 0:128], [(bT[b][:, g, :], cT[b][:, g, :])], [("bT", b), ("cT", b)], ("pb", pc))
                    T.op("dve", TT(cbm[:, g, :], P.pb[pc][:, 0:128], ucm[:], ALU.mult), reads=[("pb", pc), "ucm"], writes=[("cbm", g)])
                if stg < 5: continue
                live = {}

                def stA(g):
                    hs = slice(g * 8, (g + 1) * 8)
                    rb = rbg[g % 2]
                    T.op("dve", TT(rb[:], dta[:, hs].unsqueeze(2).broadcast_to([128, 8, 128]), ucm[:].unsqueeze(1).broadcast_to([128, 8, 128]), ALU.mult),
                         reads=[("dtt", b), "ucm"], writes=[("rb", g % 2)])
                    for half in range(2):
                        eb = (g * 2 + half) % 4
                        pseg = half
                        self.mmg(P.pb[pseg][:], [(usm[:], rb[:, half * 4:half * 4 + 4, :].rearrange("p h l -> p (h l)"))], ["usm", ("rb", g % 2)], ("pb", pseg))
                        T.op("act", ACTF(ee[eb][:].rearrange("p h l -> p (h l)"), P.pb[pseg][:], AF.Exp), reads=[("pb", pseg)], writes=[("ee", eb)])

                def stB(g):
                    for half in range(2):
                        eb = (g * 2 + half) % 4
                        h0 = g * 8 + half * 4
                        T.op("pool" if half == 0 else "dve",
                             TT(mT[:, h0:h0 + 4, :], ee[eb][:], cbm[:, g, :].unsqueeze(1).broadcast_to([128, 4, 128]), ALU.mult),
                             reads=[("ee", eb), ("cbm", g)], writes=[("mT", g)])
                    pyd = 3 + g % 2
                    for hh in range(8):
                        h = g * 8 + hh
                        T.op("pe", MM(P.pb[pyd][:, hh * 64:(hh + 1) * 64], mT[:, h, :], xdt[:, h * 64:(h + 1) * 64], True, True),
                             reads=[("mT", g), "xdt"] if hh == 0 else (), writes=[("pb", pyd)], event=(hh == 7))
                    pyo = 5
                    self.mmg(P.pb[pyo][:], [(cT[b][:, g, :], stb[:, g * 512:(g + 1) * 512])], [("cT", b), ("stb", g)], ("pb", pyo))
                    live[g] = (pyd, pyo)

                def stC(g):
                    hs = slice(g * 8, (g + 1) * 8)
                    pyd, pyo = live.pop(g)
                    gb = g % 2
                    gsl = slice(g * 512, (g + 1) * 512)
                    T.op("dve", TT(t1[gb][:].rearrange("p (h e) -> p h e", e=64), P.pb[pyo][:].rearrange("p (h e) -> p h e", e=64),
                                    eac[:, hs].unsqueeze(2).broadcast_to([128, 8, 64]), ALU.mult), reads=[("pb", pyo), "eac"], writes=[("t1", gb)])
                    T.op("dve", TT(t1[gb][:], t1[gb][:], P.pb[pyd][:], ALU.add), reads=[("t1", gb), ("pb", pyd)], writes=[("t1", gb)])
                    T.op("pool", TT(t2[gb][:].rearrange("p (h e) -> p h e", e=64), xs[b][:, gsl].rearrange("p (h e) -> p h e", e=64),
                                     dsk[:, hs].unsqueeze(2).broadcast_to([128, 8, 64]), ALU.mult), reads=[("xs", b), "dsk"], writes=[("t2", gb)])
                    T.op("pool", TT(t2[gb][:], t2[gb][:], t1[gb][:], ALU.add), reads=[("t2", gb), ("t1", gb)], writes=[("t2", gb)])
                    T.op("pool", TT(yg[:, gsl], t2[gb][:], zs[b][:, gsl], ALU.mult), reads=[("t2", gb), ("zs", b)], writes=[("yg", g)])
                    pst = 2
                    self.mmg(P.pb[pst][:], [(btm[b][:, g * 128:(g + 1) * 128], xw[:, gsl])], [("btm", b), "xw"], ("pb", pst))
                    T.op("dve", TT(st32[:, gsl].rearrange("p (h e) -> p h e", e=64), st32[:, gsl].rearrange("p (h e) -> p h e", e=64),
                                    edec[:, hs].unsqueeze(2).broadcast_to([128, 8, 64]), ALU.mult), reads=[("st32", g), "edec"], writes=[("st32", g)])
                    T.op("dve", TT(st32[:, gsl], st32[:, gsl], P.pb[pst][:], ALU.add), reads=[("st32", g), ("pb", pst)], writes=[("st32", g)])
                    T.op("act", ACP(stb[:, gsl], st32[:, gsl]), reads=[("st32", g)], writes=[("stb", g)])

                for step in range(8 + 2):
                    if step < 8:
                        stA(step)
                    if 0 <= step - 2 < 8:
                        stC(step - 2)
                    if 0 <= step - 1 < 8:
                        stB(step - 1)
                if stg < 9: continue
                T.op("act", ACTF(ynb[:], yg[:], AF.Square, accum_out=ss[:, 0:1]), reads=[("yg", g) for g in range(8)], writes=["ss", ("ynb", 0), ("ynb", 1)])
                T.op("act", ACTF(sd[:], ss[:], AF.Sqrt, scale=1.0 / 4096, bias=EPS), reads=["ss"], writes=["sd"])
                T.op("dve", RECIP(rstd[:], sd[:]), reads=["sd"], writes=["rstd"])
                for hf in range(2):
                    fs = slice(hf * 2048, (hf + 1) * 2048)
                    T.op("dve", STT(ynb[:, fs], yg[:, fs], rstd[:, 0:1], ngb[:, fs], ALU.mult, ALU.mult),
                         reads=[("yg", g) for g in range(8)] + ["rstd", "ngb"], writes=[("ynb", hf)])
                ya = 0
                for q8 in range(8):
                    pti = tp % 2
                    tp += 1
                    for c4 in range(4):
                        c = q8 * 4 + c4
                        T.op("pe", TR(pt[pti][:, c4 * 128:(c4 + 1) * 128], ynb[:, c * 128:(c + 1) * 128], ident[:]),
                             reads=[("ynb", q8 // 4), "ident"], writes=[("pt", pti)], event=(c4 == 3))
                    src = pt[pti][:, 0:512].rearrange("p (c t) -> p c t", t=128)
                    if q8 % 2 == 0:
                        T.op("dve", CP(yst[ya][:, q8 * 4:(q8 + 1) * 4, :], src), reads=[("pt", pti)], writes=[("yst", ya, q8)])
                    else:
                        T.op("act", ACP(yst[ya][:, q8 * 4:(q8 + 1) * 4, :], src), reads=[("pt", pti)], writes=[("yst", ya, q8)])
                for q8 in range(8):
                    T.dma(DMA(D["YT"][q8 * 512:(q8 + 1) * 512, rs_].rearrange("(c p) t -> p c t", p=128), yst[ya][:, q8 * 4:(q8 + 1) * 4, :]),
                          reads=[("yst", ya, q8)])


def _consts():
    c = {}
    pos = np.arange(S, dtype=np.float32)
    inv_r = (np.float32(10000.0) ** (-np.arange(128, dtype=np.float32) / np.float32(128))).astype(np.float32)
    ang = (pos[None, :] * inv_r[:, None]).astype(np.float32)
    c["COSR"] = np.cos(ang).astype(np.float32)
    c["SINR"] = np.sin(ang).astype(np.float32)
    inv_m = (np.float32(10000.0) ** (-np.arange(32, dtype=np.float32) / np.float32(32))).astype(np.float32)
    angm = (pos[None, :] * inv_m[:, None]).astype(np.float32)
    cm, sm = np.cos(angm).astype(np.float32), np.sin(angm).astype(np.float32)
    c["C2"] = np.concatenate([cm, cm, cm, cm], 0)
    c["S2"] = np.concatenate([-sm, sm, -sm, sm], 0)
    lg = np.log1p(-np.exp2(-5.0 - np.arange(4, dtype=np.float64)))
    idx = np.arange(128, dtype=np.float64)
    rel = idx[None, :] - idx[:, None]
    dec = np.where(rel >= 0, np.exp(lg[:, None, None] * np.maximum(rel, 0.0)), 0.0)
    c["DECT"] = np.ascontiguousarray(dec.transpose(1, 0, 2)).astype(np.float32)
    xi = np.exp(lg[:, None] * (idx[None, :] + 1.0))
    c["XI"] = np.ascontiguousarray(np.broadcast_to(xi[None], (128, 4, 128))).astype(np.float32)
    c["ZETA"] = np.ascontiguousarray(np.exp(lg[None, :] * (127.0 - idx[:, None]))).astype(np.float32)
    c["g128"] = np.exp(lg * 128.0)
    k = np.arange(128)
    c["MASKT"] = np.where((k[:, None] >= 64) & (k[None, :] < 64), 0.0, 1.0).astype(ml_dtypes.bfloat16)
    c["UCM"] = (k[:, None] <= k[None, :]).astype(np.float32)
    c["USM"] = (k[:, None] > k[None, :]).astype(np.float32)
    c["ident"] = np.eye(128, dtype=np.float32).astype(ml_dtypes.bfloat16)
    return c


def _prep_weights(inp):
    w = {}
    f = lambda a: np.ascontiguousarray(a, dtype=np.float32)
    w["ada_w"] = f(inp["ada_w"])
    w["ada_b"] = f(inp["ada_b"]).reshape(4, 1, 12288)
    w["norm_mix_g"] = f(inp["norm_mix_g"]).reshape(4, 1, 2048)
    w["norm_ffn_g"] = f(inp["norm_ffn_g"]).reshape(4, 1, 2048)
    win = inp["hyb_w_in"]
    kr = win[:, :, 7168:7232]
    kra = np.concatenate([kr, kr], -1)
    krb = np.concatenate([kr[:, :, 32:64], kr[:, :, 0:32], kr[:, :, 32:64], kr[:, :, 0:32]], -1)
    w["hyb_win"] = f(np.concatenate([win[:, :, 0:7168], kra, krb], -1))
    uq = inp["hyb_w_uq"].reshape(2, 512, 16, 192)
    qn = uq[:, :, :, 0:128].reshape(2, 512, 2048)
    ra = uq[:, :, :, 128:192].reshape(2, 512, 1024)
    rb = np.concatenate([uq[:, :, :, 160:192], uq[:, :, :, 128:160]], -1).reshape(2, 512, 1024)
    w["hyb_wuq"] = f(np.concatenate([qn, ra, rb], -1))
    ukv = inp["hyb_w_ukv"].reshape(2, 512, 16, 256)
    w["hyb_wukv"] = f(np.concatenate([ukv[:, :, :, 0:128].reshape(2, 512, 2048), ukv[:, :, :, 128:256].reshape(2, 512, 2048)], -1))
    w["hyb_qg"] = f(inp["hyb_q_norm_g"].reshape(2, 4, 128).transpose(0, 2, 1))
    w["hyb_kvg"] = f(inp["hyb_kv_norm_g"].reshape(2, 4, 128).transpose(0, 2, 1))
    w["hyb_gn"] = f(inp["hyb_ret_gn_g"]).reshape(2, 1, 2048)
    w["hyb_wout"] = f(inp["hyb_w_out"])
    w["ssd_win"] = f(inp["ssd_w_in"])
    w["ssd_cw"] = f(inp["ssd_conv_w"].transpose(0, 2, 1).reshape(2, 48, 128, 4).transpose(0, 2, 1, 3))
    w["ssd_cb"] = f(inp["ssd_conv_b"].reshape(2, 48, 128).transpose(0, 2, 1))
    w["ssd_dtb"] = f(inp["ssd_dt_bias"]).reshape(2, 1, 64)
    w["ssd_alog"] = f(inp["ssd_a_log"]).reshape(2, 1, 64)
    w["ssd_d"] = f(inp["ssd_d"]).reshape(2, 1, 64)
    w["ssd_ng"] = f(inp["ssd_norm_g"]).reshape(2, 1, 4096)
    w["ssd_wout"] = f(inp["ssd_w_out"])
    w["ffn_wup"] = f(inp["ffn_w_up"])
    w["ffn_cw"] = f(inp["ffn_conv_w"].transpose(0, 2, 1).reshape(4, 44, 128, 3).transpose(0, 2, 1, 3))
    w["ffn_cb"] = f(inp["ffn_conv_b"].reshape(4, 44, 128).transpose(0, 2, 1))
    w["ffn_wdown"] = f(inp["ffn_w_down"])
    w["final_norm_g"] = f(inp["final_norm_g"]).reshape(1, 2048)
    return w


def build(plan=None, dbg=False, only=None):
    B = Builder(dbg=dbg, only=only)
    nc, D = B.nc, B.D
    consts = _consts()
    B.consts = consts
    B.din("x", [S, DM])
    B.din("cT", [128, 16])
    B.din("ada_w", [4, 2048, 12288]); B.din("ada_b", [4, 1, 12288])
    B.din("norm_mix_g", [4, 1, 2048]); B.din("norm_ffn_g", [4, 1, 2048])
    B.din("hyb_win", [2, 2048, 7424]); B.din("hyb_wuq", [2, 512, 4096]); B.din("hyb_wukv", [2, 512, 4096])
    B.din("hyb_qg", [2, 128, 4]); B.din("hyb_kvg", [2, 128, 4]); B.din("hyb_gn", [2, 1, 2048]); B.din("hyb_wout", [2, 4096, 2048])
    B.din("ssd_win", [2, 2048, 10304]); B.din("ssd_cw", [2, 128, 48, 4]); B.din("ssd_cb", [2, 128, 48])
    B.din("ssd_dtb", [2, 1, 64]); B.din("ssd_alog", [2, 1, 64]); B.din("ssd_d", [2, 1, 64]); B.din("ssd_ng", [2, 1, 4096])
    B.din("ssd_wout", [2, 4096, 2048])
    B.din("ffn_wup", [4, 2048, 2 * FH]); B.din("ffn_cw", [4, 128, 44, 3]); B.din("ffn_cb", [4, 128, 44]); B.din("ffn_wdown", [4, FH, 2048])
    B.din("final_norm_g", [1, 2048])
    for k in ("COSR", "SINR", "C2", "S2"):
        B.din(k, [128, S])
    B.din("DECT", [128, 4, 128]); B.din("XI", [128, 4, 128]); B.din("ZETA", [128, 4])
    B.din("MASKT", [128, 128], BF16); B.din("UCM", [128, 128]); B.din("USM", [128, 128]); B.din("ident", [128, 128], BF16)
    B.dscr("out", [S, DM], F32, out=True)
    B.dscr("xres", [S, DM], F32)
    B.dscr("modsb", [4, 6, 128, 2048], F32)
    B.dscr("b_hyb_win", [2048, 7424], BF16); B.dscr("b_hyb_wuq", [512, 4096], BF16); B.dscr("b_hyb_wukv", [512, 4096], BF16)
    B.dscr("b_wout", [4096, 2048], BF16); B.dscr("b_ssd_win", [2048, 10304], BF16)
    B.dscr("b_ffn_wup", [2048, 2 * FH], BF16); B.dscr("b_ffn_wdown", [FH, 2048], BF16)
    B.dscr("HMT", [2048, S], BF16); B.dscr("YT", [4096, S], BF16); B.dscr("HT", [FH, S], BF16)
    B.dscr("RQT", [1024, S], BF16); B.dscr("RQXT", [1024, S], BF16); B.dscr("RKT", [1024, S], BF16)
    B.dscr("RV", [S, 2048], BF16); B.dscr("RG", [S, 2048], BF16)
    B.dscr("CQT", [512, S], BF16); B.dscr("CKVT", [512, S], BF16); B.dscr("KRT", [128, S], BF16)
    B.dscr("ZS", [S, 4096], BF16); B.dscr("XS", [S, 4096], BF16); B.dscr("BTM", [S, 1024], BF16)
    B.dscr("BT", [1024, S], BF16); B.dscr("CT", [1024, S], BF16); B.dscr("DT", [S, 2, 64], F32)
    with contextlib.ExitStack() as es:
        sems = [es.enter_context(nc.semaphore("s%d" % i)) for i in range(len(ENGS) + N_DMA_SLOTS)]
        B.T = Tracker(nc, sems)
        if plan is None:
            plan = [("ada",)]
            for l in range(4):
                plan += [("cast", l), ("mixer", l), ("ffn", l)]
            plan += [("final",)]
        xcur = D.get("x")
        for ph in plan:
            if ph[0] == "ada":
                B.phase_ada()
            elif ph[0] == "cast":
                B.phase_cast(ph[1])
            elif ph[0] == "mixer":
                l = ph[1]
                B.phase_norm(xcur, l, 1, 0, D["HMT"])
                if l % 2 == 0:
                    B.phase_hybproj(l)
                    B.phase_ret(l, consts)
                    B.phase_mla(l)
                else:
                    B.phase_ssdproj(l)
                    B.phase_ssdcore(l)
                B.phase_outproj(D["YT"], 32, D["b_wout"], l, 2, xcur, D["xres"])
                xcur = D["xres"]
            elif ph[0] == "sub":
                getattr(B, "phase_" + ph[1])(*[xcur if a == "X" else (D[a] if isinstance(a, str) else a) for a in ph[2:]])
            elif ph[0] == "ffn":
                l = ph[1]
                B.phase_norm(xcur, l, 4, 3, D["HMT"])
                B.phase_ffnup(l)
                B.phase_outproj(D["HT"], 44, D["b_ffn_wdown"], l, 5, xcur, D["xres"])
                xcur = D["xres"]
            elif ph[0] == "final":
                B.phase_final(xcur)
    return B


def make_in_maps(inputs, cores):
    w = _prep_weights(inputs)
    c = _consts()
    shared = dict(w)
    for k in ("COSR", "SINR", "C2", "S2", "DECT", "XI", "ZETA", "MASKT", "UCM", "USM", "ident"):
        shared[k] = c[k]
    maps = []
    for b in cores:
        m = dict(shared)
        m["x"] = np.ascontiguousarray(inputs["x"][b], dtype=np.float32)
        m["cT"] = np.ascontiguousarray(np.asarray(inputs["c"][b], dtype=np.float32).reshape(16, 128).T)
        maps.append(m)
    return maps


def kernel(**inputs):
    inputs = {k: np.asarray(v) for k, v in inputs.items()}
    B = build()
    maps = make_in_maps(inputs, list(range(N_CORES)))
    res = run_bass_kernel_spmd(B.nc, maps, core_ids=list(range(N_CORES)))
    out = np.stack([np.asarray(r["out"], dtype=np.float32) for r in res.results], 0)
    return out
```
